# Optimizing a Trainium2 kernel written in Bass

```python
import jax, jax.numpy as jnp
from jax import lax
import numpy as np

D_MODEL = 2048
BATCH = 2
SEQ = 4096
DEPTH = 1

HEAD_DIM = 64
A_Q_HEADS = 12
A_KV_HEADS = 4
A_GROUP = A_Q_HEADS // A_KV_HEADS
WINDOW = 128
BLOCK = 128
B_HEADS = 12
C_HEADS = 4
C_HEAD_DIM = 128
MEM_TOKENS = 256
N_BRANCHES = 3
A_WIDTH = A_Q_HEADS * HEAD_DIM
A_KV_WIDTH = A_KV_HEADS * HEAD_DIM
B_WIDTH = B_HEADS * HEAD_DIM
C_WIDTH = C_HEADS * C_HEAD_DIM
EPS = 1e-6
NEG = -1e30

IN_SIZES = (A_WIDTH, A_KV_WIDTH, A_KV_WIDTH, A_WIDTH,
            B_WIDTH, B_WIDTH, B_WIDTH, B_WIDTH, B_HEADS,
            C_WIDTH, C_WIDTH,
            N_BRANCHES * D_MODEL)
IN_WIDTH = sum(IN_SIZES)
IN_OFFSETS = tuple(int(o) for o in np.cumsum(IN_SIZES)[:-1])

kernel_name = "hybrid_swa_fox_memory_gated_block"


def rms_norm(x, gain):
    x32 = x.astype(jnp.float32)
    y = x32 * lax.rsqrt(jnp.mean(x32 * x32, axis=-1, keepdims=True) + EPS)
    return (y * gain.astype(jnp.float32)).astype(x.dtype)


def alibi_slopes(n_heads):
    h = jnp.arange(1, n_heads + 1, dtype=jnp.float32)
    return jnp.exp2(-8.0 * h / n_heads)


def sliding_window_attention(q, k, v, q_gain, k_gain, sinks):
    b, s, _, d = q.shape
    nb = s // BLOCK
    q = rms_norm(q, q_gain).astype(jnp.float32)
    k = rms_norm(k, k_gain).astype(jnp.float32)
    v = v.astype(jnp.float32)
    qb = q.reshape(b, nb, BLOCK, A_KV_HEADS, A_GROUP, d)
    pad = jnp.zeros((b, BLOCK, A_KV_HEADS, d), jnp.float32)
    kp = jnp.concatenate([pad, k], axis=1).reshape(b, nb + 1, BLOCK, A_KV_HEADS, d)
    vp = jnp.concatenate([pad, v], axis=1).reshape(b, nb + 1, BLOCK, A_KV_HEADS, d)
    kb = jnp.concatenate([kp[:, :-1], kp[:, 1:]], axis=2)
    vb = jnp.concatenate([vp[:, :-1], vp[:, 1:]], axis=2)
    scores = jnp.einsum('bnqkgd,bnskd->bnkgqs', qb, kb) * (d ** -0.5)
    qi = jnp.arange(BLOCK)[:, None]
    kj = jnp.arange(2 * BLOCK)[None, :]
    rel = qi + BLOCK - kj
    key_pos = jnp.arange(nb)[:, None, None] * BLOCK - BLOCK + kj[None]
    valid = (rel >= 0) & (rel < WINDOW) & (key_pos >= 0)
    slopes = alibi_slopes(A_Q_HEADS).reshape(A_KV_HEADS, A_GROUP)
    bias = -slopes[:, :, None, None] * rel.astype(jnp.float32)
    scores = jnp.where(valid[None, :, None, None], scores + bias[None, None], NEG)
    sink = sinks.astype(jnp.float32).reshape(A_KV_HEADS, A_GROUP)[None, None, :, :, None, None]
    m = jnp.maximum(jnp.max(scores, axis=-1, keepdims=True), sink)
    p = jnp.exp(scores - m)
    denom = jnp.sum(p, axis=-1, keepdims=True) + jnp.exp(sink - m)
    out = jnp.einsum('bnkgqs,bnskd->bnqkgd', p / denom, vb)
    return out.reshape(b, s, A_Q_HEADS * d)


def forgetting_attention(q, k, v, f_logit, q_gain, k_gain):
    b, s, h, d = q.shape
    nb = s // BLOCK
    q = rms_norm(q, q_gain).astype(jnp.float32)
    k = rms_norm(k, k_gain).astype(jnp.float32)
    v = v.astype(jnp.float32)
    log_f = jax.nn.log_sigmoid(f_logit.astype(jnp.float32))
    c = jnp.cumsum(log_f, axis=1).transpose(0, 2, 1)
    q_blocks = q.reshape(b, nb, BLOCK, h, d).transpose(1, 0, 2, 3, 4)
    cq_blocks = c.reshape(b, h, nb, BLOCK).transpose(2, 0, 1, 3)
    key_pos = jnp.arange(s)
    scale = d ** -0.5

    def block_fn(args):
        qb, cqb, n = args
        sc = jnp.einsum('bqhd,bshd->bhqs', qb, k) * scale + cqb[..., None] - c[:, :, None, :]
        qpos = n * BLOCK + jnp.arange(BLOCK)
        mask = key_pos[None, :] <= qpos[:, None]
        sc = jnp.where(mask[None, None], sc, NEG)
        p = jax.nn.softmax(sc, axis=-1)
        return jnp.einsum('bhqs,bshd->bqhd', p, v)

    out = lax.map(block_fn, (q_blocks, cq_blocks, jnp.arange(nb)))
    return out.transpose(1, 0, 2, 3, 4).reshape(b, s, h * d)


def memory_attention(q, mk, mv, q_gain, k_gain):
    b, s, h, d = q.shape
    q = rms_norm(q, q_gain).astype(jnp.float32)
    mk = rms_norm(mk, k_gain).astype(jnp.float32)
    sc = jnp.einsum('bthd,bmhd->bhtm', q, mk) * (d ** -0.5)
    p = jax.nn.softmax(sc, axis=-1)
    out = jnp.einsum('bhtm,bmhd->bthd', p, mv.astype(jnp.float32))
    return out.reshape(b, s, h * d)


def hybrid_layer(x, mem, norm_gain, mem_norm_gain, w_in, b_forget,
                 q_gain_a, k_gain_a, sinks_a, q_gain_b, k_gain_b, q_gain_c, k_gain_c,
                 w_mem_kv, w_branch_a, w_branch_b, w_branch_c, w_out):
    b, s, _ = x.shape
    hn = rms_norm(x, norm_gain)
    proj = hn @ w_in
    (qa, ka, va, za, qb, kb, vb, zb, fb, qc, zc, gate_logits) = jnp.split(proj, IN_OFFSETS, axis=-1)

    ya = sliding_window_attention(qa.reshape(b, s, A_Q_HEADS, HEAD_DIM),
                                  ka.reshape(b, s, A_KV_HEADS, HEAD_DIM),
                                  va.reshape(b, s, A_KV_HEADS, HEAD_DIM),
                                  q_gain_a, k_gain_a, sinks_a).astype(x.dtype)
    ua = (ya * jax.nn.silu(za)) @ w_branch_a

    yb = forgetting_attention(qb.reshape(b, s, B_HEADS, HEAD_DIM),
                              kb.reshape(b, s, B_HEADS, HEAD_DIM),
                              vb.reshape(b, s, B_HEADS, HEAD_DIM),
                              fb + b_forget, q_gain_b, k_gain_b).astype(x.dtype)
    ub = (yb * jax.nn.silu(zb)) @ w_branch_b

    mkv = rms_norm(mem, mem_norm_gain) @ w_mem_kv
    mk, mv = jnp.split(mkv, 2, axis=-1)
    mlen = mem.shape[1]
    yc = memory_attention(qc.reshape(b, s, C_HEADS, C_HEAD_DIM),
                          mk.reshape(b, mlen, C_HEADS, C_HEAD_DIM),
                          mv.reshape(b, mlen, C_HEADS, C_HEAD_DIM),
                          q_gain_c, k_gain_c).astype(x.dtype)
    uc = (yc * jax.nn.silu(zc)) @ w_branch_c

    g = jax.nn.sigmoid(gate_logits.reshape(b, s, N_BRANCHES, D_MODEL))
    y = g[:, :, 0] * ua + g[:, :, 1] * ub + g[:, :, 2] * uc
    return x + y @ w_out


def setup_inputs(seed: int = 0) -> dict:
    key = jax.random.key(seed)
    ks = jax.random.split(key, 20)
    f32 = jnp.float32
    nrm = lambda k, shape: jax.random.normal(k, shape, f32)
    return {
        "x": nrm(ks[0], (BATCH, SEQ, D_MODEL)),
        "mem": nrm(ks[1], (BATCH, MEM_TOKENS, D_MODEL)),
        "norm_gain": 1.0 + 0.02 * nrm(ks[2], (DEPTH, D_MODEL)),
        "mem_norm_gain": 1.0 + 0.02 * nrm(ks[3], (DEPTH, D_MODEL)),
        "w_in": nrm(ks[4], (DEPTH, D_MODEL, IN_WIDTH)) * D_MODEL ** -0.5,
        "b_forget": 3.0 + 0.5 * nrm(ks[5], (DEPTH, B_HEADS)),
        "q_gain_a": 1.0 + 0.02 * nrm(ks[6], (DEPTH, HEAD_DIM)),
        "k_gain_a": 1.0 + 0.02 * nrm(ks[7], (DEPTH, HEAD_DIM)),
        "sinks_a": 0.5 * nrm(ks[8], (DEPTH, A_Q_HEADS)),
        "q_gain_b": 1.0 + 0.02 * nrm(ks[9], (DEPTH, HEAD_DIM)),
        "k_gain_b": 1.0 + 0.02 * nrm(ks[10], (DEPTH, HEAD_DIM)),
        "q_gain_c": 1.0 + 0.02 * nrm(ks[11], (DEPTH, C_HEAD_DIM)),
        "k_gain_c": 1.0 + 0.02 * nrm(ks[12], (DEPTH, C_HEAD_DIM)),
        "w_mem_kv": nrm(ks[13], (DEPTH, D_MODEL, 2 * C_WIDTH)) * D_MODEL ** -0.5,
        "w_branch_a": nrm(ks[14], (DEPTH, A_WIDTH, D_MODEL)) * A_WIDTH ** -0.5,
        "w_branch_b": nrm(ks[15], (DEPTH, B_WIDTH, D_MODEL)) * B_WIDTH ** -0.5,
        "w_branch_c": nrm(ks[16], (DEPTH, C_WIDTH, D_MODEL)) * C_WIDTH ** -0.5,
        "w_out": nrm(ks[17], (DEPTH, D_MODEL, D_MODEL)) * D_MODEL ** -0.5,
    }


def reference(x, mem, norm_gain, mem_norm_gain, w_in, b_forget,
              q_gain_a, k_gain_a, sinks_a, q_gain_b, k_gain_b, q_gain_c, k_gain_c,
              w_mem_kv, w_branch_a, w_branch_b, w_branch_c, w_out):
    for layer in range(DEPTH):
        x = hybrid_layer(x, mem, norm_gain[layer], mem_norm_gain[layer], w_in[layer], b_forget[layer],
                         q_gain_a[layer], k_gain_a[layer], sinks_a[layer],
                         q_gain_b[layer], k_gain_b[layer], q_gain_c[layer], k_gain_c[layer],
                         w_mem_kv[layer], w_branch_a[layer], w_branch_b[layer], w_branch_c[layer],
                         w_out[layer])
    return x
```

```python
import numpy as np
from contextlib import ExitStack
import concourse.bass as bass
import concourse.mybir as mybir
from concourse.bass_utils import run_bass_kernel_spmd

F32 = mybir.dt.float32
BF16 = mybir.dt.bfloat16
ALU = mybir.AluOpType
AF = mybir.ActivationFunctionType
AX = mybir.AxisListType

D = 2048
KC = 16
NB = 32
NS = 8
EPS = 1e-6
INW = 12300
OFF = dict(qA=0, kA=768, vA=1024, zA=1280, qB=2048, kB=2816, vB=3584, zB=4352, fB=5120,
           qC=5132, zC=5644, G=6156)
NEGM = -30000.0


class _FakeIns:
    def then_inc(self, *a, **k):
        return self


class _CostProbe:
    def __init__(self):
        self.cost = 0.3
        self.dma_time = 0.0

    @staticmethod
    def _n(ap):
        n = 1
        for d in ap.shape[1:]:
            n *= int(d)
        return n

    def matmul(self, out=None, lhsT=None, rhs=None, **k):
        f32 = (rhs.dtype == F32)
        self.cost = max(self._n(rhs), 64) / 2400.0 * (4.0 if f32 else 1.0) + 0.035
        return _FakeIns()

    def transpose(self, out=None, in_=None, identity=None, **k):
        self.cost = 0.09
        return _FakeIns()

    def activation(self, out=None, in_=None, **k):
        self.cost = 0.25 + self._n(in_) / 1200.0 + (0.1 if k.get("accum_out") is not None else 0.0)
        return _FakeIns()

    def dma_start(self, out=None, in_=None, **k):
        esz = 2 if out.dtype == BF16 else 4
        nbytes = int(out.shape[0]) * self._n(out) * max(esz, 2 if in_.dtype == BF16 else 4)
        self.cost = 0.08
        self.dma_time = 2.0 + nbytes / 180e3
        return _FakeIns()

    def __getattr__(self, name):
        def f(*a, **k):
            out = k.get("out", a[0] if a else None)
            n = self._n(out) if out is not None and hasattr(out, "shape") else 64
            self.cost = 0.16 + n / 960.0
            return _FakeIns()
        return f


class Prog:
    ENGS = ("pe", "act", "dve", "pool", "sp")

    def __init__(self, nc):
        self.nc = nc
        self.ops = []
        self.phase = 0

    def op(self, eng, fn, reads=(), writes=(), dma=None):
        if dma is not None:
            km = self.__dict__.setdefault("_keymap", {})
            kc_ = self.__dict__.setdefault("_keycnt", {})
            kk = (self.phase, eng, dma)
            if kk not in km:
                n_ = kc_.get((self.phase, eng), 0)
                kc_[(self.phase, eng)] = n_ + 1
                km[kk] = (eng, n_)
            dma = km[kk]
        self.ops.append(dict(eng=eng, fn=fn, reads=list(reads), writes=list(writes), dma=dma,
                             sync=set(), order=set(), sig=None, need_sig=False, phase=self.phase))

    def barrier(self):
        self.phase += 1

    @staticmethod
    def _is_psum(r):
        name = r
        while isinstance(name, tuple):
            name = name[0]
        return name.startswith("ps")

    def resolve(self):
        import heapq
        ops = self.ops
        n = len(ops)
        last_w = {}
        readers = {}
        dma_count = {}
        for i, o in enumerate(ops):
            pr = _CostProbe()
            o["fn"](pr)
            o["cost"] = pr.cost
            o["dma_time"] = pr.dma_time
            if o["dma"] is not None:
                dma_count[o["dma"]] = dma_count.get(o["dma"], 0) + 1
                o["dma_val"] = 16 * dma_count[o["dma"]]
            deps = {}
            for r in o["reads"]:
                w = last_w.get(r)
                if w is not None:
                    deps[w] = "raw"
                if self._is_psum(r):
                    for rd in readers.get(r, ()):
                        if ops[rd]["eng"] != o["eng"]:
                            deps.setdefault(rd, "psrr")
            for r in o["writes"]:
                w = last_w.get(r)
                if w is not None:
                    deps.setdefault(w, "waw")
                for rd in readers.get(r, ()):
                    deps.setdefault(rd, "war")
            for r in o["reads"]:
                readers.setdefault(r, []).append(i)
            for r in o["writes"]:
                last_w[r] = i
                readers[r] = []
            deps.pop(i, None)
            for d, kind in deps.items():
                y = ops[d]
                if y["phase"] != o["phase"]:
                    continue
                if y["dma"] is not None or o["dma"] is not None or y["eng"] != o["eng"]:
                    o["sync"].add(d)
                elif o["eng"] != "pe":
                    o["sync"].add(d)
                else:
                    o["order"].add(d)
        succ = [[] for _ in range(n)]
        indeg = [0] * n
        for i, o in enumerate(ops):
            for d in o["sync"] | o["order"]:
                succ[d].append(i)
                indeg[i] += 1
        fin = [0.0] * n
        start = [0.0] * n
        eng_free = {e: 0.0 for e in self.ENGS}
        order = {e: [] for e in self.ENGS}
        LAT = 0.35
        nph = self.phase + 1
        byphase = [[] for _ in range(nph)]
        for i, o in enumerate(ops):
            byphase[o["phase"]].append(i)
        tphase = 0.0
        rcause = {}
        self.fin = fin
        self.start = start
        blev = [0.0] * n
        for i in range(n - 1, -1, -1):
            o = ops[i]
            m = 0.0
            for sidx in succ[i]:
                if blev[sidx] > m:
                    m = blev[sidx]
            blev[i] = m + o["cost"] + (o["dma_time"] if o["dma"] is not None else 0.0)
        PRIO = getattr(self, "prio_mode", 1)
        for ph in range(nph):
            ready = {e: [] for e in self.ENGS}
            rtime = {}
            for i in byphase[ph]:
                if indeg[i] == 0:
                    rtime[i] = tphase
                    ready[ops[i]["eng"]].append(i)
            left = len(byphase[ph])
            while left:
                best = None
                for e in self.ENGS:
                    lst = ready[e]
                    if not lst:
                        continue
                    ef = eng_free[e]
                    cand = None
                    for i in lst:
                        rt = rtime[i]
                        st_ = rt if rt > ef else ef
                        if PRIO:
                            key = (st_, -blev[i], i) if st_ > ef else (ef, -blev[i], i)
                        else:
                            key = (st_, i, i)
                        if cand is None or key < cand[0]:
                            cand = (key, st_, i)
                    if best is None or cand[0] < best[0]:
                        best = (cand[0], cand[1], cand[2], e)
                _, st_, i, e = best
                ready[e].remove(i)
                o = ops[i]
                start[i] = st_
                o['crit'] = ('eng', order[e][-1]) if (order[e] and eng_free[e] >= rtime[i]) else ('dep', rcause.get(i))
                eng_free[e] = st_ + o["cost"]
                fin[i] = st_ + o["cost"] + (o["dma_time"] if o["dma"] is not None else 0.0)
                order[e].append(i)
                left -= 1
                for sidx in succ[i]:
                    indeg[sidx] -= 1
                    same = (ops[sidx]["eng"] == e and o["dma"] is None and ops[sidx]["dma"] is None)
                    t_ = fin[i] + (0.0 if same else LAT)
                    if t_ > rtime.get(sidx, tphase):
                        rcause[sidx] = i
                    rtime[sidx] = max(rtime.get(sidx, tphase), t_)
                    if indeg[sidx] == 0:
                        ready[ops[sidx]["eng"]].append(sidx)
            t_prev = tphase
            tphase = max([tphase] + [fin[i] for i in byphase[ph]]) + 2.0
            self.phase_span = getattr(self, 'phase_span', []) + [tphase - t_prev]
            for e in self.ENGS:
                eng_free[e] = max(eng_free[e], tphase)
        self.sim_time = tphase
        self.order = order
        posn = {}
        for e in self.ENGS:
            for k_, idx in enumerate(order[e]):
                posn[idx] = k_
        for o in ops:
            best = {}
            keep = set()
            for d in o["sync"]:
                y = ops[d]
                if y["dma"] is not None:
                    keep.add(d)
                else:
                    b_ = best.get(y["eng"])
                    if b_ is None or posn[d] > posn[b_]:
                        best[y["eng"]] = d
            keep.update(best.values())
            o["sync"] = keep
            for d in keep:
                if ops[d]["dma"] is None:
                    ops[d]["need_sig"] = True
        self.bar_wait = []
        cnt = {e: 0 for e in self.ENGS}
        pos = {e: 0 for e in self.ENGS}
        dma_hi = {}
        for ph in range(nph):
            self.bar_wait.append(dict([(("eng", e), cnt[e]) for e in self.ENGS if cnt[e] > 0]
                                      + [(("dma", k), v) for k, v in dma_hi.items()]))
            for e in self.ENGS:
                lst = order[e]
                lastc = None
                p0 = pos[e]
                while pos[e] < len(lst) and ops[lst[pos[e]]]["phase"] == ph:
                    pos[e] += 1
                for idx in lst[p0:pos[e]]:
                    if ops[idx]["dma"] is None:
                        lastc = idx
                if lastc is not None and ph < nph - 1:
                    ops[lastc]["need_sig"] = True
                for idx in lst[p0:pos[e]]:
                    oo = ops[idx]
                    if oo["dma"] is None:
                        if oo["need_sig"]:
                            cnt[e] += 1
                            oo["sig"] = cnt[e]
                    else:
                        dma_hi[oo["dma"]] = max(dma_hi.get(oo["dma"], 0), oo["dma_val"])
        self.dma_keys = sorted(dma_count.keys(), key=str)
        self.dma_final = dict(dma_hi)

    def emit(self):
        nc = self.nc
        ops = self.ops
        self.resolve()
        with ExitStack() as st:
            sems = {}
            for e in self.ENGS:
                sems[("eng", e)] = st.enter_context(nc.semaphore("s_" + e))
            for n_, k in enumerate(self.dma_keys):
                sems[("dma", k)] = st.enter_context(nc.semaphore("d%d" % n_))
            block = st.enter_context(nc.Block())

            def run(engname, eng):
                waited = {}

                def wait(key, val):
                    if val <= 0 or waited.get(key, 0) >= val:
                        return
                    eng.wait_ge(sems[key], val)
                    waited[key] = val
                cur_phase = 0
                for idx in self.order[engname]:
                    o = ops[idx]
                    if o["phase"] != cur_phase:
                        cur_phase = o["phase"]
                        for key, val in self.bar_wait[cur_phase].items():
                            if key == ("eng", engname):
                                continue
                            wait(key, val)
                    for d in sorted(o["sync"]):
                        y = ops[d]
                        if y["dma"] is not None:
                            wait(("dma", y["dma"]), y["dma_val"])
                        else:
                            wait(("eng", y["eng"]), y["sig"])
                    ins = o["fn"](eng)
                    if o["dma"] is not None:
                        ins.then_inc(sems[("dma", o["dma"])], 16)
                    elif o["need_sig"]:
                        ins.then_inc(sems[("eng", engname)], 1)
                if engname == "sp":
                    for k, v in self.dma_final.items():
                        wait(("dma", k), v)

            @block.tensor
            def _(e):
                run("pe", e)

            @block.scalar
            def _(e):
                run("act", e)

            @block.vector
            def _(e):
                run("dve", e)

            @block.gpsimd
            def _(e):
                run("pool", e)

            @block.sync
            def _(e):
                run("sp", e)


def build_nc(debug=None, stop_after=None):
    debug = debug or []
    nc = bass.Bass("TRN2", target_bir_lowering=False)
    dr = lambda name, shape, dt=F32: nc.dram_tensor(name, shape, dt, kind="ExternalInput").ap()
    xv = dr("xv", [NB * 128, D])
    padm_d = dr("padm", [128, NB])
    mem_d = dr("mem", [256, D])
    ng_d = dr("norm_gain", [1, D])
    mg_d = dr("mem_norm_gain", [1, D])
    w_in = dr("w_in", [D, INW])
    bf_d = dr("b_forget", [1, 12])
    gqa_d = dr("q_gain_a", [1, 64]); gka_d = dr("k_gain_a", [1, 64])
    gqb_d = dr("q_gain_b", [1, 64]); gkb_d = dr("k_gain_b", [1, 64])
    gqc_d = dr("q_gain_c", [1, 128]); gkc_d = dr("k_gain_c", [1, 128])
    sinks_d = dr("sinks2", [2, 6])
    wmkv = dr("w_mem_kv", [D, 1024])
    wba = dr("w_branch_a", [768, D]); wbb = dr("w_branch_b", [768, D]); wbc = dr("w_branch_c", [512, D])
    wout = dr("w_out", [D, D])
    cmask_d = dr("cmask", [128, 128])
    ab0_d = dr("abias_b0", [128, 12 * 128 * 2])
    ab1_d = dr("abias_b1", [128, 12 * 128 * 2])
    ab00_d = dr("abias_b0s0", [128, 12 * 128 * 2])
    y_d = nc.dram_tensor("y", [NS * 128, D], F32, kind="ExternalOutput").ap()
    dbg_out = {}

    P = Prog(nc)
    st = ExitStack()
    sb = lambda name, shape, dt: st.enter_context(nc.sbuf_tensor(name, shape, dt))
    with st:
        BIGN = 74000
        big = sb("big", [128, BIGN], BF16)
        gain_bc = sb("gain_bc", [128, D], F32)
        xt = [sb("xt0", [128, D], F32), sb("xt1", [128, D], F32)]
        hn = sb("hn", [128, D], BF16)
        hnb = sb("hnb", [128, D], BF16)
        sqjunk = sb("sqjunk", [128, D], BF16)
        hTt = [sb("hTt0", [128, KC, 128], BF16), sb("hTt1", [128, KC, 128], BF16)]
        ksq = sb("ksq", [128, 768], F32)
        kn = sb("kn", [128, 768], BF16)
        qtmp = sb("qtmp", [128, 768], F32)
        ident = sb("ident", [128, 128], BF16)
        identf = sb("identf", [128, 128], F32)
        U = sb("U", [128, 128], F32)
        onesf = sb("onesf", [128, 128], F32)
        onesb = sb("onesb", [128, 128], BF16)
        cmf = sb("cmf", [128, 128], F32)
        cmb = sb("cmb", [128, 128], BF16)
        bf_bc = sb("bf_bc", [128, 12], F32)
        gA = sb("gA", [128, 64], F32); gA2 = sb("gA2", [128, 64], F32)
        gB = sb("gB", [128, 64], F32); gB2 = sb("gB2", [128, 64], F32)
        gC = sb("gC", [128, 128], F32); gC2 = sb("gC2", [128, 128], F32)
        sinkexp = sb("sinkexp", [128, 6], F32)
        padm = sb("padm_s", [128, NB], F32)
        ss_all = sb("ss_all", [128, 64], F32)
        ln_all = sb("ln_all", [128, 64], F32)
        rstd_all = sb("rstd_all", [128, 64], F32)
        kss = sb("kss", [128, 8, 12], F32)
        kln = sb("kln", [128, 8, 12], F32)
        krs = sb("krs", [128, 8, 12], F32)
        fl = sb("fl", [128, 4, 12], F32)
        lneg = sb("lneg", [128, 4, 12], F32)
        cumprev = sb("cumprev", [128, 12], F32)
        n_all = sb("n_all", [128, NB, 12], F32)
        Ntab = sb("Ntab", [128, NS, 12], F32)
        btab = sb("btab", [128, NB, 12], F32)
        Dfull = sb("Dfull", [128, NS, 12], F32)
        Dpair = sb("Dpair", [128, NS, 6], F32)
        psall = st.enter_context(nc.psum_tensor("psall", [128, 4096], F32))

        def bank(b, n=1):
            return psall[:, b * 512:(b + n) * 512]

        def bankb(b, n=1):
            return psall[:, b * 512:(b + n) * 512].bitcast(BF16)

        def bigv(off, n):
            return big[:, off:off + n]

        def v3(off, a, b):
            return big[:, off:off + a * b].rearrange("p (a b) -> p a b", b=b)

        def dump(name, ap, res, dt):
            if name not in debug:
                return
            shape = list(ap.shape)
            t = nc.dram_tensor("dbg_" + name, shape, dt, kind="ExternalOutput").ap()
            dbg_out[name] = t
            P.op("sp", lambda e: e.dma_start(out=t, in_=ap), reads=res, dma="dbg_" + name)

        A = P.op
        A("sp", lambda e: e.dma_start(out=gain_bc[:], in_=ng_d.partition_broadcast(128)), writes=["gain"], dma="gain")
        A("sp", lambda e: e.dma_start(out=bf_bc[:], in_=bf_d.partition_broadcast(128)), writes=["bf_bc"], dma="c0")
        A("sp", lambda e: e.dma_start(out=gA[:], in_=gqa_d.partition_broadcast(128)), writes=["gA"], dma="c1")
        A("sp", lambda e: e.dma_start(out=gA2[:], in_=gka_d.partition_broadcast(128)), writes=["gA2"], dma="c2")
        A("sp", lambda e: e.dma_start(out=gB[:], in_=gqb_d.partition_broadcast(128)), writes=["gB"], dma="c3")
        A("sp", lambda e: e.dma_start(out=gB2[:], in_=gkb_d.partition_broadcast(128)), writes=["gB2"], dma="c4")
        A("sp", lambda e: e.dma_start(out=gC[:], in_=gqc_d.partition_broadcast(128)), writes=["gC"], dma="c5")
        A("sp", lambda e: e.dma_start(out=gC2[:], in_=gkc_d.partition_broadcast(128)), writes=["gC2"], dma="c6")
        A("sp", lambda e: e.dma_start(out=sinkexp[0:64, :], in_=sinks_d[0:1, :].partition_broadcast(64)),
          writes=["sk0"], dma="c7")
        A("sp", lambda e: e.dma_start(out=sinkexp[64:128, :], in_=sinks_d[1:2, :].partition_broadcast(64)),
          writes=["sk1"], dma="c8")
        A("sp", lambda e: e.dma_start(out=padm[:], in_=padm_d), writes=["padm"], dma="c9")
        A("sp", lambda e: e.dma_start(out=cmf[:], in_=cmask_d), writes=["cmf"], dma="c10")
        A("dve", lambda e: e.memset(identf[:], 1.0), writes=["identf"])
        A("pool", lambda e: e.affine_select(out=identf[:], in_=identf[:], pattern=[[-1, 128]],
                                            compare_op=ALU.is_equal, fill=0.0, base=0, channel_multiplier=1),
          reads=["identf"], writes=["identf"])
        A("dve", lambda e: e.tensor_copy(out=ident[:], in_=identf[:]), reads=["identf"], writes=["ident"])
        A("dve", lambda e: e.memset(U[:], 1.0), writes=["U"])
        A("pool", lambda e: e.affine_select(out=U[:], in_=U[:], pattern=[[1, 128]],
                                            compare_op=ALU.is_ge, fill=0.0, base=0, channel_multiplier=-1),
          reads=["U"], writes=["U"])
        A("dve", lambda e: e.memset(onesf[:], 1.0), writes=["onesf"])
        A("dve", lambda e: e.memset(onesb[:], 1.0), writes=["onesb"])
        A("dve", lambda e: e.memset(cumprev[:], 0.0), writes=["cumprev"])
        A("dve", lambda e: e.tensor_copy(out=cmb[:], in_=cmf[:]), reads=["cmf"], writes=["cmb"])
        A("dve", lambda e: e.scalar_tensor_tensor(out=gA[:], in0=gA[:], scalar=0.125, in1=gA2[:],
                                                  op0=ALU.mult, op1=ALU.mult), reads=["gA", "gA2"], writes=["gA"])
        A("dve", lambda e: e.scalar_tensor_tensor(out=gB[:], in0=gB[:], scalar=0.125, in1=gB2[:],
                                                  op0=ALU.mult, op1=ALU.mult), reads=["gB", "gB2"], writes=["gB"])
        A("dve", lambda e: e.scalar_tensor_tensor(out=gC[:], in0=gC[:], scalar=float(128 ** -0.5), in1=gC2[:],
                                                  op0=ALU.mult, op1=ALU.mult), reads=["gC", "gC2"], writes=["gC"])
        A("act", lambda e: e.activation(out=sinkexp[:], in_=sinkexp[:], func=AF.Exp),
          reads=["sk0", "sk1"], writes=["sinkexp"])

        w_in_v = w_in.rearrange("(kc p) n -> p kc n", p=128)


        def load_w4(dst, src, resname, key):
            for q4 in range(4):
                A("pool", lambda e, q4=q4: e.dma_start(out=dst[:, 4 * q4:4 * q4 + 4, :], in_=src[:, 4 * q4:4 * q4 + 4, :]),
                  writes=[(resname, q4)], dma=(key, q4))

        def norm_block(tag, idx, src_ap, dst_lo, dst_hi, dst_res, junk_ap, junk_res):
            s = idx % 2
            hx = hn if idx % 2 == 0 else hnb
            hres_ = ("hn", idx % 2)
            A("sp", lambda e: e.dma_start(out=xt[s][:], in_=src_ap), writes=[("xt", s)], dma=("xt", s))
            A("act", lambda e: e.activation(out=junk_ap, in_=xt[s][:], func=AF.Square, accum_out=ss_all[:, idx:idx + 1]),
              reads=[("xt", s)], writes=junk_res + [("ss", idx)])
            A("act", lambda e: e.activation(out=ln_all[:, idx:idx + 1], in_=ss_all[:, idx:idx + 1], func=AF.Ln,
                                            scale=1.0 / D, bias=EPS), reads=[("ss", idx)], writes=[("ln", idx)])
            A("act", lambda e: e.activation(out=rstd_all[:, idx:idx + 1], in_=ln_all[:, idx:idx + 1], func=AF.Exp,
                                            scale=-0.5), reads=[("ln", idx)], writes=[("rstd", idx)])
            A("dve", lambda e: e.scalar_tensor_tensor(out=hx[:], in0=xt[s][:], scalar=rstd_all[:, idx:idx + 1],
                                                      in1=gain_bc[:], op0=ALU.mult, op1=ALU.mult),
              reads=[("xt", s), ("rstd", idx), "gain"], writes=[hres_])
            pT = bankb(0, 2)
            for kc in range(KC):
                A("pe", lambda e, kc=kc: e.transpose(out=pT[:, kc * 128:(kc + 1) * 128],
                                                     in_=hx[:, kc * 128:(kc + 1) * 128], identity=ident[:]),
                  reads=[hres_, "ident"], writes=["ps0" if kc < 8 else "ps1"])
            A("act", lambda e: e.activation(out=dst_lo, in_=pT[:, 0:1024].rearrange("p (a b) -> p a b", b=128),
                                            func=AF.Copy), reads=["ps0"], writes=[dst_res + ("lo",)])
            A("dve", lambda e: e.tensor_copy(out=dst_hi, in_=pT[:, 1024:2048].rearrange("p (a b) -> p a b", b=128)),
              reads=["ps1"], writes=[dst_res + ("hi",)])

        def proj_tm(lhs_fn, lhs_res, w_view, w_res, c0, n, out_ap, out_res):
            for kc in range(KC):
                A("pe", lambda e, kc=kc: e.matmul(out=out_ap, lhsT=lhs_fn(kc), rhs=w_view[:, kc, c0:c0 + n],
                                                  start=(kc == 0), stop=(kc == KC - 1)),
                  reads=[lhs_res + ("lo",) if kc < 8 else lhs_res + ("hi",), (w_res, kc // 4)], writes=out_res)

        def head_rstd(tag, idx, ps_ap, ps_res, nh, dh):
            idx = idx % 8
            A("act", lambda e: e.activation(out=ksq[:, 0:nh * dh], in_=ps_ap, func=AF.Square),
              reads=ps_res, writes=["ksq"])
            A("dve", lambda e: e.tensor_reduce(out=kss[:, idx, 0:nh],
                                               in_=ksq[:, 0:nh * dh].rearrange("p (h d) -> p h d", d=dh),
                                               axis=AX.X, op=ALU.add), reads=["ksq"], writes=[("kss", idx)])
            A("act", lambda e: e.activation(out=kln[:, idx, 0:nh], in_=kss[:, idx, 0:nh], func=AF.Ln,
                                            scale=1.0 / dh, bias=EPS), reads=[("kss", idx)], writes=[("kln", idx)])
            A("act", lambda e: e.activation(out=krs[:, idx, 0:nh], in_=kln[:, idx, 0:nh], func=AF.Exp, scale=-0.5),
              reads=[("kln", idx)], writes=[("krs", idx)])

        W1a = v3(0, KC, 1548)
        QT_B = v3(24768, 6, 1024)
        KT_B = v3(30912, 6, 4096)
        load_w4(W1a[:, :, 0:768], w_in_v[:, :, OFF["kB"]:OFF["kB"] + 768], "W1a_k", "W1a_k")
        load_w4(W1a[:, :, 768:780], w_in_v[:, :, OFF["fB"]:OFF["fB"] + 12], "W1a_f", "W1a_f")
        load_w4(W1a[:, :, 780:1548], w_in_v[:, :, OFF["qB"]:OFF["qB"] + 768], "W1a_q", "W1a_q")
        pKT = bankb(6)
        for j in range(NB):
            s = j % 2
            hres = ("hTt", s)
            norm_block("c1", j, xv[j * 128:(j + 1) * 128, :], hTt[s][:, 0:8, :], hTt[s][:, 8:16, :], hres,
                       sqjunk[:], ["sqjunk"])
            lhs = lambda kc, s=s: hTt[s][:, kc, :]
            kb0 = 2 + 2 * (j % 2)
            kr0, kr1 = "ps%d" % kb0, "ps%d" % (kb0 + 1)
            proj_tm(lhs, hres, W1a, "W1a_k", 0, 512, bank(kb0)[:, 0:512], [kr0])
            for kc in range(KC):
                A("pe", lambda e, kc=kc, s=s, kb0=kb0: e.matmul(out=bank(kb0 + 1)[:, 0:268], lhsT=hTt[s][:, kc, :],
                                                       rhs=W1a[:, kc, 512:780], start=(kc == 0), stop=(kc == KC - 1)),
                  reads=[hres + ("lo",) if kc < 8 else hres + ("hi",), ("W1a_k", kc // 4), ("W1a_f", kc // 4)], writes=[kr1])
            pK = psall[:, kb0 * 512:kb0 * 512 + 768]
            head_rstd("k", j, pK, [kr0, kr1], 12, 64)
            A("dve", lambda e, j=j, pK=pK: e.tensor_tensor(out=kn[:].rearrange("p (h d) -> p h d", d=64),
                                                    in0=pK.rearrange("p (h d) -> p h d", d=64),
                                                    in1=krs[:, j % 8, 0:12].unsqueeze(2).to_broadcast([128, 12, 64]),
                                                    op=ALU.mult),
              reads=[kr0, kr1, ("krs", j % 8)], writes=["kn"])
            for hp in range(6):
                A("pe", lambda e, hp=hp: e.transpose(out=pKT[:, hp * 128:(hp + 1) * 128],
                                                     in_=kn[:, hp * 128:(hp + 1) * 128], identity=ident[:]),
                  reads=["kn", "ident"], writes=["ps6"])
            A("act", lambda e, j=j: e.activation(out=KT_B[:, :, j * 128:(j + 1) * 128],
                                                 in_=pKT[:, 0:768].rearrange("p (a b) -> p a b", b=128), func=AF.Copy),
              reads=["ps6"], writes=[("KT_B", j)])
            A("dve", lambda e, j=j, kb0=kb0: e.tensor_tensor(out=fl[:, j % 4, :], in0=bank(kb0 + 1)[:, 256:268], in1=bf_bc[:], op=ALU.add),
              reads=[kr1, "bf_bc"], writes=[("fl", j % 4)])
            A("act", lambda e, j=j: e.activation(out=fl[:, j % 4, :], in_=fl[:, j % 4, :], func=AF.Exp, scale=-1.0),
              reads=[("fl", j % 4)], writes=[("fl", j % 4)])
            A("act", lambda e, j=j: e.activation(out=lneg[:, j % 4, :], in_=fl[:, j % 4, :], func=AF.Ln, bias=1.0),
              reads=[("fl", j % 4)], writes=[("lneg", j % 4)])
            A("pe", lambda e, j=j: e.matmul(out=bank(7)[:, 0:12], lhsT=U[:], rhs=lneg[:, j % 4, :], start=True, stop=False),
              reads=["U", ("lneg", j % 4)], writes=["ps7"])
            A("pe", lambda e: e.matmul(out=bank(7)[:, 0:12], lhsT=onesf[:], rhs=cumprev[:], start=False, stop=True),
              reads=["onesf", "cumprev"], writes=["ps7"])
            A("dve", lambda e, j=j: e.tensor_copy(out=n_all[:, j, :], in_=bank(7)[:, 0:12]),
              reads=["ps7"], writes=[("n_all", j)])
            A("dve", lambda e, j=j: e.tensor_tensor(out=cumprev[:], in0=cumprev[:], in1=lneg[:, j % 4, :], op=ALU.add),
              reads=["cumprev", ("lneg", j % 4)], writes=["cumprev"])
            if j % 4 == 3:
                g = j // 4
                A("pe", lambda e: e.matmul(out=bank(7)[:, 16:28], lhsT=onesf[:], rhs=cumprev[:], start=True, stop=True),
                  reads=["onesf", "cumprev"], writes=["ps7"])
                A("dve", lambda e, g=g: e.tensor_copy(out=Ntab[:, g, :], in_=bank(7)[:, 16:28]),
                  reads=["ps7"], writes=[("Ntab", g)])
                pQ = psall[:, 2 * 512:2 * 512 + 768]
                proj_tm(lhs, hres, W1a, "W1a_q", 780, 512, bank(2)[:, 0:512], ["ps2"])
                proj_tm(lhs, hres, W1a, "W1a_q", 1292, 256, bank(3)[:, 0:256], ["ps3"])
                head_rstd("q", 32 + g, pQ, ["ps2", "ps3"], 12, 64)
                A("dve", lambda e, g=g: e.tensor_tensor(out=qtmp[:].rearrange("p (h d) -> p h d", d=64),
                                                        in0=pQ.rearrange("p (h d) -> p h d", d=64),
                                                        in1=krs[:, (32 + g) % 8, 0:12].unsqueeze(2).to_broadcast([128, 12, 64]),
                                                        op=ALU.mult),
                  reads=["ps2", "ps3", ("krs", (32 + g) % 8)], writes=["qtmp"])
                A("dve", lambda e: e.tensor_tensor(out=kn[:].rearrange("p (h d) -> p h d", d=64),
                                                   in0=qtmp[:].rearrange("p (h d) -> p h d", d=64),
                                                   in1=gB[:].unsqueeze(1).to_broadcast([128, 12, 64]), op=ALU.mult),
                  reads=["qtmp", "gB"], writes=["kn"])
                for hp in range(6):
                    A("pe", lambda e, hp=hp: e.transpose(out=pKT[:, hp * 128:(hp + 1) * 128],
                                                         in_=kn[:, hp * 128:(hp + 1) * 128], identity=ident[:]),
                      reads=["kn", "ident"], writes=["ps6"])
                A("dve", lambda e, g=g: e.tensor_copy(out=QT_B[:, :, g * 128:(g + 1) * 128],
                                                      in_=pKT[:, 0:768].rearrange("p (a b) -> p a b", b=128)),
                  reads=["ps6"], writes=[("QT_B", g)])
        dump("KT_B", KT_B, [("KT_B", j) for j in range(NB)], BF16)
        dump("QT_B", QT_B, [("QT_B", g) for g in range(NS)], BF16)
        dump("n_all", n_all[:], [("n_all", j) for j in range(NB)], F32)
        dump("Ntab", Ntab[:], [("Ntab", g) for g in range(NS)], F32)
        if stop_after == "1a":
            A("sp", lambda e: e.dma_start(out=y_d[0:128, :], in_=xt[0][:]), reads=[("xt", 0)], dma="y")
            P.emit()
            return nc, dbg_out
        P.barrier()

        W1b = v3(0, KC, 768)
        V_Blo = v3(12288, 16, 768)
        V_Bhi = v3(55488, 16, 768)

        def VB(j):
            return V_Blo[:, j, :] if j < 16 else V_Bhi[:, j - 16, :]
        W1A_NAMES = [("W1a_k", q_) for q_ in range(4)] + [("W1a_f", q_) for q_ in range(4)] + [("W1a_q", q_) for q_ in range(4)]
        load_w4(W1b, w_in_v[:, :, OFF["vB"]:OFF["vB"] + 768], "W1b", "W1b")
        for j in range(NB):
            s = j % 2
            hres = ("hTt", s)
            norm_block("c2", j, xv[j * 128:(j + 1) * 128, :], hTt[s][:, 0:8, :], hTt[s][:, 8:16, :], hres,
                       sqjunk[:], ["sqjunk"])
            lhs = lambda kc, s=s: hTt[s][:, kc, :]
            b0 = 2 + 2 * (j % 2)
            proj_tm(lhs, hres, W1b, "W1b", 0, 512, bank(b0)[:, 0:512], ["ps%d" % b0])
            proj_tm(lhs, hres, W1b, "W1b", 512, 256, bank(b0 + 1)[:, 0:256], ["ps%d" % (b0 + 1)])
            pV = psall[:, b0 * 512:b0 * 512 + 768]
            eng = "act" if j % 2 == 0 else "dve"
            if eng == "act":
                A("act", lambda e, j=j, pV=pV: e.activation(out=VB(j), in_=pV, func=AF.Copy),
                  reads=["ps%d" % b0, "ps%d" % (b0 + 1)], writes=[("V_B", j)])
            else:
                A("dve", lambda e, j=j, pV=pV: e.tensor_copy(out=VB(j), in_=pV),
                  reads=["ps%d" % b0, "ps%d" % (b0 + 1)], writes=[("V_B", j)])
        dump("V_Blo", V_Blo, [("V_B", j) for j in range(16)], BF16)
        dump("V_Bhi", V_Bhi, [("V_B", j) for j in range(16, 32)], BF16)
        if stop_after == "1b":
            A("sp", lambda e: e.dma_start(out=y_d[0:128, :], in_=xt[0][:]), reads=[("xt", 0)], dma="y")
            P.emit()
            return nc, dbg_out
        P.barrier()

        ybT = v3(0, 6, 1024)
        PT = [bigv(6144, 512), bigv(6656, 512), bigv(11264, 512), bigv(11776, 512)]
        Anum = bigv(7168, 1024).bitcast(F32)
        Aden = bigv(8192, 1024).bitcast(F32)
        rden = bigv(9216, 1024).bitcast(F32)
        VX = [big[:, 10240:11264].rearrange("p (r e c) -> p r e c", r=4, e=2),
              big[:, 72000:73024].rearrange("p (r e c) -> p r e c", r=4, e=2)]
        for b_ in range(2):
            A("dve", lambda e, b_=b_: e.memset(VX[b_][:, :, :, 0:64], 1.0), writes=[("VXones", b_)])
        for g in range(NS):
            A("dve", lambda e, g=g: e.tensor_tensor(out=btab[:, 4 * g:4 * g + 4, :], in0=n_all[:, 4 * g:4 * g + 4, :],
                                                    in1=Ntab[:, g, :].unsqueeze(1).to_broadcast([128, 4, 12]),
                                                    op=ALU.subtract),
              reads=[("n_all", 4 * g + r) for r in range(4)] + [("Ntab", g)], writes=[("btab0", g)])
        A("dve", lambda e: e.tensor_tensor(out=btab[:], in0=btab[:],
                                           in1=padm[:].unsqueeze(2).to_broadcast([128, NB, 12]), op=ALU.subtract),
          reads=[("btab0", g) for g in range(NS)] + ["padm"], writes=["btab"])
        A("dve", lambda e: e.tensor_tensor(out=Dfull[:, 1:8, :], in0=Ntab[:, 0:7, :], in1=Ntab[:, 1:8, :], op=ALU.subtract),
          reads=[("Ntab", g) for g in range(NS)], writes=["Dfull"])
        A("act", lambda e: e.activation(out=Dfull[:, 1:8, :], in_=Dfull[:, 1:8, :], func=AF.Exp),
          reads=["Dfull"], writes=["Dfull"])
        for e2 in range(2):
            A("dve", lambda e, e2=e2: e.tensor_copy(out=Dpair[64 * e2:64 * e2 + 64, 1:8, :],
                                                    in_=Dfull[64 * e2:64 * e2 + 64, 1:8, e2::2]),
              reads=["Dfull"], writes=[("Dpair", e2)])
        dump("btab", btab[:], ["btab"], F32)
        dump("Dpair", Dpair[:], [("Dpair", 0), ("Dpair", 1)], F32)
        PASSES = [(0, 3), (4, 7)]
        cnt = 0
        gcnt = 0
        for hp in range(6):
            for (slo, shi) in PASSES:
                qlo_pass = slo * 128
                for g in range(shi + 1):
                    q0 = max(g, slo) * 128
                    Nq = (shi + 1) * 128 - q0
                    gset = gcnt % 2
                    gcnt += 1
                    vpiece = V_Blo if g < 4 else V_Bhi
                    kbl = 4 * g if g < 4 else 4 * g - 16
                    A("pool", lambda e, gset=gset, vpiece=vpiece, kbl=kbl, hp=hp: e.tensor_copy(
                        out=VX[gset][:, :, :, 64:128],
                        in_=vpiece[:, kbl:kbl + 4, hp * 128:(hp + 1) * 128].rearrange("p r (e c) -> p r e c", e=2)),
                      reads=[("V_B", 4 * g + r_) for r_ in range(4)], writes=[("VX", gset)])
                    for r in range(4):
                        kb = 4 * g + r
                        for e2 in range(2):
                            h = 2 * hp + e2
                            sbk = cnt % 4
                            cnt += 1
                            ob = 4 + 2 * gset + e2
                            S = bank(sbk)[:, 0:Nq]
                            diag = (r == 3 and g >= slo)
                            A("pe", lambda e, S=S, e2=e2, hp=hp, kb=kb, q0=q0, Nq=Nq, diag=diag: e.matmul(
                                out=S, lhsT=KT_B[64 * e2:64 * e2 + 64, hp, kb * 128:(kb + 1) * 128],
                                rhs=QT_B[64 * e2:64 * e2 + 64, hp, q0:q0 + Nq], start=True, stop=(not diag)),
                              reads=[("KT_B", kb)] + [("QT_B", i) for i in range(q0 // 128, shi + 1)],
                              writes=["ps%d" % sbk])
                            if diag:
                                A("pe", lambda e, S=S: e.matmul(out=S[:, 0:128], lhsT=ident[:], rhs=cmb[:],
                                                                start=False, stop=True),
                                  reads=["ident", "cmb"], writes=["ps%d" % sbk])
                            A("act", lambda e, S=S, sbk=sbk, Nq=Nq, kb=kb, h=h: e.activation(
                                out=PT[sbk][:, 0:Nq], in_=S, func=AF.Exp, bias=btab[:, kb, h:h + 1], scale=1.0),
                              reads=["ps%d" % sbk, "btab"], writes=[("PT", sbk)])
                            A("pe", lambda e, ob=ob, Nq=Nq, gset=gset, r=r, e2=e2, sbk=sbk: e.matmul(
                                out=bank(ob)[:, 0:Nq], lhsT=VX[gset][:, r, e2, :], rhs=PT[sbk][:, 0:Nq],
                                start=(r == 0), stop=(r == 3)),
                              reads=[("VX", gset), ("VXones", gset), ("PT", sbk)], writes=["ps%d" % ob])
                    a0 = q0 - qlo_pass
                    for e2 in range(2):
                        h = 2 * hp + e2
                        ob = 4 + 2 * gset + e2
                        Ah = Anum if e2 == 0 else Aden
                        ares = "A%d" % e2
                        if g == 0:
                            if e2 == 0:
                                A("dve", lambda e, ob=ob, Nq=Nq, a0=a0, Ah=Ah: e.tensor_copy(out=Ah[:, a0:a0 + Nq], in_=bank(ob)[:, 0:Nq]),
                                  reads=["ps%d" % ob], writes=[ares])
                            else:
                                A("act", lambda e, ob=ob, Nq=Nq, a0=a0, Ah=Ah: e.activation(out=Ah[:, a0:a0 + Nq], in_=bank(ob)[:, 0:Nq],
                                                                                          func=AF.Copy),
                                  reads=["ps%d" % ob], writes=[ares])
                        else:
                            A("dve", lambda e, ob=ob, Nq=Nq, a0=a0, g=g, h=h, Ah=Ah: e.scalar_tensor_tensor(
                                out=Ah[:, a0:a0 + Nq], in0=Ah[:, a0:a0 + Nq], scalar=Dfull[:, g, h:h + 1],
                                in1=bank(ob)[:, 0:Nq], op0=ALU.mult, op1=ALU.add),
                              reads=["ps%d" % ob, ares, "Dfull"], writes=[ares])
                A("dve", lambda e: e.tensor_copy(out=rden[64:128, 0:512], in_=Aden[0:64, 0:512]), reads=["A1"], writes=["rd1"])
                A("dve", lambda e: e.reciprocal(out=rden[64:128, 0:512], in_=rden[64:128, 0:512]), reads=["rd1"], writes=["rd1"])
                A("dve", lambda e, hp=hp, qlo=qlo_pass: e.tensor_tensor(out=ybT[64:128, hp, qlo:qlo + 512], in0=Aden[64:128, 0:512],
                                                                        in1=rden[64:128, 0:512], op=ALU.mult),
                  reads=["A1", "rd1"], writes=[("ybT", hp, slo, 1)])
                A("dve", lambda e: e.tensor_copy(out=rden[0:64, 0:512], in_=Anum[64:128, 0:512]), reads=["A0"], writes=["tm0"])
                A("dve", lambda e: e.reciprocal(out=Anum[0:64, 0:512], in_=Anum[0:64, 0:512]), reads=["A0", "tm0"], writes=["A0"])
                A("dve", lambda e, hp=hp, qlo=qlo_pass: e.tensor_tensor(out=ybT[0:64, hp, qlo:qlo + 512], in0=rden[0:64, 0:512],
                                                                        in1=Anum[0:64, 0:512], op=ALU.mult),
                  reads=["tm0", "A0"], writes=[("ybT", hp, slo, 0)])
        dump("ybT", ybT, [("ybT", hp, slo, e2) for hp in range(6) for slo in (0, 4) for e2 in range(2)], BF16)
        if stop_after == "2b":
            A("sp", lambda e: e.dma_start(out=y_d[0:128, :], in_=xt[0][:]), reads=[("xt", 0)], dma="y")
            P.emit()
            return nc, dbg_out
        P.barrier()

        hT_own = v3(6144, KC, 1024)
        W_kvA = v3(22528, KC, 512)
        KT_A = big[:, 30720:38912].rearrange("p (k b t) -> p k b t", k=4, b=16)
        V_A = v3(38912, 16, 256)
        pKT = bankb(6)
        load_w4(W_kvA, w_in_v[:, :, OFF["kA"]:OFF["kA"] + 512], "W_kvA", "W_kvA")
        junkf = sqjunk[:]
        junkr = ["sqjunk"]
        for i in range(NS):
            for own in (0, 1):
                vb = 4 * i + 2 + own
                blk = 2 * i + own
                if own:
                    hres = ("hT_own", i)
                    norm_block("o", blk, xv[vb * 128:(vb + 1) * 128, :], hT_own[:, 0:8, i * 128:(i + 1) * 128],
                               hT_own[:, 8:16, i * 128:(i + 1) * 128], hres, junkf, junkr)
                    lhs = lambda kc, i=i: hT_own[:, kc, i * 128:(i + 1) * 128]
                else:
                    hres = ("hTt", 0)
                    norm_block("o", blk, xv[vb * 128:(vb + 1) * 128, :], hTt[0][:, 0:8, :], hTt[0][:, 8:16, :],
                               hres, junkf, junkr)
                    lhs = lambda kc: hTt[0][:, kc, :]
                pb = 2 + (blk % 2)
                proj_tm(lhs, hres, W_kvA, "W_kvA", 0, 512, bank(pb)[:, 0:512], ["ps%d" % pb])
                head_rstd("ka", blk, bank(pb)[:, 0:256], ["ps%d" % pb], 4, 64)
                A("dve", lambda e, pb=pb, blk=blk: e.tensor_tensor(
                    out=kn[:, 0:512].rearrange("p (k u d) -> p k u d", k=4, u=2),
                    in0=bank(pb)[:, 0:256].rearrange("p (k d) -> p k d", d=64).unsqueeze(2).to_broadcast([128, 4, 2, 64]),
                    in1=krs[:, blk % 8, 0:4].unsqueeze(2).unsqueeze(3).to_broadcast([128, 4, 2, 64]), op=ALU.mult),
                  reads=["ps%d" % pb, ("krs", blk % 8)], writes=["kn"])
                for k4 in range(4):
                    A("pe", lambda e, k4=k4: e.transpose(out=pKT[:, k4 * 128:(k4 + 1) * 128],
                                                         in_=kn[:, k4 * 128:(k4 + 1) * 128], identity=ident[:]),
                      reads=["kn", "ident"], writes=["ps6"])
                A("act", lambda e, blk=blk: e.activation(out=KT_A[:, :, blk, :],
                                                         in_=pKT[:, 0:512].rearrange("p (a b) -> p a b", b=128), func=AF.Copy),
                  reads=["ps6"], writes=[("KT_A", blk)])
                A("act", lambda e, blk=blk, pb=pb: e.activation(out=V_A[:, blk, :], in_=bank(pb)[:, 256:512], func=AF.Copy),
                  reads=["ps%d" % pb], writes=[("V_A", blk)])
        dump("hT_own", hT_own, [("hT_own", i, x) for i in range(NS) for x in ("lo", "hi")], BF16)
        dump("KT_A", KT_A, [("KT_A", b_) for b_ in range(16)], BF16)
        dump("V_A", V_A, [("V_A", b_) for b_ in range(16)], BF16)
        hT_res = [("hT_own", i, x) for i in range(NS) for x in ("lo", "hi")]
        if stop_after == "2a":
            A("sp", lambda e: e.dma_start(out=y_d[0:128, :], in_=xt[0][:]), reads=[("xt", 0)], dma="y")
            P.emit()
            return nc, dbg_out

        wm = v3(43008, KC, 1024)
        mnT = v3(59392, KC, 256)
        KT_C = v3(63488, 4, 256)
        V_C = v3(64512, 2, 512)
        wm_v = wmkv.rearrange("(kc p) n -> p kc n", p=128)
        A("sp", lambda e: e.dma_start(out=gain_bc[:], in_=mg_d.partition_broadcast(128)), writes=["gain"], dma="gain")
        load_w4(wm[:, :, 0:512], wm_v[:, :, 0:512], "wm0", "wm0")
        load_w4(wm[:, :, 512:1024], wm_v[:, :, 512:1024], "wm1", "wm1")
        for mb in range(2):
            hres = ("mnT", mb)
            norm_block("m", 16 + mb, mem_d[mb * 128:(mb + 1) * 128, :], mnT[:, 0:8, mb * 128:(mb + 1) * 128],
                       mnT[:, 8:16, mb * 128:(mb + 1) * 128], hres, junkf, junkr)
            lhs = lambda kc, mb=mb: mnT[:, kc, mb * 128:(mb + 1) * 128]
            proj_tm(lhs, hres, wm, "wm0", 0, 512, bank(2)[:, 0:512], ["ps2"])
            proj_tm(lhs, hres, wm, "wm1", 512, 512, bank(3)[:, 0:512], ["ps3"])
            head_rstd("kc", 16 + mb, bank(2)[:, 0:512], ["ps2"], 4, 128)
            A("dve", lambda e, mb=mb: e.tensor_tensor(
                out=kn[:, 0:512].rearrange("p (k d) -> p k d", d=128),
                in0=bank(2)[:, 0:512].rearrange("p (k d) -> p k d", d=128),
                in1=krs[:, (16 + mb) % 8, 0:4].unsqueeze(2).to_broadcast([128, 4, 128]), op=ALU.mult),
              reads=["ps2", ("krs", (16 + mb) % 8)], writes=["kn"])
            for k4 in range(4):
                A("pe", lambda e, k4=k4: e.transpose(out=pKT[:, k4 * 128:(k4 + 1) * 128],
                                                     in_=kn[:, k4 * 128:(k4 + 1) * 128], identity=ident[:]),
                  reads=["kn", "ident"], writes=["ps6"])
            A("act", lambda e, mb=mb: e.activation(out=KT_C[:, :, mb * 128:(mb + 1) * 128],
                                                   in_=pKT[:, 0:512].rearrange("p (a b) -> p a b", b=128), func=AF.Copy),
              reads=["ps6"], writes=[("KT_C", mb)])
            A("dve", lambda e, mb=mb: e.tensor_copy(out=V_C[:, mb, :], in_=bank(3)[:, 0:512]),
              reads=["ps3"], writes=[("V_C", mb)])
        dump("KT_C", KT_C, [("KT_C", 0), ("KT_C", 1)], BF16)
        dump("V_C", V_C, [("V_C", 0), ("V_C", 1)], BF16)
        if stop_after == "C1":
            A("sp", lambda e: e.dma_start(out=y_d[0:128, :], in_=xt[0][:]), reads=[("xt", 0)], dma="y")
            P.emit()
            return nc, dbg_out
        P.barrier()

        ws = [v3(22528, KC, 512), v3(43008, KC, 512)]
        wsc = [0]

        def load_ws(col0, n, dst0=0):
            k = wsc[0] % 2
            wsc[0] += 1
            load_w4(ws[k][:, :, dst0:dst0 + n], w_in_v[:, :, col0:col0 + n], ("ws", k), ("ws", k))
            return k

        bctr = [0]

        def next_bank():
            b_ = bctr[0] % 8
            bctr[0] += 1
            return b_

        def q_proj(k, c0, nh, dh, gtile, gres, QT, hp0, tagi):
            n = nh * dh
            nt = n // 128
            for i in range(NS):
                pb = next_bank()
                idx = tagi * 8 + i
                proj_tm(lambda kc, i=i: hT_own[:, kc, i * 128:(i + 1) * 128], ("hT_own", i), ws[k], ("ws", k),
                        c0, n, bank(pb)[:, 0:n], ["ps%d" % pb])
                head_rstd("q", idx, bank(pb)[:, 0:n], ["ps%d" % pb], nh, dh)
                A("dve", lambda e, pb=pb, idx=idx: e.tensor_tensor(
                    out=qtmp[:, 0:n].rearrange("p (h d) -> p h d", d=dh),
                    in0=bank(pb)[:, 0:n].rearrange("p (h d) -> p h d", d=dh),
                    in1=krs[:, idx % 8, 0:nh].unsqueeze(2).to_broadcast([128, nh, dh]), op=ALU.mult),
                  reads=["ps%d" % pb, ("krs", idx % 8)], writes=["qtmp"])
                A("dve", lambda e: e.tensor_tensor(
                    out=kn[:, 0:n].rearrange("p (h d) -> p h d", d=dh),
                    in0=qtmp[:, 0:n].rearrange("p (h d) -> p h d", d=dh),
                    in1=gtile[:].unsqueeze(1).to_broadcast([128, nh, dh]), op=ALU.mult),
                  reads=["qtmp", gres], writes=["kn"])
                pt = next_bank()
                pTb = bankb(pt)
                for t in range(nt):
                    A("pe", lambda e, t=t, pTb=pTb: e.transpose(out=pTb[:, t * 128:(t + 1) * 128],
                                                               in_=kn[:, t * 128:(t + 1) * 128], identity=ident[:]),
                      reads=["kn", "ident"], writes=["ps%d" % pt])
                A("act", lambda e, i=i, pTb=pTb: e.activation(
                    out=QT[:, hp0:hp0 + nt, i * 128:(i + 1) * 128],
                    in_=pTb[:, 0:nt * 128].rearrange("p (a b) -> p a b", b=128), func=AF.Copy),
                  reads=["ps%d" % pt], writes=[("QT", hp0, i)])

        def z_proj(k, c0, nch, SZ, ch0):
            for c in range(nch):
                for half in range(2):
                    pb = next_bank()
                    for kc in range(KC):
                        A("pe", lambda e, kc=kc, c=c, half=half, pb=pb: e.matmul(
                            out=bank(pb)[:, 0:512], lhsT=ws[k][:, kc, c0 + c * 128:c0 + (c + 1) * 128],
                            rhs=hT_own[:, kc, half * 512:(half + 1) * 512], start=(kc == 0), stop=(kc == KC - 1)),
                          reads=[(("ws", k), kc // 4)] + [("hT_own", i, "lo" if kc < 8 else "hi") for i in range(4 * half, 4 * half + 4)],
                          writes=["ps%d" % pb])
                    A("act", lambda e, c=c, half=half, pb=pb: e.activation(
                        out=SZ[:, ch0 + c, half * 512:(half + 1) * 512], in_=bank(pb)[:, 0:512], func=AF.Silu),
                      reads=["ps%d" % pb], writes=[("SZ", ch0 + c, half)])

        QT_A = v3(51200, 6, 1024)
        szA = v3(57344, 6, 1024)
        yazT = v3(65536, 6, 1024)
        T0 = xt[0][:].bitcast(BF16)[:, 0:3072].rearrange("p (h l q) -> p h l q", h=12, l=2)
        T1 = xt[1][:].bitcast(BF16)[:, 0:3072].rearrange("p (h l q) -> p h l q", h=12, l=2)
        T00 = gain_bc[:].bitcast(BF16)[:, 0:3072].rearrange("p (h l q) -> p h l q", h=12, l=2)
        A("pool", lambda e: e.dma_start(out=xt[0][:].bitcast(BF16)[:, 0:3072], in_=ab0_d), writes=[("xt", 0)], dma=("xtp", 0))
        A("pool", lambda e: e.dma_start(out=xt[1][:].bitcast(BF16)[:, 0:3072], in_=ab1_d), writes=[("xt", 1)], dma=("xtp", 1))
        A("pool", lambda e: e.dma_start(out=gain_bc[:].bitcast(BF16)[:, 0:3072], in_=ab00_d), writes=["gain"], dma="gainp")
        k = load_ws(OFF["qA"], 512)
        q_proj(k, 0, 8, 64, gA, "gA", QT_A, 0, 0)
        k = load_ws(OFF["qA"] + 512, 256)
        q_proj(k, 0, 4, 64, gA, "gA", QT_A, 4, 1)
        k = load_ws(OFF["zA"], 512)
        z_proj(k, 0, 4, szA, 0)
        k = load_ws(OFF["zA"] + 512, 256)
        z_proj(k, 0, 2, szA, 4)
        PTa = [hn[:, 0:512], hn[:, 512:1024]]
        tmpf = hn[:, 1024:2048].bitcast(F32)
        QA_res = lambda hp, i: [("QT", 0 if hp < 4 else 4, i)]
        pcnt = 0
        for i in range(NS):
            for hp in range(6):
                sb_ = pcnt % 2
                pcnt += 1
                S = bank(sb_)
                OD = bank(2 + sb_)
                for e2 in range(2):
                    h = 2 * hp + e2
                    kvh = h // 3
                    for blk in range(2):
                        Tt = (T00 if i == 0 else T0) if blk == 0 else T1
                        tres = ("gain" if i == 0 else ("xt", 0)) if blk == 0 else ("xt", 1)
                        reg = S[:, (e2 * 2 + blk) * 128:(e2 * 2 + blk + 1) * 128]
                        A("pe", lambda e, reg=reg, e2=e2, kvh=kvh, i=i, blk=blk, hp=hp: e.matmul(
                            out=reg, lhsT=KT_A[64 * e2:64 * e2 + 64, kvh, 2 * i + blk, :],
                            rhs=QT_A[64 * e2:64 * e2 + 64, hp, i * 128:(i + 1) * 128], start=True, stop=False),
                          reads=[("KT_A", 2 * i + blk)] + QA_res(hp, i), writes=["ps%d" % sb_])
                        A("pe", lambda e, reg=reg, Tt=Tt, h=h: e.matmul(out=reg, lhsT=ident[:], rhs=Tt[:, h, 0, :],
                                                                       start=False, stop=False),
                          reads=["ident", tres], writes=["ps%d" % sb_])
                        A("pe", lambda e, reg=reg, Tt=Tt, h=h: e.matmul(out=reg, lhsT=ident[:], rhs=Tt[:, h, 1, :],
                                                                       start=False, stop=True),
                          reads=["ident", tres], writes=["ps%d" % sb_])
                A("act", lambda e, S=S, sb_=sb_: e.activation(out=PTa[sb_], in_=S[:, 0:512], func=AF.Exp),
                  reads=["ps%d" % sb_], writes=[("PTa", sb_)])
                for e2 in range(2):
                    h = 2 * hp + e2
                    kvh = h // 3
                    for blk in range(2):
                        A("pe", lambda e, OD=OD, e2=e2, kvh=kvh, i=i, blk=blk, sb_=sb_: e.matmul(
                            out=OD[64 * e2:64 * e2 + 64, 0:128], lhsT=V_A[:, 2 * i + blk, kvh * 64:(kvh + 1) * 64],
                            rhs=PTa[sb_][:, (e2 * 2 + blk) * 128:(e2 * 2 + blk + 1) * 128], start=(blk == 0), stop=(blk == 1)),
                          reads=[("V_A", 2 * i + blk), ("PTa", sb_)], writes=["ps%d" % (2 + sb_)])
                    for blk in range(2):
                        A("pe", lambda e, OD=OD, e2=e2, blk=blk, sb_=sb_: e.matmul(
                            out=OD[64 * e2:64 * e2 + 64, 128:256], lhsT=onesb[:, 0:64],
                            rhs=PTa[sb_][:, (e2 * 2 + blk) * 128:(e2 * 2 + blk + 1) * 128], start=(blk == 0), stop=(blk == 1)),
                          reads=["onesb", ("PTa", sb_)], writes=["ps%d" % (2 + sb_)])
                rd = tmpf[:, sb_ * 256:sb_ * 256 + 128]
                tm = tmpf[:, sb_ * 256 + 128:sb_ * 256 + 256]
                A("dve", lambda e, OD=OD, rd=rd, hp=hp: e.tensor_scalar(out=rd, in0=OD[:, 128:256], scalar1=sinkexp[:, hp:hp + 1],
                                                                       scalar2=None, op0=ALU.add),
                  reads=["ps%d" % (2 + sb_), "sinkexp"], writes=[("rd", sb_)])
                A("dve", lambda e, rd=rd: e.reciprocal(out=rd, in_=rd), reads=[("rd", sb_)], writes=[("rd", sb_)])
                A("dve", lambda e, OD=OD, rd=rd, tm=tm: e.tensor_tensor(out=tm, in0=OD[:, 0:128], in1=rd, op=ALU.mult),
                  reads=["ps%d" % (2 + sb_), ("rd", sb_)], writes=[("tm", sb_)])
                A("dve", lambda e, tm=tm, hp=hp, i=i: e.tensor_tensor(out=yazT[:, hp, i * 128:(i + 1) * 128], in0=tm,
                                                                     in1=szA[:, hp, i * 128:(i + 1) * 128], op=ALU.mult),
                  reads=[("tm", sb_), ("SZ", hp, i // 4)], writes=[("yazT", hp, i)])
        dump("QT_A", QT_A, [("QT", 0, i) for i in range(NS)] + [("QT", 4, i) for i in range(NS)], BF16)
        dump("yazT", yazT, [("yazT", hp, i) for hp in range(6) for i in range(NS)], BF16)
        yaz_res = [("yazT", hp, i) for hp in range(6) for i in range(NS)]
        if stop_after == "C2":
            A("sp", lambda e: e.dma_start(out=y_d[0:128, 0:512], in_=tmpf), reads=[("tm", 0), ("tm", 1)], dma="y")
            P.emit()
            return nc, dbg_out
        P.barrier()

        QT_C = v3(51200, 4, 1024)
        szC = v3(55296, 4, 1024)
        yczT = v3(59392, 4, 1024)
        k = load_ws(OFF["qC"], 512)
        q_proj(k, 0, 4, 128, gC, "gC", QT_C, 0, 2)
        k = load_ws(OFF["zC"], 512)
        z_proj(k, 0, 4, szC, 0)
        pcnt = 0
        for h in range(4):
            for half in range(2):
                ob = 2 + pcnt % 2
                db = 4 + pcnt % 2
                tq = tmpf
                for blk in range(2):
                    sb_ = pcnt % 2 if blk == 0 else (pcnt + 1) % 2
                    sb_ = blk
                    A("pe", lambda e, h=h, half=half, blk=blk, sb_=sb_: e.matmul(
                        out=bank(sb_)[:, 0:512], lhsT=KT_C[:, h, blk * 128:(blk + 1) * 128],
                        rhs=QT_C[:, h, half * 512:(half + 1) * 512], start=True, stop=True),
                      reads=[("KT_C", blk)] + [("QT", 0, i) for i in range(4 * half, 4 * half + 4)], writes=["ps%d" % sb_])
                    A("act", lambda e, sb_=sb_: e.activation(out=PTa[sb_], in_=bank(sb_)[:, 0:512], func=AF.Exp),
                      reads=["ps%d" % sb_], writes=[("PTa", sb_)])
                    A("pe", lambda e, h=h, blk=blk, sb_=sb_, ob=ob: e.matmul(
                        out=bank(ob)[:, 0:512], lhsT=V_C[:, blk, h * 128:(h + 1) * 128], rhs=PTa[sb_],
                        start=(blk == 0), stop=(blk == 1)),
                      reads=[("V_C", blk), ("PTa", sb_)], writes=["ps%d" % ob])
                    A("pe", lambda e, blk=blk, sb_=sb_, db=db: e.matmul(
                        out=bank(db)[:, 0:512], lhsT=onesb[:, :], rhs=PTa[sb_], start=(blk == 0), stop=(blk == 1)),
                      reads=["onesb", ("PTa", sb_)], writes=["ps%d" % db])
                pcnt += 1
                A("dve", lambda e, db=db: e.reciprocal(out=tmpf, in_=bank(db)[:, 0:512]), reads=["ps%d" % db], writes=["tmpf"])
                A("dve", lambda e, ob=ob: e.tensor_tensor(out=tmpf, in0=bank(ob)[:, 0:512], in1=tmpf, op=ALU.mult),
                  reads=["ps%d" % ob, "tmpf"], writes=["tmpf"])
                A("dve", lambda e, h=h, half=half: e.tensor_tensor(out=yczT[:, h, half * 512:(half + 1) * 512], in0=tmpf,
                                                                  in1=szC[:, h, half * 512:(half + 1) * 512], op=ALU.mult),
                  reads=["tmpf", ("SZ", h, half)], writes=[("yczT", h, half)])
        dump("yczT", yczT, [("yczT", h, half) for h in range(4) for half in range(2)], BF16)
        ycz_res = [("yczT", h, half) for h in range(4) for half in range(2)]
        if stop_after == "C3":
            A("sp", lambda e: e.dma_start(out=y_d[0:128, :], in_=xt[0][:]), reads=[("xt", 0)], dma="y")
            P.emit()
            return nc, dbg_out
        P.barrier()

        szB = v3(51200, 6, 1024)
        k = load_ws(OFF["zB"], 512)
        z_proj(k, 0, 4, szB, 0)
        k = load_ws(OFF["zB"] + 512, 256)
        z_proj(k, 0, 2, szB, 4)
        for hp in range(6):
            A("dve", lambda e, hp=hp: e.tensor_tensor(out=ybT[:, hp, :], in0=ybT[:, hp, :], in1=szB[:, hp, :], op=ALU.mult),
              reads=[("SZ", hp, 0), ("SZ", hp, 1)] + [("ybT", hp, sl_, e_) for sl_ in (0, 4) for e_ in (0, 1)], writes=[("ybzT", hp)])
        dump("ybzT", ybT, [("ybzT", hp) for hp in range(6)], BF16)
        ybz_res = [("ybzT", hp) for hp in range(6)]
        if stop_after == "C4":
            A("sp", lambda e: e.dma_start(out=y_d[0:128, :], in_=xt[0][:]), reads=[("xt", 0)], dma="y")
            P.emit()
            return nc, dbg_out
        P.barrier()

        yT_lo = v3(30720, 8, 1024)
        yT_hi = v3(51200, 8, 1024)

        def yT(cc):
            return yT_lo[:, cc, :] if cc < 8 else yT_hi[:, cc - 8, :]
        wb_v = [wba.rearrange("(kc p) n -> p kc n", p=128), wbb.rearrange("(kc p) n -> p kc n", p=128),
                wbc.rearrange("(kc p) n -> p kc n", p=128)]
        yz = [yazT, ybT, yczT]
        yz_res = [yaz_res, ybz_res, ycz_res]
        nkc = [6, 6, 4]
        koff = [0, 6, 12]
        sg = [ksq[:, 0:512], qtmp[:, 0:512], tmpf]
        acc = [hn[:, 0:1024].bitcast(F32), None]
        for cc in range(16):
            k = wsc[0] % 2
            wsc[0] += 1
            for b3 in range(3):
                A("pool", lambda e, k=k, b3=b3, cc=cc: e.dma_start(
                    out=ws[k][:, :, b3 * 128:(b3 + 1) * 128],
                    in_=w_in_v[:, :, OFF["G"] + b3 * D + cc * 128:OFF["G"] + b3 * D + (cc + 1) * 128]),
                  writes=[("wsg", k, b3)], dma=("wsg", k, b3))
                A("pool", lambda e, k=k, b3=b3, cc=cc: e.dma_start(
                    out=ws[k][:, koff[b3]:koff[b3] + nkc[b3], 384:512], in_=wb_v[b3][:, :, cc * 128:(cc + 1) * 128]),
                  writes=[("wsb", k, b3)], dma=("wsb", k, b3))
            wres = [("wsg", k, 0), ("wsg", k, 1), ("wsg", k, 2)]
            for half in range(2):
                gb = []
                ub = []
                for b3 in range(3):
                    pb = next_bank()
                    gb.append(pb)
                    for kc in range(KC):
                        A("pe", lambda e, kc=kc, b3=b3, half=half, pb=pb, k=k: e.matmul(
                            out=bank(pb)[:, 0:512], lhsT=ws[k][:, kc, b3 * 128:(b3 + 1) * 128],
                            rhs=hT_own[:, kc, half * 512:(half + 1) * 512], start=(kc == 0), stop=(kc == KC - 1)),
                          reads=[wres[b3]] + [("hT_own", i, "lo" if kc < 8 else "hi") for i in range(4 * half, 4 * half + 4)],
                          writes=["ps%d" % pb])
                    A("act", lambda e, b3=b3, pb=pb: e.activation(out=sg[b3], in_=bank(pb)[:, 0:512], func=AF.Sigmoid),
                      reads=["ps%d" % pb], writes=[("sg", b3)])
                for b3 in range(3):
                    pb = next_bank()
                    ub.append(pb)
                    for kk in range(nkc[b3]):
                        A("pe", lambda e, kk=kk, b3=b3, half=half, pb=pb, k=k: e.matmul(
                            out=bank(pb)[:, 0:512], lhsT=ws[k][:, koff[b3] + kk, 384:512],
                            rhs=yz[b3][:, kk, half * 512:(half + 1) * 512], start=(kk == 0), stop=(kk == nkc[b3] - 1)),
                          reads=[("wsb", k, b3)] + yz_res[b3], writes=["ps%d" % pb])
                accA = hn[:, 0:1024].bitcast(F32)
                A("dve", lambda e, ub=ub, accA=accA: e.tensor_tensor(out=accA, in0=sg[0], in1=bank(ub[0])[:, 0:512], op=ALU.mult),
                  reads=[("sg", 0), "ps%d" % ub[0]], writes=["accA"])
                A("dve", lambda e, ub=ub: e.tensor_tensor(out=sg[1], in0=sg[1], in1=bank(ub[1])[:, 0:512], op=ALU.mult),
                  reads=[("sg", 1), "ps%d" % ub[1]], writes=[("sg", 1)])
                A("dve", lambda e, accA=accA: e.tensor_tensor(out=accA, in0=accA, in1=sg[1], op=ALU.add),
                  reads=["accA", ("sg", 1)], writes=["accA"])
                A("dve", lambda e, ub=ub: e.tensor_tensor(out=sg[2], in0=sg[2], in1=bank(ub[2])[:, 0:512], op=ALU.mult),
                  reads=[("sg", 2), "ps%d" % ub[2]], writes=[("sg", 2)])
                A("dve", lambda e, accA=accA, cc=cc, half=half: e.tensor_tensor(
                    out=yT(cc)[:, half * 512:(half + 1) * 512], in0=accA, in1=sg[2], op=ALU.add),
                  reads=["accA", ("sg", 2)], writes=[("yT", cc, half)])
        dump("yT_lo", yT_lo, [("yT", cc, half) for cc in range(8) for half in range(2)], BF16)
        dump("yT_hi", yT_hi, [("yT", cc, half) for cc in range(8, 16) for half in range(2)], BF16)
        if stop_after == "2d":
            A("sp", lambda e: e.dma_start(out=y_d[0:128, :], in_=xt[0][:]), reads=[("xt", 0)], dma="y")
            P.emit()
            return nc, dbg_out

        wo_v = wout.rearrange("(kc p) n -> p kc n", p=128)
        xr = [xt[0][:, 0:512], xt[0][:, 512:1024], xt[0][:, 1024:1536], xt[0][:, 1536:2048],
              xt[1][:, 0:512], xt[1][:, 512:1024], xt[1][:, 1024:1536], xt[1][:, 1536:2048]]
        rc = 0
        for cg in range(4):
            k = wsc[0] % 2
            wsc[0] += 1
            for q4 in range(4):
                A("pool", lambda e, q4=q4, k=k, cg=cg: e.dma_start(out=ws[k][:, 4 * q4:4 * q4 + 4, :],
                                                                   in_=wo_v[:, 4 * q4:4 * q4 + 4, cg * 512:(cg + 1) * 512]),
                  writes=[(("wso", k), q4)] + [("wsg", k, b_) for b_ in range(3)] + [("wsb", k, b_) for b_ in range(3)],
                  dma=(("wso", k), q4))
            for i in range(NS):
                r8 = rc % 8
                rc += 1
                vb = 4 * i + 3
                A("sp", lambda e, r8=r8, vb=vb, cg=cg: e.dma_start(out=xr[r8], in_=xv[vb * 128:(vb + 1) * 128, cg * 512:(cg + 1) * 512]),
                  writes=[("xr", r8)], dma=("xr", r8))
                pb = next_bank()
                for kc in range(KC):
                    A("pe", lambda e, kc=kc, i=i, pb=pb, k=k: e.matmul(
                        out=bank(pb)[:, 0:512], lhsT=yT(kc)[:, i * 128:(i + 1) * 128], rhs=ws[k][:, kc, :],
                        start=(kc == 0), stop=(kc == KC - 1)),
                      reads=[(("wso", k), kc // 4), ("yT", kc, i // 4)], writes=["ps%d" % pb])
                A("dve", lambda e, r8=r8, pb=pb: e.tensor_tensor(out=xr[r8], in0=xr[r8], in1=bank(pb)[:, 0:512], op=ALU.add),
                  reads=[("xr", r8), "ps%d" % pb], writes=[("xr", r8)])
                A("sp", lambda e, r8=r8, i=i, cg=cg: e.dma_start(out=y_d[i * 128:(i + 1) * 128, cg * 512:(cg + 1) * 512], in_=xr[r8]),
                  reads=[("xr", r8)], dma=("yo", r8))
        P.emit()
    return nc, dbg_out


def _bf16_round(a):
    u = np.ascontiguousarray(a, dtype=np.float32).view(np.uint32).astype(np.uint64)
    r = ((u + 0x7FFF + ((u >> 16) & 1)) & 0xFFFF0000).astype(np.uint32)
    return r.view(np.float32)


def _const_tables():
    s = np.arange(128)[:, None].astype(np.float64)
    t = np.arange(128)[None, :].astype(np.float64)
    cmask = np.where(s <= t, 0.0, NEGM).astype(np.float32)
    slopes = np.exp2(-8.0 * np.arange(1, 13) / 12.0)
    tabs = []
    for blk in range(2):
        rel = (t + 128 - s) if blk == 0 else (t - s)
        valid = (rel < 128) if blk == 0 else (rel >= 0)
        tab = np.zeros((128, 12, 2, 128), np.float32)
        for h in range(12):
            b = np.where(valid, -slopes[h] * rel, NEGM).astype(np.float32)
            hi = _bf16_round(b)
            lo = _bf16_round((b.astype(np.float64) - hi.astype(np.float64)).astype(np.float32))
            tab[:, h, 0, :] = hi
            tab[:, h, 1, :] = lo
        tabs.append(tab.reshape(128, 12 * 2 * 128))
    masked = np.zeros((128, 12, 2, 128), np.float32)
    masked[:, :, 0, :] = NEGM
    return cmask, tabs[0], tabs[1], masked.reshape(128, 12 * 2 * 128)


def make_in_maps(inp):
    f = lambda a: np.ascontiguousarray(np.asarray(a, dtype=np.float32))
    x = f(inp["x"]); mem = f(inp["mem"])
    cmask, ab0, ab1, abm = _const_tables()
    shared = dict(
        norm_gain=f(inp["norm_gain"]).reshape(1, D), mem_norm_gain=f(inp["mem_norm_gain"]).reshape(1, D),
        w_in=f(inp["w_in"]).reshape(D, INW), b_forget=f(inp["b_forget"]).reshape(1, 12),
        q_gain_a=f(inp["q_gain_a"]).reshape(1, 64), k_gain_a=f(inp["k_gain_a"]).reshape(1, 64),
        q_gain_b=f(inp["q_gain_b"]).reshape(1, 64), k_gain_b=f(inp["k_gain_b"]).reshape(1, 64),
        q_gain_c=f(inp["q_gain_c"]).reshape(1, 128), k_gain_c=f(inp["k_gain_c"]).reshape(1, 128),
        sinks2=np.ascontiguousarray(f(inp["sinks_a"]).reshape(6, 2).T),
        w_mem_kv=f(inp["w_mem_kv"]).reshape(D, 1024),
        w_branch_a=f(inp["w_branch_a"]).reshape(768, D), w_branch_b=f(inp["w_branch_b"]).reshape(768, D),
        w_branch_c=f(inp["w_branch_c"]).reshape(512, D), w_out=f(inp["w_out"]).reshape(D, D),
        cmask=cmask, abias_b0=ab0, abias_b1=ab1,
    )
    maps = []
    for core in range(8):
        b, c = core // 4, core % 4
        npad = 3 - c
        xv = np.zeros((NB * 128, D), np.float32)
        xv[npad * 128:] = x[b, :(NB - npad) * 128]
        padm = np.zeros((128, NB), np.float32)
        padm[:, :npad] = -NEGM
        m = dict(shared)
        m.update(xv=xv, padm=padm, mem=np.ascontiguousarray(mem[b]),
                 abias_b0s0=(abm if c == 0 else ab0))
        maps.append(m)
    return maps


_NC_CACHE = {}


def kernel(**inputs):
    if "nc" not in _NC_CACHE:
        _NC_CACHE["nc"] = build_nc()[0]
    nc = _NC_CACHE["nc"]
    maps = make_in_maps(inputs)
    res = run_bass_kernel_spmd(nc, maps, core_ids=list(range(8)))
    out = np.zeros((2, 4096, D), np.float32)
    for core in range(8):
        b, c = core // 4, core % 4
        y = res.results[core]["y"].reshape(NS, 128, D)
        for i in range(NS):
            blk = 4 * i + c
            out[b, blk * 128:(blk + 1) * 128] = y[i]
    return out
```

```python
import numpy as np
from contextlib import ExitStack
import concourse.bass as bass
import concourse.mybir as mybir
from concourse.bass_utils import run_bass_kernel_spmd

F32 = mybir.dt.float32
BF16 = mybir.dt.bfloat16
ALU = mybir.AluOpType
AF = mybir.ActivationFunctionType
AX = mybir.AxisListType

D = 2048
KC = 16
NB = 32
NS = 8
EPS = 1e-6
INW = 12300
OFF = dict(qA=0, kA=768, vA=1024, zA=1280, qB=2048, kB=2816, vB=3584, zB=4352, fB=5120,
           qC=5132, zC=5644, G=6156)
NEGM = -30000.0


class _FakeIns:
    def then_inc(self, *a, **k):
        return self


class _CostProbe:
    def __init__(self):
        self.cost = 0.3
        self.dma_time = 0.0

    @staticmethod
    def _n(ap):
        n = 1
        for d in ap.shape[1:]:
            n *= int(d)
        return n

    def matmul(self, out=None, lhsT=None, rhs=None, **k):
        f32 = (rhs.dtype == F32)
        self.cost = max(self._n(rhs), 64) / 2400.0 * (4.0 if f32 else 1.0) + 0.035
        return _FakeIns()

    def transpose(self, out=None, in_=None, identity=None, **k):
        self.cost = 0.09
        return _FakeIns()

    def activation(self, out=None, in_=None, **k):
        self.cost = 0.25 + self._n(in_) / 1200.0 + (0.1 if k.get("accum_out") is not None else 0.0)
        return _FakeIns()

    def dma_start(self, out=None, in_=None, **k):
        esz = 2 if out.dtype == BF16 else 4
        nbytes = int(out.shape[0]) * self._n(out) * max(esz, 2 if in_.dtype == BF16 else 4)
        self.cost = 0.08
        self.dma_time = 2.0 + nbytes / 180e3
        return _FakeIns()

    def __getattr__(self, name):
        def f(*a, **k):
            out = k.get("out", a[0] if a else None)
            n = self._n(out) if out is not None and hasattr(out, "shape") else 64
            self.cost = 0.16 + n / 960.0
            return _FakeIns()
        return f


class Prog:
    ENGS = ("pe", "act", "dve", "pool", "sp")

    def __init__(self, nc):
        self.nc = nc
        self.ops = []
        self.phase = 0

    def op(self, eng, fn, reads=(), writes=(), dma=None):
        if dma is not None:
            km = self.__dict__.setdefault("_keymap", {})
            kc_ = self.__dict__.setdefault("_keycnt", {})
            kk = (self.phase, eng, dma)
            if kk not in km:
                n_ = kc_.get((self.phase, eng), 0)
                kc_[(self.phase, eng)] = n_ + 1
                km[kk] = (eng, n_)
            dma = km[kk]
        self.ops.append(dict(eng=eng, fn=fn, reads=list(reads), writes=list(writes), dma=dma,
                             sync=set(), order=set(), sig=None, need_sig=False, phase=self.phase))

    def barrier(self):
        self.phase += 1

    @staticmethod
    def _is_psum(r):
        name = r
        while isinstance(name, tuple):
            name = name[0]
        return name.startswith("ps")

    def resolve(self):
        import heapq
        ops = self.ops
        n = len(ops)
        last_w = {}
        readers = {}
        dma_count = {}
        for i, o in enumerate(ops):
            pr = _CostProbe()
            o["fn"](pr)
            o["cost"] = pr.cost
            o["dma_time"] = pr.dma_time
            if o["dma"] is not None:
                dma_count[o["dma"]] = dma_count.get(o["dma"], 0) + 1
                o["dma_val"] = 16 * dma_count[o["dma"]]
            deps = {}
            for r in o["reads"]:
                w = last_w.get(r)
                if w is not None:
                    deps[w] = "raw"
                if self._is_psum(r):
                    for rd in readers.get(r, ()):
                        if ops[rd]["eng"] != o["eng"]:
                            deps.setdefault(rd, "psrr")
            for r in o["writes"]:
                w = last_w.get(r)
                if w is not None:
                    deps.setdefault(w, "waw")
                for rd in readers.get(r, ()):
                    deps.setdefault(rd, "war")
            for r in o["reads"]:
                readers.setdefault(r, []).append(i)
            for r in o["writes"]:
                last_w[r] = i
                readers[r] = []
            deps.pop(i, None)
            for d, kind in deps.items():
                y = ops[d]
                if y["phase"] != o["phase"]:
                    continue
                if y["dma"] is not None or o["dma"] is not None or y["eng"] != o["eng"]:
                    o["sync"].add(d)
                elif o["eng"] != "pe":
                    o["sync"].add(d)
                else:
                    o["order"].add(d)
        succ = [[] for _ in range(n)]
        indeg = [0] * n
        for i, o in enumerate(ops):
            for d in o["sync"] | o["order"]:
                succ[d].append(i)
                indeg[i] += 1
        fin = [0.0] * n
        start = [0.0] * n
        eng_free = {e: 0.0 for e in self.ENGS}
        order = {e: [] for e in self.ENGS}
        LAT = 0.35
        nph = self.phase + 1
        byphase = [[] for _ in range(nph)]
        for i, o in enumerate(ops):
            byphase[o["phase"]].append(i)
        tphase = 0.0
        rcause = {}
        self.fin = fin
        self.start = start
        blev = [0.0] * n
        for i in range(n - 1, -1, -1):
            o = ops[i]
            m = 0.0
            for sidx in succ[i]:
                if blev[sidx] > m:
                    m = blev[sidx]
            blev[i] = m + o["cost"] + (o["dma_time"] if o["dma"] is not None else 0.0)
        PRIO = getattr(self, "prio_mode", 1)
        for ph in range(nph):
            ready = {e: [] for e in self.ENGS}
            rtime = {}
            for i in byphase[ph]:
                if indeg[i] == 0:
                    rtime[i] = tphase
                    ready[ops[i]["eng"]].append(i)
            left = len(byphase[ph])
            while left:
                best = None
                for e in self.ENGS:
                    lst = ready[e]
                    if not lst:
                        continue
                    ef = eng_free[e]
                    cand = None
                    for i in lst:
                        rt = rtime[i]
                        st_ = rt if rt > ef else ef
                        if PRIO:
                            key = (st_, -blev[i], i) if st_ > ef else (ef, -blev[i], i)
                        else:
                            key = (st_, i, i)
                        if cand is None or key < cand[0]:
                            cand = (key, st_, i)
                    if best is None or cand[0] < best[0]:
                        best = (cand[0], cand[1], cand[2], e)
                _, st_, i, e = best
                ready[e].remove(i)
                o = ops[i]
                start[i] = st_
                o['crit'] = ('eng', order[e][-1]) if (order[e] and eng_free[e] >= rtime[i]) else ('dep', rcause.get(i))
                eng_free[e] = st_ + o["cost"]
                fin[i] = st_ + o["cost"] + (o["dma_time"] if o["dma"] is not None else 0.0)
                order[e].append(i)
                left -= 1
                for sidx in succ[i]:
                    indeg[sidx] -= 1
                    same = (ops[sidx]["eng"] == e and o["dma"] is None and ops[sidx]["dma"] is None)
                    t_ = fin[i] + (0.0 if same else LAT)
                    if t_ > rtime.get(sidx, tphase):
                        rcause[sidx] = i
                    rtime[sidx] = max(rtime.get(sidx, tphase), t_)
                    if indeg[sidx] == 0:
                        ready[ops[sidx]["eng"]].append(sidx)
            t_prev = tphase
            tphase = max([tphase] + [fin[i] for i in byphase[ph]]) + 2.0
            self.phase_span = getattr(self, 'phase_span', []) + [tphase - t_prev]
            for e in self.ENGS:
                eng_free[e] = max(eng_free[e], tphase)
        self.sim_time = tphase
        self.order = order
        posn = {}
        for e in self.ENGS:
            for k_, idx in enumerate(order[e]):
                posn[idx] = k_
        for o in ops:
            best = {}
            keep = set()
            for d in o["sync"]:
                y = ops[d]
                if y["dma"] is not None:
                    keep.add(d)
                else:
                    b_ = best.get(y["eng"])
                    if b_ is None or posn[d] > posn[b_]:
                        best[y["eng"]] = d
            keep.update(best.values())
            o["sync"] = keep
            for d in keep:
                if ops[d]["dma"] is None:
                    ops[d]["need_sig"] = True
        self.bar_wait = []
        cnt = {e: 0 for e in self.ENGS}
        pos = {e: 0 for e in self.ENGS}
        dma_hi = {}
        for ph in range(nph):
            self.bar_wait.append(dict([(("eng", e), cnt[e]) for e in self.ENGS if cnt[e] > 0]
                                      + [(("dma", k), v) for k, v in dma_hi.items()]))
            for e in self.ENGS:
                lst = order[e]
                lastc = None
                p0 = pos[e]
                while pos[e] < len(lst) and ops[lst[pos[e]]]["phase"] == ph:
                    pos[e] += 1
                for idx in lst[p0:pos[e]]:
                    if ops[idx]["dma"] is None:
                        lastc = idx
                if lastc is not None and ph < nph - 1:
                    ops[lastc]["need_sig"] = True
                for idx in lst[p0:pos[e]]:
                    oo = ops[idx]
                    if oo["dma"] is None:
                        if oo["need_sig"]:
                            cnt[e] += 1
                            oo["sig"] = cnt[e]
                    else:
                        dma_hi[oo["dma"]] = max(dma_hi.get(oo["dma"], 0), oo["dma_val"])
        self.dma_keys = sorted(dma_count.keys(), key=str)
        self.dma_final = dict(dma_hi)

    def emit(self):
        nc = self.nc
        ops = self.ops
        self.resolve()
        with ExitStack() as st:
            sems = {}
            for e in self.ENGS:
                sems[("eng", e)] = st.enter_context(nc.semaphore("s_" + e))
            for n_, k in enumerate(self.dma_keys):
                sems[("dma", k)] = st.enter_context(nc.semaphore("d%d" % n_))
            block = st.enter_context(nc.Block())

            def run(engname, eng):
                waited = {}

                def wait(key, val):
                    if val <= 0 or waited.get(key, 0) >= val:
                        return
                    eng.wait_ge(sems[key], val)
                    waited[key] = val
                cur_phase = 0
                for idx in self.order[engname]:
                    o = ops[idx]
                    if o["phase"] != cur_phase:
                        cur_phase = o["phase"]
                        for key, val in self.bar_wait[cur_phase].items():
                            if key == ("eng", engname):
                                continue
                            wait(key, val)
                    for d in sorted(o["sync"]):
                        y = ops[d]
                        if y["dma"] is not None:
                            wait(("dma", y["dma"]), y["dma_val"])
                        else:
                            wait(("eng", y["eng"]), y["sig"])
                    ins = o["fn"](eng)
                    if o["dma"] is not None:
                        ins.then_inc(sems[("dma", o["dma"])], 16)
                    elif o["need_sig"]:
                        ins.then_inc(sems[("eng", engname)], 1)
                if engname == "sp":
                    for k, v in self.dma_final.items():
                        wait(("dma", k), v)

            @block.tensor
            def _(e):
                run("pe", e)

            @block.scalar
            def _(e):
                run("act", e)

            @block.vector
            def _(e):
                run("dve", e)

            @block.gpsimd
            def _(e):
                run("pool", e)

            @block.sync
            def _(e):
                run("sp", e)


def build_nc(debug=None, stop_after=None):
    debug = debug or []
    nc = bass.Bass("TRN2", target_bir_lowering=False)
    dr = lambda name, shape, dt=F32: nc.dram_tensor(name, shape, dt, kind="ExternalInput").ap()
    xv = dr("xv", [NB * 128, D])
    padm_d = dr("padm", [128, NB])
    mem_d = dr("mem", [256, D])
    ng_d = dr("norm_gain", [1, D])
    mg_d = dr("mem_norm_gain", [1, D])
    w_in = dr("w_in", [D, INW])
    bf_d = dr("b_forget", [1, 12])
    gqa_d = dr("q_gain_a", [1, 64]); gka_d = dr("k_gain_a", [1, 64])
    gqb_d = dr("q_gain_b", [1, 64]); gkb_d = dr("k_gain_b", [1, 64])
    gqc_d = dr("q_gain_c", [1, 128]); gkc_d = dr("k_gain_c", [1, 128])
    sinks_d = dr("sinks2", [2, 6])
    wmkv = dr("w_mem_kv", [D, 1024])
    wba = dr("w_branch_a", [768, D]); wbb = dr("w_branch_b", [768, D]); wbc = dr("w_branch_c", [512, D])
    wout = dr("w_out", [D, D])
    cmask_d = dr("cmask", [128, 128])
    ab0_d = dr("abias_b0", [128, 12 * 128 * 2])
    ab1_d = dr("abias_b1", [128, 12 * 128 * 2])
    ab00_d = dr("abias_b0s0", [128, 12 * 128 * 2])
    y_d = nc.dram_tensor("y", [NS * 128, D], F32, kind="ExternalOutput").ap()
    dbg_out = {}

    P = Prog(nc)
    st = ExitStack()
    sb = lambda name, shape, dt: st.enter_context(nc.sbuf_tensor(name, shape, dt))
    with st:
        BIGN = 72000
        big = sb("big", [128, BIGN], BF16)
        gain_bc = sb("gain_bc", [128, D], F32)
        xt = [sb("xt0", [128, D], F32), sb("xt1", [128, D], F32)]
        hn = sb("hn", [128, D], BF16)
        hnb = sb("hnb", [128, D], BF16)
        sqjunk = sb("sqjunk", [128, D], BF16)
        hTt = [sb("hTt0", [128, KC, 128], BF16), sb("hTt1", [128, KC, 128], BF16)]
        ksq = sb("ksq", [128, 768], F32)
        kn = sb("kn", [128, 768], BF16)
        qtmp = sb("qtmp", [128, 768], F32)
        qn = sb("qn", [128, 768], BF16)
        ident = sb("ident", [128, 128], BF16)
        identf = sb("identf", [128, 128], F32)
        U = sb("U", [128, 128], F32)
        onesf = sb("onesf", [128, 128], F32)
        onesb = sb("onesb", [128, 128], BF16)
        cmf = sb("cmf", [128, 128], F32)
        cmb = sb("cmb", [128, 128], BF16)
        bf_bc = sb("bf_bc", [128, 12], F32)
        gA = sb("gA", [128, 64], F32); gA2 = sb("gA2", [128, 64], F32)
        gB = sb("gB", [128, 64], F32); gB2 = sb("gB2", [128, 64], F32)
        gC = sb("gC", [128, 128], F32); gC2 = sb("gC2", [128, 128], F32)
        sinkexp = sb("sinkexp", [128, 6], F32)
        padm = sb("padm_s", [128, NB], F32)
        ss_all = sb("ss_all", [128, 64], F32)
        ln_all = sb("ln_all", [128, 64], F32)
        rstd_all = sb("rstd_all", [128, 64], F32)
        kss = sb("kss", [128, 8, 12], F32)
        kln = sb("kln", [128, 8, 12], F32)
        krs = sb("krs", [128, 8, 12], F32)
        fl = sb("fl", [128, 4, 12], F32)
        lneg = sb("lneg", [128, 4, 12], F32)
        cumprev = sb("cumprev", [128, 12], F32)
        n_all = sb("n_all", [128, NB, 12], F32)
        Ntab = sb("Ntab", [128, NS, 12], F32)
        btab = sb("btab", [128, NB, 12], F32)
        Dfull = sb("Dfull", [128, NS, 12], F32)
        Dpair = sb("Dpair", [128, NS, 6], F32)
        psall = st.enter_context(nc.psum_tensor("psall", [128, 4096], F32))

        def bank(b, n=1):
            return psall[:, b * 512:(b + n) * 512]

        def bankb(b, n=1):
            return psall[:, b * 512:(b + n) * 512].bitcast(BF16)

        def bigv(off, n):
            return big[:, off:off + n]

        def v3(off, a, b):
            return big[:, off:off + a * b].rearrange("p (a b) -> p a b", b=b)

        def dump(name, ap, res, dt):
            if name not in debug:
                return
            shape = list(ap.shape)
            t = nc.dram_tensor("dbg_" + name, shape, dt, kind="ExternalOutput").ap()
            dbg_out[name] = t
            P.op("sp", lambda e: e.dma_start(out=t, in_=ap), reads=res, dma="dbg_" + name)

        A = P.op
        A("sp", lambda e: e.dma_start(out=gain_bc[:], in_=ng_d.partition_broadcast(128)), writes=["gain"], dma="gain")
        A("sp", lambda e: e.dma_start(out=bf_bc[:], in_=bf_d.partition_broadcast(128)), writes=["bf_bc"], dma="c0")
        A("sp", lambda e: e.dma_start(out=gA[:], in_=gqa_d.partition_broadcast(128)), writes=["gA"], dma="c1")
        A("sp", lambda e: e.dma_start(out=gA2[:], in_=gka_d.partition_broadcast(128)), writes=["gA2"], dma="c2")
        A("sp", lambda e: e.dma_start(out=gB[:], in_=gqb_d.partition_broadcast(128)), writes=["gB"], dma="c3")
        A("sp", lambda e: e.dma_start(out=gB2[:], in_=gkb_d.partition_broadcast(128)), writes=["gB2"], dma="c4")
        A("sp", lambda e: e.dma_start(out=gC[:], in_=gqc_d.partition_broadcast(128)), writes=["gC"], dma="c5")
        A("sp", lambda e: e.dma_start(out=gC2[:], in_=gkc_d.partition_broadcast(128)), writes=["gC2"], dma="c6")
        A("sp", lambda e: e.dma_start(out=sinkexp[0:64, :], in_=sinks_d[0:1, :].partition_broadcast(64)),
          writes=["sk0"], dma="c7")
        A("sp", lambda e: e.dma_start(out=sinkexp[64:128, :], in_=sinks_d[1:2, :].partition_broadcast(64)),
          writes=["sk1"], dma="c8")
        A("sp", lambda e: e.dma_start(out=padm[:], in_=padm_d), writes=["padm"], dma="c9")
        A("sp", lambda e: e.dma_start(out=cmf[:], in_=cmask_d), writes=["cmf"], dma="c10")
        A("dve", lambda e: e.memset(identf[:], 1.0), writes=["identf"])
        A("pool", lambda e: e.affine_select(out=identf[:], in_=identf[:], pattern=[[-1, 128]],
                                            compare_op=ALU.is_equal, fill=0.0, base=0, channel_multiplier=1),
          reads=["identf"], writes=["identf"])
        A("dve", lambda e: e.tensor_copy(out=ident[:], in_=identf[:]), reads=["identf"], writes=["ident"])
        A("dve", lambda e: e.memset(U[:], 1.0), writes=["U"])
        A("pool", lambda e: e.affine_select(out=U[:], in_=U[:], pattern=[[1, 128]],
                                            compare_op=ALU.is_ge, fill=0.0, base=0, channel_multiplier=-1),
          reads=["U"], writes=["U"])
        A("dve", lambda e: e.memset(onesf[:], 1.0), writes=["onesf"])
        A("dve", lambda e: e.memset(onesb[:], 1.0), writes=["onesb"])
        A("dve", lambda e: e.memset(cumprev[:], 0.0), writes=["cumprev"])
        A("dve", lambda e: e.tensor_copy(out=cmb[:], in_=cmf[:]), reads=["cmf"], writes=["cmb"])
        A("dve", lambda e: e.scalar_tensor_tensor(out=gA[:], in0=gA[:], scalar=0.125, in1=gA2[:],
                                                  op0=ALU.mult, op1=ALU.mult), reads=["gA", "gA2"], writes=["gA"])
        A("dve", lambda e: e.scalar_tensor_tensor(out=gB[:], in0=gB[:], scalar=0.125, in1=gB2[:],
                                                  op0=ALU.mult, op1=ALU.mult), reads=["gB", "gB2"], writes=["gB"])
        A("dve", lambda e: e.scalar_tensor_tensor(out=gC[:], in0=gC[:], scalar=float(128 ** -0.5), in1=gC2[:],
                                                  op0=ALU.mult, op1=ALU.mult), reads=["gC", "gC2"], writes=["gC"])
        A("act", lambda e: e.activation(out=sinkexp[:], in_=sinkexp[:], func=AF.Exp),
          reads=["sk0", "sk1"], writes=["sinkexp"])

        w_in_v = w_in.rearrange("(kc p) n -> p kc n", p=128)


        def load_w4(dst, src, resname, key):
            for q4 in range(4):
                A("pool", lambda e, q4=q4: e.dma_start(out=dst[:, 4 * q4:4 * q4 + 4, :], in_=src[:, 4 * q4:4 * q4 + 4, :]),
                  writes=[(resname, q4)], dma=(key, q4))

        def norm_block(tag, idx, src_ap, dst_lo, dst_hi, dst_res, junk_ap, junk_res):
            s = idx % 2
            hx = hn if idx % 2 == 0 else hnb
            hres_ = ("hn", idx % 2)
            A("sp", lambda e: e.dma_start(out=xt[s][:], in_=src_ap), writes=[("xt", s)], dma=("xt", s))
            A("act", lambda e: e.activation(out=junk_ap, in_=xt[s][:], func=AF.Square, accum_out=ss_all[:, idx:idx + 1]),
              reads=[("xt", s)], writes=junk_res + [("ss", idx)])
            A("act", lambda e: e.activation(out=ln_all[:, idx:idx + 1], in_=ss_all[:, idx:idx + 1], func=AF.Ln,
                                            scale=1.0 / D, bias=EPS), reads=[("ss", idx)], writes=[("ln", idx)])
            A("act", lambda e: e.activation(out=rstd_all[:, idx:idx + 1], in_=ln_all[:, idx:idx + 1], func=AF.Exp,
                                            scale=-0.5), reads=[("ln", idx)], writes=[("rstd", idx)])
            A("dve", lambda e: e.scalar_tensor_tensor(out=hx[:], in0=xt[s][:], scalar=rstd_all[:, idx:idx + 1],
                                                      in1=gain_bc[:], op0=ALU.mult, op1=ALU.mult),
              reads=[("xt", s), ("rstd", idx), "gain"], writes=[hres_])
            pT = bankb(0, 2)
            for kc in range(KC):
                A("pe", lambda e, kc=kc: e.transpose(out=pT[:, kc * 128:(kc + 1) * 128],
                                                     in_=hx[:, kc * 128:(kc + 1) * 128], identity=ident[:]),
                  reads=[hres_, "ident"], writes=["ps0" if kc < 8 else "ps1"])
            A("act", lambda e: e.activation(out=dst_lo, in_=pT[:, 0:1024].rearrange("p (a b) -> p a b", b=128),
                                            func=AF.Copy), reads=["ps0"], writes=[dst_res + ("lo",)])
            A("dve", lambda e: e.tensor_copy(out=dst_hi, in_=pT[:, 1024:2048].rearrange("p (a b) -> p a b", b=128)),
              reads=["ps1"], writes=[dst_res + ("hi",)])

        def proj_tm(lhs_fn, lhs_res, w_view, w_res, c0, n, out_ap, out_res):
            for kc in range(KC):
                A("pe", lambda e, kc=kc: e.matmul(out=out_ap, lhsT=lhs_fn(kc), rhs=w_view[:, kc, c0:c0 + n],
                                                  start=(kc == 0), stop=(kc == KC - 1)),
                  reads=[lhs_res + ("lo",) if kc < 8 else lhs_res + ("hi",), (w_res, kc // 4)], writes=out_res)

        def head_rstd(tag, idx, ps_ap, ps_res, nh, dh):
            idx = idx % 8
            A("act", lambda e: e.activation(out=ksq[:, 0:nh * dh], in_=ps_ap, func=AF.Square),
              reads=ps_res, writes=["ksq"])
            A("dve", lambda e: e.tensor_reduce(out=kss[:, idx, 0:nh],
                                               in_=ksq[:, 0:nh * dh].rearrange("p (h d) -> p h d", d=dh),
                                               axis=AX.X, op=ALU.add), reads=["ksq"], writes=[("kss", idx)])
            A("act", lambda e: e.activation(out=kln[:, idx, 0:nh], in_=kss[:, idx, 0:nh], func=AF.Ln,
                                            scale=1.0 / dh, bias=EPS), reads=[("kss", idx)], writes=[("kln", idx)])
            A("act", lambda e: e.activation(out=krs[:, idx, 0:nh], in_=kln[:, idx, 0:nh], func=AF.Exp, scale=-0.5),
              reads=[("kln", idx)], writes=[("krs", idx)])

        W1a = v3(0, KC, 1548)
        QT_B = v3(24768, 6, 1024)
        KT_B = v3(30912, 6, 4096)
        load_w4(W1a[:, :, 0:768], w_in_v[:, :, OFF["kB"]:OFF["kB"] + 768], "W1a_k", "W1a_k")
        load_w4(W1a[:, :, 768:780], w_in_v[:, :, OFF["fB"]:OFF["fB"] + 12], "W1a_f", "W1a_f")
        load_w4(W1a[:, :, 780:1548], w_in_v[:, :, OFF["qB"]:OFF["qB"] + 768], "W1a_q", "W1a_q")
        pKT = bankb(6)
        for j in range(NB):
            s = j % 2
            hres = ("hTt", s)
            norm_block("c1", j, xv[j * 128:(j + 1) * 128, :], hTt[s][:, 0:8, :], hTt[s][:, 8:16, :], hres,
                       sqjunk[:], ["sqjunk"])
            lhs = lambda kc, s=s: hTt[s][:, kc, :]
            kb0 = 2 + 2 * (j % 2)
            kr0, kr1 = "ps%d" % kb0, "ps%d" % (kb0 + 1)
            proj_tm(lhs, hres, W1a, "W1a_k", 0, 512, bank(kb0)[:, 0:512], [kr0])
            for kc in range(KC):
                A("pe", lambda e, kc=kc, s=s, kb0=kb0: e.matmul(out=bank(kb0 + 1)[:, 0:268], lhsT=hTt[s][:, kc, :],
                                                       rhs=W1a[:, kc, 512:780], start=(kc == 0), stop=(kc == KC - 1)),
                  reads=[hres + ("lo",) if kc < 8 else hres + ("hi",), ("W1a_k", kc // 4), ("W1a_f", kc // 4)], writes=[kr1])
            pK = psall[:, kb0 * 512:kb0 * 512 + 768]
            head_rstd("k", j, pK, [kr0, kr1], 12, 64)
            A("dve", lambda e, j=j, pK=pK: e.tensor_tensor(out=kn[:].rearrange("p (h d) -> p h d", d=64),
                                                    in0=pK.rearrange("p (h d) -> p h d", d=64),
                                                    in1=krs[:, j % 8, 0:12].unsqueeze(2).to_broadcast([128, 12, 64]),
                                                    op=ALU.mult),
              reads=[kr0, kr1, ("krs", j % 8)], writes=["kn"])
            for hp in range(6):
                A("pe", lambda e, hp=hp: e.transpose(out=pKT[:, hp * 128:(hp + 1) * 128],
                                                     in_=kn[:, hp * 128:(hp + 1) * 128], identity=ident[:]),
                  reads=["kn", "ident"], writes=["ps6"])
            A("act", lambda e, j=j: e.activation(out=KT_B[:, :, j * 128:(j + 1) * 128],
                                                 in_=pKT[:, 0:768].rearrange("p (a b) -> p a b", b=128), func=AF.Copy),
              reads=["ps6"], writes=[("KT_B", j)])
            A("dve", lambda e, j=j, kb0=kb0: e.tensor_tensor(out=fl[:, j % 4, :], in0=bank(kb0 + 1)[:, 256:268], in1=bf_bc[:], op=ALU.add),
              reads=[kr1, "bf_bc"], writes=[("fl", j % 4)])
            A("act", lambda e, j=j: e.activation(out=fl[:, j % 4, :], in_=fl[:, j % 4, :], func=AF.Exp, scale=-1.0),
              reads=[("fl", j % 4)], writes=[("fl", j % 4)])
            A("act", lambda e, j=j: e.activation(out=lneg[:, j % 4, :], in_=fl[:, j % 4, :], func=AF.Ln, bias=1.0),
              reads=[("fl", j % 4)], writes=[("lneg", j % 4)])
            A("pe", lambda e, j=j: e.matmul(out=bank(7)[:, 0:12], lhsT=U[:], rhs=lneg[:, j % 4, :], start=True, stop=False),
              reads=["U", ("lneg", j % 4)], writes=["ps7"])
            A("pe", lambda e: e.matmul(out=bank(7)[:, 0:12], lhsT=onesf[:], rhs=cumprev[:], start=False, stop=True),
              reads=["onesf", "cumprev"], writes=["ps7"])
            A("dve", lambda e, j=j: e.tensor_copy(out=n_all[:, j, :], in_=bank(7)[:, 0:12]),
              reads=["ps7"], writes=[("n_all", j)])
            A("dve", lambda e, j=j: e.tensor_tensor(out=cumprev[:], in0=cumprev[:], in1=lneg[:, j % 4, :], op=ALU.add),
              reads=["cumprev", ("lneg", j % 4)], writes=["cumprev"])
            if j % 4 == 3:
                g = j // 4
                A("pe", lambda e: e.matmul(out=bank(7)[:, 16:28], lhsT=onesf[:], rhs=cumprev[:], start=True, stop=True),
                  reads=["onesf", "cumprev"], writes=["ps7"])
                A("dve", lambda e, g=g: e.tensor_copy(out=Ntab[:, g, :], in_=bank(7)[:, 16:28]),
                  reads=["ps7"], writes=[("Ntab", g)])
                pQ = psall[:, 2 * 512:2 * 512 + 768]
                proj_tm(lhs, hres, W1a, "W1a_q", 780, 512, bank(2)[:, 0:512], ["ps2"])
                proj_tm(lhs, hres, W1a, "W1a_q", 1292, 256, bank(3)[:, 0:256], ["ps3"])
                qi = (32 + g) % 8
                A("act", lambda e, pQ=pQ: e.activation(out=qtmp[:], in_=pQ, func=AF.Square), reads=["ps2", "ps3"], writes=["qtmp"])
                A("dve", lambda e, qi=qi: e.tensor_reduce(out=kss[:, qi, 0:12], in_=qtmp[:].rearrange("p (h d) -> p h d", d=64),
                                                         axis=AX.X, op=ALU.add), reads=["qtmp"], writes=[("kss", qi)])
                A("act", lambda e, qi=qi: e.activation(out=kln[:, qi, 0:12], in_=kss[:, qi, 0:12], func=AF.Ln, scale=1.0 / 64, bias=EPS),
                  reads=[("kss", qi)], writes=[("kln", qi)])
                A("act", lambda e, qi=qi: e.activation(out=krs[:, qi, 0:12], in_=kln[:, qi, 0:12], func=AF.Exp, scale=-0.5),
                  reads=[("kln", qi)], writes=[("krs", qi)])
                A("dve", lambda e, qi=qi, pQ=pQ: e.tensor_tensor(out=qtmp[:].rearrange("p (h d) -> p h d", d=64),
                                                               in0=pQ.rearrange("p (h d) -> p h d", d=64),
                                                               in1=krs[:, qi, 0:12].unsqueeze(2).to_broadcast([128, 12, 64]),
                                                               op=ALU.mult),
                  reads=["ps2", "ps3", ("krs", qi)], writes=["qtmp"])
                A("dve", lambda e: e.tensor_tensor(out=qn[:].rearrange("p (h d) -> p h d", d=64),
                                                   in0=qtmp[:].rearrange("p (h d) -> p h d", d=64),
                                                   in1=gB[:].unsqueeze(1).to_broadcast([128, 12, 64]), op=ALU.mult),
                  reads=["qtmp", "gB"], writes=["qn"])
                pQT = bank(7)[:, 128:512].bitcast(BF16)
                for hp in range(6):
                    A("pe", lambda e, hp=hp, pQT=pQT: e.transpose(out=pQT[:, hp * 128:(hp + 1) * 128],
                                                                 in_=qn[:, hp * 128:(hp + 1) * 128], identity=ident[:]),
                      reads=["qn", "ident"], writes=["ps7"])
                A("dve", lambda e, g=g, pQT=pQT: e.tensor_copy(out=QT_B[:, :, g * 128:(g + 1) * 128],
                                                              in_=pQT.rearrange("p (a b) -> p a b", b=128)),
                  reads=["ps7"], writes=[("QT_B", g)])
        dump("KT_B", KT_B, [("KT_B", j) for j in range(NB)], BF16)
        dump("QT_B", QT_B, [("QT_B", g) for g in range(NS)], BF16)
        dump("n_all", n_all[:], [("n_all", j) for j in range(NB)], F32)
        dump("Ntab", Ntab[:], [("Ntab", g) for g in range(NS)], F32)
        if stop_after == "1a":
            A("sp", lambda e: e.dma_start(out=y_d[0:128, :], in_=xt[0][:]), reads=[("xt", 0)], dma="y")
            P.emit()
            return nc, dbg_out
        P.barrier()

        W1b = v3(0, KC, 768)
        V_Blo = v3(12288, 16, 768)
        V_Bhi = v3(55488, 16, 768)

        def VB(j):
            return V_Blo[:, j, :] if j < 16 else V_Bhi[:, j - 16, :]
        W1A_NAMES = [("W1a_k", q_) for q_ in range(4)] + [("W1a_f", q_) for q_ in range(4)] + [("W1a_q", q_) for q_ in range(4)]
        load_w4(W1b, w_in_v[:, :, OFF["vB"]:OFF["vB"] + 768], "W1b", "W1b")
        for j in range(NB):
            s = j % 2
            hres = ("hTt", s)
            norm_block("c2", j, xv[j * 128:(j + 1) * 128, :], hTt[s][:, 0:8, :], hTt[s][:, 8:16, :], hres,
                       sqjunk[:], ["sqjunk"])
            lhs = lambda kc, s=s: hTt[s][:, kc, :]
            b0 = 2 + 2 * (j % 2)
            proj_tm(lhs, hres, W1b, "W1b", 0, 512, bank(b0)[:, 0:512], ["ps%d" % b0])
            proj_tm(lhs, hres, W1b, "W1b", 512, 256, bank(b0 + 1)[:, 0:256], ["ps%d" % (b0 + 1)])
            pV = psall[:, b0 * 512:b0 * 512 + 768]
            eng = "act" if j % 2 == 0 else "dve"
            if eng == "act":
                A("act", lambda e, j=j, pV=pV: e.activation(out=VB(j), in_=pV, func=AF.Copy),
                  reads=["ps%d" % b0, "ps%d" % (b0 + 1)], writes=[("V_B", j)])
            else:
                A("dve", lambda e, j=j, pV=pV: e.tensor_copy(out=VB(j), in_=pV),
                  reads=["ps%d" % b0, "ps%d" % (b0 + 1)], writes=[("V_B", j)])
        dump("V_Blo", V_Blo, [("V_B", j) for j in range(16)], BF16)
        dump("V_Bhi", V_Bhi, [("V_B", j) for j in range(16, 32)], BF16)
        if stop_after == "1b":
            A("sp", lambda e: e.dma_start(out=y_d[0:128, :], in_=xt[0][:]), reads=[("xt", 0)], dma="y")
            P.emit()
            return nc, dbg_out
        P.barrier()

        ybT = v3(0, 6, 1024)
        PT = [bigv(6144, 512), bigv(6656, 512), bigv(11264, 512), bigv(11776, 512)]
        Anum = bigv(7168, 1024).bitcast(F32)
        Aden = bigv(8192, 1024).bitcast(F32)
        rden = bigv(9216, 2048).bitcast(F32)
        for g in range(NS):
            A("dve", lambda e, g=g: e.tensor_tensor(out=btab[:, 4 * g:4 * g + 4, :], in0=n_all[:, 4 * g:4 * g + 4, :],
                                                    in1=Ntab[:, g, :].unsqueeze(1).to_broadcast([128, 4, 12]),
                                                    op=ALU.subtract),
              reads=[("n_all", 4 * g + r) for r in range(4)] + [("Ntab", g)], writes=[("btab0", g)])
        A("dve", lambda e: e.tensor_tensor(out=btab[:], in0=btab[:],
                                           in1=padm[:].unsqueeze(2).to_broadcast([128, NB, 12]), op=ALU.subtract),
          reads=[("btab0", g) for g in range(NS)] + ["padm"], writes=["btab"])
        A("dve", lambda e: e.tensor_tensor(out=Dfull[:, 1:8, :], in0=Ntab[:, 0:7, :], in1=Ntab[:, 1:8, :], op=ALU.subtract),
          reads=[("Ntab", g) for g in range(NS)], writes=["Dfull"])
        A("act", lambda e: e.activation(out=Dfull[:, 1:8, :], in_=Dfull[:, 1:8, :], func=AF.Exp),
          reads=["Dfull"], writes=["Dfull"])
        for e2 in range(2):
            A("dve", lambda e, e2=e2: e.tensor_copy(out=Dpair[64 * e2:64 * e2 + 64, 1:8, :],
                                                    in_=Dfull[64 * e2:64 * e2 + 64, 1:8, e2::2]),
              reads=["Dfull"], writes=[("Dpair", e2)])
        dump("btab", btab[:], ["btab"], F32)
        dump("Dpair", Dpair[:], [("Dpair", 0), ("Dpair", 1)], F32)
        PASSES = [(0, 3), (4, 7)]
        cnt = 0
        gcnt = 0
        for hp in range(6):
            for (slo, shi) in PASSES:
                qlo_pass = slo * 128
                for g in range(shi + 1):
                    q0 = max(g, slo) * 128
                    Nq = (shi + 1) * 128 - q0
                    ob = 4 + (gcnt % 2)
                    db = 6 + (gcnt % 2)
                    gcnt += 1
                    for r in range(4):
                        kb = 4 * g + r
                        for e2 in range(2):
                            h = 2 * hp + e2
                            sbk = cnt % 4
                            cnt += 1
                            S = bank(sbk)[:, 0:Nq]
                            diag = (r == 3 and g >= slo)
                            A("pe", lambda e, S=S, e2=e2, hp=hp, kb=kb, q0=q0, Nq=Nq, diag=diag: e.matmul(
                                out=S, lhsT=KT_B[64 * e2:64 * e2 + 64, hp, kb * 128:(kb + 1) * 128],
                                rhs=QT_B[64 * e2:64 * e2 + 64, hp, q0:q0 + Nq], start=True, stop=(not diag)),
                              reads=[("KT_B", kb)] + [("QT_B", i) for i in range(q0 // 128, shi + 1)],
                              writes=["ps%d" % sbk])
                            if diag:
                                A("pe", lambda e, S=S: e.matmul(out=S[:, 0:128], lhsT=ident[:], rhs=cmb[:],
                                                                start=False, stop=True),
                                  reads=["ident", "cmb"], writes=["ps%d" % sbk])
                            A("act", lambda e, S=S, sbk=sbk, Nq=Nq, kb=kb, h=h: e.activation(
                                out=PT[sbk][:, 0:Nq], in_=S, func=AF.Exp, bias=btab[:, kb, h:h + 1], scale=1.0),
                              reads=["ps%d" % sbk, "btab"], writes=[("PT", sbk)])
                            A("pe", lambda e, e2=e2, ob=ob, Nq=Nq, kb=kb, h=h, sbk=sbk, r=r: e.matmul(
                                out=bank(ob)[64 * e2:64 * e2 + 64, 0:Nq], lhsT=VB(kb)[:, h * 64:(h + 1) * 64],
                                rhs=PT[sbk][:, 0:Nq], start=(r == 0), stop=(r == 3)),
                              reads=[("V_B", kb), ("PT", sbk)], writes=["ps%d" % ob])
                            A("pe", lambda e, e2=e2, db=db, Nq=Nq, sbk=sbk, r=r: e.matmul(
                                out=bank(db)[64 * e2:64 * e2 + 64, 0:Nq], lhsT=onesb[:, 0:64],
                                rhs=PT[sbk][:, 0:Nq], start=(r == 0), stop=(r == 3)),
                              reads=["onesb", ("PT", sbk)], writes=["ps%d" % db])
                    a0 = q0 - qlo_pass
                    if g == 0:
                        A("dve", lambda e, ob=ob, Nq=Nq, a0=a0: e.tensor_copy(out=Anum[:, a0:a0 + Nq], in_=bank(ob)[:, 0:Nq]),
                          reads=["ps%d" % ob], writes=["Anum"])
                        A("act", lambda e, db=db, Nq=Nq, a0=a0: e.activation(out=Aden[:, a0:a0 + Nq], in_=bank(db)[:, 0:Nq],
                                                                            func=AF.Copy),
                          reads=["ps%d" % db], writes=["Aden"])
                    else:
                        A("dve", lambda e, ob=ob, Nq=Nq, a0=a0, g=g, hp=hp: e.scalar_tensor_tensor(
                            out=Anum[:, a0:a0 + Nq], in0=Anum[:, a0:a0 + Nq], scalar=Dpair[:, g, hp:hp + 1],
                            in1=bank(ob)[:, 0:Nq], op0=ALU.mult, op1=ALU.add),
                          reads=["ps%d" % ob, "Anum", ("Dpair", 0), ("Dpair", 1)], writes=["Anum"])
                        A("dve", lambda e, db=db, Nq=Nq, a0=a0, g=g, hp=hp: e.scalar_tensor_tensor(
                            out=Aden[:, a0:a0 + Nq], in0=Aden[:, a0:a0 + Nq], scalar=Dpair[:, g, hp:hp + 1],
                            in1=bank(db)[:, 0:Nq], op0=ALU.mult, op1=ALU.add),
                          reads=["ps%d" % db, "Aden", ("Dpair", 0), ("Dpair", 1)], writes=["Aden"])
                A("dve", lambda e: e.reciprocal(out=rden[:, 0:512], in_=Aden[:, 0:512]), reads=["Aden"], writes=["rden"])
                A("dve", lambda e, hp=hp, qlo=qlo_pass: e.tensor_tensor(out=ybT[:, hp, qlo:qlo + 512], in0=Anum[:, 0:512],
                                                                        in1=rden[:, 0:512], op=ALU.mult),
                  reads=["Anum", "rden"], writes=[("ybT", hp, slo)])
        dump("ybT", ybT, [("ybT", hp, slo) for hp in range(6) for slo in (0, 4)], BF16)
        if stop_after == "2b":
            A("sp", lambda e: e.dma_start(out=y_d[0:128, :], in_=xt[0][:]), reads=[("xt", 0)], dma="y")
            P.emit()
            return nc, dbg_out
        P.barrier()

        hT_own = v3(6144, KC, 1024)
        W_kvA = v3(22528, KC, 512)
        KT_A = big[:, 30720:38912].rearrange("p (k b t) -> p k b t", k=4, b=16)
        V_A = v3(38912, 16, 256)
        pKT = bankb(6)
        load_w4(W_kvA, w_in_v[:, :, OFF["kA"]:OFF["kA"] + 512], "W_kvA", "W_kvA")
        junkf = sqjunk[:]
        junkr = ["sqjunk"]
        for i in range(NS):
            for own in (0, 1):
                vb = 4 * i + 2 + own
                blk = 2 * i + own
                if own:
                    hres = ("hT_own", i)
                    norm_block("o", blk, xv[vb * 128:(vb + 1) * 128, :], hT_own[:, 0:8, i * 128:(i + 1) * 128],
                               hT_own[:, 8:16, i * 128:(i + 1) * 128], hres, junkf, junkr)
                    lhs = lambda kc, i=i: hT_own[:, kc, i * 128:(i + 1) * 128]
                else:
                    hres = ("hTt", 0)
                    norm_block("o", blk, xv[vb * 128:(vb + 1) * 128, :], hTt[0][:, 0:8, :], hTt[0][:, 8:16, :],
                               hres, junkf, junkr)
                    lhs = lambda kc: hTt[0][:, kc, :]
                pb = 2 + (blk % 2)
                proj_tm(lhs, hres, W_kvA, "W_kvA", 0, 512, bank(pb)[:, 0:512], ["ps%d" % pb])
                head_rstd("ka", blk, bank(pb)[:, 0:256], ["ps%d" % pb], 4, 64)
                A("dve", lambda e, pb=pb, blk=blk: e.tensor_tensor(
                    out=kn[:, 0:512].rearrange("p (k u d) -> p k u d", k=4, u=2),
                    in0=bank(pb)[:, 0:256].rearrange("p (k d) -> p k d", d=64).unsqueeze(2).to_broadcast([128, 4, 2, 64]),
                    in1=krs[:, blk % 8, 0:4].unsqueeze(2).unsqueeze(3).to_broadcast([128, 4, 2, 64]), op=ALU.mult),
                  reads=["ps%d" % pb, ("krs", blk % 8)], writes=["kn"])
                for k4 in range(4):
                    A("pe", lambda e, k4=k4: e.transpose(out=pKT[:, k4 * 128:(k4 + 1) * 128],
                                                         in_=kn[:, k4 * 128:(k4 + 1) * 128], identity=ident[:]),
                      reads=["kn", "ident"], writes=["ps6"])
                A("act", lambda e, blk=blk: e.activation(out=KT_A[:, :, blk, :],
                                                         in_=pKT[:, 0:512].rearrange("p (a b) -> p a b", b=128), func=AF.Copy),
                  reads=["ps6"], writes=[("KT_A", blk)])
                A("act", lambda e, blk=blk, pb=pb: e.activation(out=V_A[:, blk, :], in_=bank(pb)[:, 256:512], func=AF.Copy),
                  reads=["ps%d" % pb], writes=[("V_A", blk)])
        dump("hT_own", hT_own, [("hT_own", i, x) for i in range(NS) for x in ("lo", "hi")], BF16)
        dump("KT_A", KT_A, [("KT_A", b_) for b_ in range(16)], BF16)
        dump("V_A", V_A, [("V_A", b_) for b_ in range(16)], BF16)
        hT_res = [("hT_own", i, x) for i in range(NS) for x in ("lo", "hi")]
        if stop_after == "2a":
            A("sp", lambda e: e.dma_start(out=y_d[0:128, :], in_=xt[0][:]), reads=[("xt", 0)], dma="y")
            P.emit()
            return nc, dbg_out

        wm = v3(43008, KC, 1024)
        mnT = v3(59392, KC, 256)
        KT_C = v3(63488, 4, 256)
        V_C = v3(64512, 2, 512)
        wm_v = wmkv.rearrange("(kc p) n -> p kc n", p=128)
        A("sp", lambda e: e.dma_start(out=gain_bc[:], in_=mg_d.partition_broadcast(128)), writes=["gain"], dma="gain")
        load_w4(wm[:, :, 0:512], wm_v[:, :, 0:512], "wm0", "wm0")
        load_w4(wm[:, :, 512:1024], wm_v[:, :, 512:1024], "wm1", "wm1")
        for mb in range(2):
            hres = ("mnT", mb)
            norm_block("m", 16 + mb, mem_d[mb * 128:(mb + 1) * 128, :], mnT[:, 0:8, mb * 128:(mb + 1) * 128],
                       mnT[:, 8:16, mb * 128:(mb + 1) * 128], hres, junkf, junkr)
            lhs = lambda kc, mb=mb: mnT[:, kc, mb * 128:(mb + 1) * 128]
            proj_tm(lhs, hres, wm, "wm0", 0, 512, bank(2)[:, 0:512], ["ps2"])
            proj_tm(lhs, hres, wm, "wm1", 512, 512, bank(3)[:, 0:512], ["ps3"])
            head_rstd("kc", 16 + mb, bank(2)[:, 0:512], ["ps2"], 4, 128)
            A("dve", lambda e, mb=mb: e.tensor_tensor(
                out=kn[:, 0:512].rearrange("p (k d) -> p k d", d=128),
                in0=bank(2)[:, 0:512].rearrange("p (k d) -> p k d", d=128),
                in1=krs[:, (16 + mb) % 8, 0:4].unsqueeze(2).to_broadcast([128, 4, 128]), op=ALU.mult),
              reads=["ps2", ("krs", (16 + mb) % 8)], writes=["kn"])
            for k4 in range(4):
                A("pe", lambda e, k4=k4: e.transpose(out=pKT[:, k4 * 128:(k4 + 1) * 128],
                                                     in_=kn[:, k4 * 128:(k4 + 1) * 128], identity=ident[:]),
                  reads=["kn", "ident"], writes=["ps6"])
            A("act", lambda e, mb=mb: e.activation(out=KT_C[:, :, mb * 128:(mb + 1) * 128],
                                                   in_=pKT[:, 0:512].rearrange("p (a b) -> p a b", b=128), func=AF.Copy),
              reads=["ps6"], writes=[("KT_C", mb)])
            A("dve", lambda e, mb=mb: e.tensor_copy(out=V_C[:, mb, :], in_=bank(3)[:, 0:512]),
              reads=["ps3"], writes=[("V_C", mb)])
        dump("KT_C", KT_C, [("KT_C", 0), ("KT_C", 1)], BF16)
        dump("V_C", V_C, [("V_C", 0), ("V_C", 1)], BF16)
        if stop_after == "C1":
            A("sp", lambda e: e.dma_start(out=y_d[0:128, :], in_=xt[0][:]), reads=[("xt", 0)], dma="y")
            P.emit()
            return nc, dbg_out
        P.barrier()

        ws = [v3(22528, KC, 512), v3(43008, KC, 512)]
        wsc = [0]

        def load_ws(col0, n, dst0=0):
            k = wsc[0] % 2
            wsc[0] += 1
            load_w4(ws[k][:, :, dst0:dst0 + n], w_in_v[:, :, col0:col0 + n], ("ws", k), ("ws", k))
            return k

        bctr = [0]

        def next_bank():
            b_ = bctr[0] % 8
            bctr[0] += 1
            return b_

        def q_proj(k, c0, nh, dh, gtile, gres, QT, hp0, tagi):
            n = nh * dh
            nt = n // 128
            for i in range(NS):
                pb = next_bank()
                idx = tagi * 8 + i
                proj_tm(lambda kc, i=i: hT_own[:, kc, i * 128:(i + 1) * 128], ("hT_own", i), ws[k], ("ws", k),
                        c0, n, bank(pb)[:, 0:n], ["ps%d" % pb])
                head_rstd("q", idx, bank(pb)[:, 0:n], ["ps%d" % pb], nh, dh)
                A("dve", lambda e, pb=pb, idx=idx: e.tensor_tensor(
                    out=qtmp[:, 0:n].rearrange("p (h d) -> p h d", d=dh),
                    in0=bank(pb)[:, 0:n].rearrange("p (h d) -> p h d", d=dh),
                    in1=krs[:, idx % 8, 0:nh].unsqueeze(2).to_broadcast([128, nh, dh]), op=ALU.mult),
                  reads=["ps%d" % pb, ("krs", idx % 8)], writes=["qtmp"])
                A("dve", lambda e: e.tensor_tensor(
                    out=kn[:, 0:n].rearrange("p (h d) -> p h d", d=dh),
                    in0=qtmp[:, 0:n].rearrange("p (h d) -> p h d", d=dh),
                    in1=gtile[:].unsqueeze(1).to_broadcast([128, nh, dh]), op=ALU.mult),
                  reads=["qtmp", gres], writes=["kn"])
                pt = next_bank()
                pTb = bankb(pt)
                for t in range(nt):
                    A("pe", lambda e, t=t, pTb=pTb: e.transpose(out=pTb[:, t * 128:(t + 1) * 128],
                                                               in_=kn[:, t * 128:(t + 1) * 128], identity=ident[:]),
                      reads=["kn", "ident"], writes=["ps%d" % pt])
                A("act", lambda e, i=i, pTb=pTb: e.activation(
                    out=QT[:, hp0:hp0 + nt, i * 128:(i + 1) * 128],
                    in_=pTb[:, 0:nt * 128].rearrange("p (a b) -> p a b", b=128), func=AF.Copy),
                  reads=["ps%d" % pt], writes=[("QT", hp0, i)])

        def z_proj(k, c0, nch, SZ, ch0):
            for c in range(nch):
                for half in range(2):
                    pb = next_bank()
                    for kc in range(KC):
                        A("pe", lambda e, kc=kc, c=c, half=half, pb=pb: e.matmul(
                            out=bank(pb)[:, 0:512], lhsT=ws[k][:, kc, c0 + c * 128:c0 + (c + 1) * 128],
                            rhs=hT_own[:, kc, half * 512:(half + 1) * 512], start=(kc == 0), stop=(kc == KC - 1)),
                          reads=[(("ws", k), kc // 4)] + [("hT_own", i, "lo" if kc < 8 else "hi") for i in range(4 * half, 4 * half + 4)],
                          writes=["ps%d" % pb])
                    A("act", lambda e, c=c, half=half, pb=pb: e.activation(
                        out=SZ[:, ch0 + c, half * 512:(half + 1) * 512], in_=bank(pb)[:, 0:512], func=AF.Silu),
                      reads=["ps%d" % pb], writes=[("SZ", ch0 + c, half)])

        QT_A = v3(51200, 6, 1024)
        szA = v3(57344, 6, 1024)
        yazT = v3(65536, 6, 1024)
        T0 = xt[0][:].bitcast(BF16)[:, 0:3072].rearrange("p (h l q) -> p h l q", h=12, l=2)
        T1 = xt[1][:].bitcast(BF16)[:, 0:3072].rearrange("p (h l q) -> p h l q", h=12, l=2)
        T00 = gain_bc[:].bitcast(BF16)[:, 0:3072].rearrange("p (h l q) -> p h l q", h=12, l=2)
        A("pool", lambda e: e.dma_start(out=xt[0][:].bitcast(BF16)[:, 0:3072], in_=ab0_d), writes=[("xt", 0)], dma=("xtp", 0))
        A("pool", lambda e: e.dma_start(out=xt[1][:].bitcast(BF16)[:, 0:3072], in_=ab1_d), writes=[("xt", 1)], dma=("xtp", 1))
        A("pool", lambda e: e.dma_start(out=gain_bc[:].bitcast(BF16)[:, 0:3072], in_=ab00_d), writes=["gain"], dma="gainp")
        k = load_ws(OFF["qA"], 512)
        q_proj(k, 0, 8, 64, gA, "gA", QT_A, 0, 0)
        k = load_ws(OFF["qA"] + 512, 256)
        q_proj(k, 0, 4, 64, gA, "gA", QT_A, 4, 1)
        k = load_ws(OFF["zA"], 512)
        z_proj(k, 0, 4, szA, 0)
        k = load_ws(OFF["zA"] + 512, 256)
        z_proj(k, 0, 2, szA, 4)
        PTa = [hn[:, 0:512], hn[:, 512:1024]]
        tmpf = hn[:, 1024:2048].bitcast(F32)
        QA_res = lambda hp, i: [("QT", 0 if hp < 4 else 4, i)]
        pcnt = 0
        for i in range(NS):
            for hp in range(6):
                sb_ = pcnt % 2
                pcnt += 1
                S = bank(sb_)
                OD = bank(2 + sb_)
                for e2 in range(2):
                    h = 2 * hp + e2
                    kvh = h // 3
                    for blk in range(2):
                        Tt = (T00 if i == 0 else T0) if blk == 0 else T1
                        tres = ("gain" if i == 0 else ("xt", 0)) if blk == 0 else ("xt", 1)
                        reg = S[:, (e2 * 2 + blk) * 128:(e2 * 2 + blk + 1) * 128]
                        A("pe", lambda e, reg=reg, e2=e2, kvh=kvh, i=i, blk=blk, hp=hp: e.matmul(
                            out=reg, lhsT=KT_A[64 * e2:64 * e2 + 64, kvh, 2 * i + blk, :],
                            rhs=QT_A[64 * e2:64 * e2 + 64, hp, i * 128:(i + 1) * 128], start=True, stop=False),
                          reads=[("KT_A", 2 * i + blk)] + QA_res(hp, i), writes=["ps%d" % sb_])
                        A("pe", lambda e, reg=reg, Tt=Tt, h=h: e.matmul(out=reg, lhsT=ident[:], rhs=Tt[:, h, 0, :],
                                                                       start=False, stop=False),
                          reads=["ident", tres], writes=["ps%d" % sb_])
                        A("pe", lambda e, reg=reg, Tt=Tt, h=h: e.matmul(out=reg, lhsT=ident[:], rhs=Tt[:, h, 1, :],
                                                                       start=False, stop=True),
                          reads=["ident", tres], writes=["ps%d" % sb_])
                A("act", lambda e, S=S, sb_=sb_: e.activation(out=PTa[sb_], in_=S[:, 0:512], func=AF.Exp),
                  reads=["ps%d" % sb_], writes=[("PTa", sb_)])
                for e2 in range(2):
                    h = 2 * hp + e2
                    kvh = h // 3
                    for blk in range(2):
                        A("pe", lambda e, OD=OD, e2=e2, kvh=kvh, i=i, blk=blk, sb_=sb_: e.matmul(
                            out=OD[64 * e2:64 * e2 + 64, 0:128], lhsT=V_A[:, 2 * i + blk, kvh * 64:(kvh + 1) * 64],
                            rhs=PTa[sb_][:, (e2 * 2 + blk) * 128:(e2 * 2 + blk + 1) * 128], start=(blk == 0), stop=(blk == 1)),
                          reads=[("V_A", 2 * i + blk), ("PTa", sb_)], writes=["ps%d" % (2 + sb_)])
                    for blk in range(2):
                        A("pe", lambda e, OD=OD, e2=e2, blk=blk, sb_=sb_: e.matmul(
                            out=OD[64 * e2:64 * e2 + 64, 128:256], lhsT=onesb[:, 0:64],
                            rhs=PTa[sb_][:, (e2 * 2 + blk) * 128:(e2 * 2 + blk + 1) * 128], start=(blk == 0), stop=(blk == 1)),
                          reads=["onesb", ("PTa", sb_)], writes=["ps%d" % (2 + sb_)])
                rd = tmpf[:, sb_ * 256:sb_ * 256 + 128]
                tm = tmpf[:, sb_ * 256 + 128:sb_ * 256 + 256]
                A("dve", lambda e, OD=OD, rd=rd, hp=hp: e.tensor_scalar(out=rd, in0=OD[:, 128:256], scalar1=sinkexp[:, hp:hp + 1],
                                                                       scalar2=None, op0=ALU.add),
                  reads=["ps%d" % (2 + sb_), "sinkexp"], writes=[("rd", sb_)])
                A("dve", lambda e, rd=rd: e.reciprocal(out=rd, in_=rd), reads=[("rd", sb_)], writes=[("rd", sb_)])
                A("dve", lambda e, OD=OD, rd=rd, tm=tm: e.tensor_tensor(out=tm, in0=OD[:, 0:128], in1=rd, op=ALU.mult),
                  reads=["ps%d" % (2 + sb_), ("rd", sb_)], writes=[("tm", sb_)])
                A("dve", lambda e, tm=tm, hp=hp, i=i: e.tensor_tensor(out=yazT[:, hp, i * 128:(i + 1) * 128], in0=tm,
                                                                     in1=szA[:, hp, i * 128:(i + 1) * 128], op=ALU.mult),
                  reads=[("tm", sb_), ("SZ", hp, i // 4)], writes=[("yazT", hp, i)])
        dump("QT_A", QT_A, [("QT", 0, i) for i in range(NS)] + [("QT", 4, i) for i in range(NS)], BF16)
        dump("yazT", yazT, [("yazT", hp, i) for hp in range(6) for i in range(NS)], BF16)
        yaz_res = [("yazT", hp, i) for hp in range(6) for i in range(NS)]
        if stop_after == "C2":
            A("sp", lambda e: e.dma_start(out=y_d[0:128, 0:512], in_=tmpf), reads=[("tm", 0), ("tm", 1)], dma="y")
            P.emit()
            return nc, dbg_out
        P.barrier()

        QT_C = v3(51200, 4, 1024)
        szC = v3(55296, 4, 1024)
        yczT = v3(59392, 4, 1024)
        k = load_ws(OFF["qC"], 512)
        q_proj(k, 0, 4, 128, gC, "gC", QT_C, 0, 2)
        k = load_ws(OFF["zC"], 512)
        z_proj(k, 0, 4, szC, 0)
        pcnt = 0
        for h in range(4):
            for half in range(2):
                ob = 2 + pcnt % 2
                db = 4 + pcnt % 2
                tq = tmpf
                for blk in range(2):
                    sb_ = pcnt % 2 if blk == 0 else (pcnt + 1) % 2
                    sb_ = blk
                    A("pe", lambda e, h=h, half=half, blk=blk, sb_=sb_: e.matmul(
                        out=bank(sb_)[:, 0:512], lhsT=KT_C[:, h, blk * 128:(blk + 1) * 128],
                        rhs=QT_C[:, h, half * 512:(half + 1) * 512], start=True, stop=True),
                      reads=[("KT_C", blk)] + [("QT", 0, i) for i in range(4 * half, 4 * half + 4)], writes=["ps%d" % sb_])
                    A("act", lambda e, sb_=sb_: e.activation(out=PTa[sb_], in_=bank(sb_)[:, 0:512], func=AF.Exp),
                      reads=["ps%d" % sb_], writes=[("PTa", sb_)])
                    A("pe", lambda e, h=h, blk=blk, sb_=sb_, ob=ob: e.matmul(
                        out=bank(ob)[:, 0:512], lhsT=V_C[:, blk, h * 128:(h + 1) * 128], rhs=PTa[sb_],
                        start=(blk == 0), stop=(blk == 1)),
                      reads=[("V_C", blk), ("PTa", sb_)], writes=["ps%d" % ob])
                    A("pe", lambda e, blk=blk, sb_=sb_, db=db: e.matmul(
                        out=bank(db)[:, 0:512], lhsT=onesb[:, :], rhs=PTa[sb_], start=(blk == 0), stop=(blk == 1)),
                      reads=["onesb", ("PTa", sb_)], writes=["ps%d" % db])
                pcnt += 1
                A("dve", lambda e, db=db: e.reciprocal(out=tmpf, in_=bank(db)[:, 0:512]), reads=["ps%d" % db], writes=["tmpf"])
                A("dve", lambda e, ob=ob: e.tensor_tensor(out=tmpf, in0=bank(ob)[:, 0:512], in1=tmpf, op=ALU.mult),
                  reads=["ps%d" % ob, "tmpf"], writes=["tmpf"])
                A("dve", lambda e, h=h, half=half: e.tensor_tensor(out=yczT[:, h, half * 512:(half + 1) * 512], in0=tmpf,
                                                                  in1=szC[:, h, half * 512:(half + 1) * 512], op=ALU.mult),
                  reads=["tmpf", ("SZ", h, half)], writes=[("yczT", h, half)])
        dump("yczT", yczT, [("yczT", h, half) for h in range(4) for half in range(2)], BF16)
        ycz_res = [("yczT", h, half) for h in range(4) for half in range(2)]
        if stop_after == "C3":
            A("sp", lambda e: e.dma_start(out=y_d[0:128, :], in_=xt[0][:]), reads=[("xt", 0)], dma="y")
            P.emit()
            return nc, dbg_out
        P.barrier()

        szB = v3(51200, 6, 1024)
        k = load_ws(OFF["zB"], 512)
        z_proj(k, 0, 4, szB, 0)
        k = load_ws(OFF["zB"] + 512, 256)
        z_proj(k, 0, 2, szB, 4)
        for hp in range(6):
            A("dve", lambda e, hp=hp: e.tensor_tensor(out=ybT[:, hp, :], in0=ybT[:, hp, :], in1=szB[:, hp, :], op=ALU.mult),
              reads=[("SZ", hp, 0), ("SZ", hp, 1), ("ybT", hp, 0), ("ybT", hp, 4)], writes=[("ybzT", hp)])
        dump("ybzT", ybT, [("ybzT", hp) for hp in range(6)], BF16)
        ybz_res = [("ybzT", hp) for hp in range(6)]
        if stop_after == "C4":
            A("sp", lambda e: e.dma_start(out=y_d[0:128, :], in_=xt[0][:]), reads=[("xt", 0)], dma="y")
            P.emit()
            return nc, dbg_out
        P.barrier()

        yT_lo = v3(30720, 8, 1024)
        yT_hi = v3(51200, 8, 1024)

        def yT(cc):
            return yT_lo[:, cc, :] if cc < 8 else yT_hi[:, cc - 8, :]
        wb_v = [wba.rearrange("(kc p) n -> p kc n", p=128), wbb.rearrange("(kc p) n -> p kc n", p=128),
                wbc.rearrange("(kc p) n -> p kc n", p=128)]
        yz = [yazT, ybT, yczT]
        yz_res = [yaz_res, ybz_res, ycz_res]
        nkc = [6, 6, 4]
        koff = [0, 6, 12]
        sg = [ksq[:, 0:512], qtmp[:, 0:512], tmpf]
        acc = [hn[:, 0:1024].bitcast(F32), None]
        for cc in range(16):
            k = wsc[0] % 2
            wsc[0] += 1
            for b3 in range(3):
                A("pool", lambda e, k=k, b3=b3, cc=cc: e.dma_start(
                    out=ws[k][:, :, b3 * 128:(b3 + 1) * 128],
                    in_=w_in_v[:, :, OFF["G"] + b3 * D + cc * 128:OFF["G"] + b3 * D + (cc + 1) * 128]),
                  writes=[("wsg", k, b3)], dma=("wsg", k, b3))
                A("pool", lambda e, k=k, b3=b3, cc=cc: e.dma_start(
                    out=ws[k][:, koff[b3]:koff[b3] + nkc[b3], 384:512], in_=wb_v[b3][:, :, cc * 128:(cc + 1) * 128]),
                  writes=[("wsb", k, b3)], dma=("wsb", k, b3))
            wres = [("wsg", k, 0), ("wsg", k, 1), ("wsg", k, 2)]
            for half in range(2):
                gb = []
                ub = []
                for b3 in range(3):
                    pb = next_bank()
                    gb.append(pb)
                    for kc in range(KC):
                        A("pe", lambda e, kc=kc, b3=b3, half=half, pb=pb, k=k: e.matmul(
                            out=bank(pb)[:, 0:512], lhsT=ws[k][:, kc, b3 * 128:(b3 + 1) * 128],
                            rhs=hT_own[:, kc, half * 512:(half + 1) * 512], start=(kc == 0), stop=(kc == KC - 1)),
                          reads=[wres[b3]] + [("hT_own", i, "lo" if kc < 8 else "hi") for i in range(4 * half, 4 * half + 4)],
                          writes=["ps%d" % pb])
                    A("act", lambda e, b3=b3, pb=pb: e.activation(out=sg[b3], in_=bank(pb)[:, 0:512], func=AF.Sigmoid),
                      reads=["ps%d" % pb], writes=[("sg", b3)])
                for b3 in range(3):
                    pb = next_bank()
                    ub.append(pb)
                    for kk in range(nkc[b3]):
                        A("pe", lambda e, kk=kk, b3=b3, half=half, pb=pb, k=k: e.matmul(
                            out=bank(pb)[:, 0:512], lhsT=ws[k][:, koff[b3] + kk, 384:512],
                            rhs=yz[b3][:, kk, half * 512:(half + 1) * 512], start=(kk == 0), stop=(kk == nkc[b3] - 1)),
                          reads=[("wsb", k, b3)] + yz_res[b3], writes=["ps%d" % pb])
                accA = hn[:, 0:1024].bitcast(F32)
                A("dve", lambda e, ub=ub, accA=accA: e.tensor_tensor(out=accA, in0=sg[0], in1=bank(ub[0])[:, 0:512], op=ALU.mult),
                  reads=[("sg", 0), "ps%d" % ub[0]], writes=["accA"])
                A("dve", lambda e, ub=ub: e.tensor_tensor(out=sg[1], in0=sg[1], in1=bank(ub[1])[:, 0:512], op=ALU.mult),
                  reads=[("sg", 1), "ps%d" % ub[1]], writes=[("sg", 1)])
                A("dve", lambda e, accA=accA: e.tensor_tensor(out=accA, in0=accA, in1=sg[1], op=ALU.add),
                  reads=["accA", ("sg", 1)], writes=["accA"])
                A("dve", lambda e, ub=ub: e.tensor_tensor(out=sg[2], in0=sg[2], in1=bank(ub[2])[:, 0:512], op=ALU.mult),
                  reads=[("sg", 2), "ps%d" % ub[2]], writes=[("sg", 2)])
                A("dve", lambda e, accA=accA, cc=cc, half=half: e.tensor_tensor(
                    out=yT(cc)[:, half * 512:(half + 1) * 512], in0=accA, in1=sg[2], op=ALU.add),
                  reads=["accA", ("sg", 2)], writes=[("yT", cc, half)])
        dump("yT_lo", yT_lo, [("yT", cc, half) for cc in range(8) for half in range(2)], BF16)
        dump("yT_hi", yT_hi, [("yT", cc, half) for cc in range(8, 16) for half in range(2)], BF16)
        if stop_after == "2d":
            A("sp", lambda e: e.dma_start(out=y_d[0:128, :], in_=xt[0][:]), reads=[("xt", 0)], dma="y")
            P.emit()
            return nc, dbg_out

        wo_v = wout.rearrange("(kc p) n -> p kc n", p=128)
        xr = [xt[0][:, 0:512], xt[0][:, 512:1024], xt[0][:, 1024:1536], xt[0][:, 1536:2048],
              xt[1][:, 0:512], xt[1][:, 512:1024], xt[1][:, 1024:1536], xt[1][:, 1536:2048]]
        rc = 0
        for cg in range(4):
            k = wsc[0] % 2
            wsc[0] += 1
            for q4 in range(4):
                A("pool", lambda e, q4=q4, k=k, cg=cg: e.dma_start(out=ws[k][:, 4 * q4:4 * q4 + 4, :],
                                                                   in_=wo_v[:, 4 * q4:4 * q4 + 4, cg * 512:(cg + 1) * 512]),
                  writes=[(("wso", k), q4)] + [("wsg", k, b_) for b_ in range(3)] + [("wsb", k, b_) for b_ in range(3)],
                  dma=(("wso", k), q4))
            for i in range(NS):
                r8 = rc % 8
                rc += 1
                vb = 4 * i + 3
                A("sp", lambda e, r8=r8, vb=vb, cg=cg: e.dma_start(out=xr[r8], in_=xv[vb * 128:(vb + 1) * 128, cg * 512:(cg + 1) * 512]),
                  writes=[("xr", r8)], dma=("xr", r8))
                pb = next_bank()
                for kc in range(KC):
                    A("pe", lambda e, kc=kc, i=i, pb=pb, k=k: e.matmul(
                        out=bank(pb)[:, 0:512], lhsT=yT(kc)[:, i * 128:(i + 1) * 128], rhs=ws[k][:, kc, :],
                        start=(kc == 0), stop=(kc == KC - 1)),
                      reads=[(("wso", k), kc // 4), ("yT", kc, i // 4)], writes=["ps%d" % pb])
                A("dve", lambda e, r8=r8, pb=pb: e.tensor_tensor(out=xr[r8], in0=xr[r8], in1=bank(pb)[:, 0:512], op=ALU.add),
                  reads=[("xr", r8), "ps%d" % pb], writes=[("xr", r8)])
                A("sp", lambda e, r8=r8, i=i, cg=cg: e.dma_start(out=y_d[i * 128:(i + 1) * 128, cg * 512:(cg + 1) * 512], in_=xr[r8]),
                  reads=[("xr", r8)], dma=("yo", r8))
        P.emit()
    return nc, dbg_out


def _bf16_round(a):
    u = np.ascontiguousarray(a, dtype=np.float32).view(np.uint32).astype(np.uint64)
    r = ((u + 0x7FFF + ((u >> 16) & 1)) & 0xFFFF0000).astype(np.uint32)
    return r.view(np.float32)


def _const_tables():
    s = np.arange(128)[:, None].astype(np.float64)
    t = np.arange(128)[None, :].astype(np.float64)
    cmask = np.where(s <= t, 0.0, NEGM).astype(np.float32)
    slopes = np.exp2(-8.0 * np.arange(1, 13) / 12.0)
    tabs = []
    for blk in range(2):
        rel = (t + 128 - s) if blk == 0 else (t - s)
        valid = (rel < 128) if blk == 0 else (rel >= 0)
        tab = np.zeros((128, 12, 2, 128), np.float32)
        for h in range(12):
            b = np.where(valid, -slopes[h] * rel, NEGM).astype(np.float32)
            hi = _bf16_round(b)
            lo = _bf16_round((b.astype(np.float64) - hi.astype(np.float64)).astype(np.float32))
            tab[:, h, 0, :] = hi
            tab[:, h, 1, :] = lo
        tabs.append(tab.reshape(128, 12 * 2 * 128))
    masked = np.zeros((128, 12, 2, 128), np.float32)
    masked[:, :, 0, :] = NEGM
    return cmask, tabs[0], tabs[1], masked.reshape(128, 12 * 2 * 128)


def make_in_maps(inp):
    f = lambda a: np.ascontiguousarray(np.asarray(a, dtype=np.float32))
    x = f(inp["x"]); mem = f(inp["mem"])
    cmask, ab0, ab1, abm = _const_tables()
    shared = dict(
        norm_gain=f(inp["norm_gain"]).reshape(1, D), mem_norm_gain=f(inp["mem_norm_gain"]).reshape(1, D),
        w_in=f(inp["w_in"]).reshape(D, INW), b_forget=f(inp["b_forget"]).reshape(1, 12),
        q_gain_a=f(inp["q_gain_a"]).reshape(1, 64), k_gain_a=f(inp["k_gain_a"]).reshape(1, 64),
        q_gain_b=f(inp["q_gain_b"]).reshape(1, 64), k_gain_b=f(inp["k_gain_b"]).reshape(1, 64),
        q_gain_c=f(inp["q_gain_c"]).reshape(1, 128), k_gain_c=f(inp["k_gain_c"]).reshape(1, 128),
        sinks2=np.ascontiguousarray(f(inp["sinks_a"]).reshape(6, 2).T),
        w_mem_kv=f(inp["w_mem_kv"]).reshape(D, 1024),
        w_branch_a=f(inp["w_branch_a"]).reshape(768, D), w_branch_b=f(inp["w_branch_b"]).reshape(768, D),
        w_branch_c=f(inp["w_branch_c"]).reshape(512, D), w_out=f(inp["w_out"]).reshape(D, D),
        cmask=cmask, abias_b0=ab0, abias_b1=ab1,
    )
    maps = []
    for core in range(8):
        b, c = core // 4, core % 4
        npad = 3 - c
        xv = np.zeros((NB * 128, D), np.float32)
        xv[npad * 128:] = x[b, :(NB - npad) * 128]
        padm = np.zeros((128, NB), np.float32)
        padm[:, :npad] = -NEGM
        m = dict(shared)
        m.update(xv=xv, padm=padm, mem=np.ascontiguousarray(mem[b]),
                 abias_b0s0=(abm if c == 0 else ab0))
        maps.append(m)
    return maps


_NC_CACHE = {}


def kernel(**inputs):
    if "nc" not in _NC_CACHE:
        _NC_CACHE["nc"] = build_nc()[0]
    nc = _NC_CACHE["nc"]
    maps = make_in_maps(inputs)
    res = run_bass_kernel_spmd(nc, maps, core_ids=list(range(8)))
    out = np.zeros((2, 4096, D), np.float32)
    for core in range(8):
        b, c = core // 4, core % 4
        y = res.results[core]["y"].reshape(NS, 128, D)
        for i in range(NS):
            blk = 4 * i + c
            out[b, blk * 128:(blk + 1) * 128] = y[i]
    return out
```

```python
import numpy as np
from contextlib import ExitStack
import concourse.bass as bass
import concourse.mybir as mybir
from concourse.bass_utils import run_bass_kernel_spmd

F32 = mybir.dt.float32
BF16 = mybir.dt.bfloat16
ALU = mybir.AluOpType
AF = mybir.ActivationFunctionType
AX = mybir.AxisListType

D = 2048
KC = 16
NB = 32
NS = 8
EPS = 1e-6
INW = 12300
OFF = dict(qA=0, kA=768, vA=1024, zA=1280, qB=2048, kB=2816, vB=3584, zB=4352, fB=5120,
           qC=5132, zC=5644, G=6156)
NEGM = -30000.0


class _FakeIns:
    def then_inc(self, *a, **k):
        return self


class _CostProbe:
    def __init__(self):
        self.cost = 0.3
        self.dma_time = 0.0

    @staticmethod
    def _n(ap):
        n = 1
        for d in ap.shape[1:]:
            n *= int(d)
        return n

    def matmul(self, out=None, lhsT=None, rhs=None, **k):
        f32 = (rhs.dtype == F32)
        self.cost = max(self._n(rhs), 64) / 2400.0 * (4.0 if f32 else 1.0) + 0.035
        return _FakeIns()

    def transpose(self, out=None, in_=None, identity=None, **k):
        self.cost = 0.09
        return _FakeIns()

    def activation(self, out=None, in_=None, **k):
        self.cost = 0.25 + self._n(in_) / 1200.0 + (0.1 if k.get("accum_out") is not None else 0.0)
        return _FakeIns()

    def dma_start(self, out=None, in_=None, **k):
        esz = 2 if out.dtype == BF16 else 4
        nbytes = int(out.shape[0]) * self._n(out) * max(esz, 2 if in_.dtype == BF16 else 4)
        self.cost = 0.08
        self.dma_time = 2.0 + nbytes / 180e3
        return _FakeIns()

    def __getattr__(self, name):
        def f(*a, **k):
            out = k.get("out", a[0] if a else None)
            n = self._n(out) if out is not None and hasattr(out, "shape") else 64
            self.cost = 0.16 + n / 960.0
            return _FakeIns()
        return f


class Prog:
    ENGS = ("pe", "act", "dve", "pool", "sp")

    def __init__(self, nc):
        self.nc = nc
        self.ops = []
        self.phase = 0

    def op(self, eng, fn, reads=(), writes=(), dma=None):
        if dma is not None:
            km = self.__dict__.setdefault("_keymap", {})
            kc_ = self.__dict__.setdefault("_keycnt", {})
            kk = (self.phase, eng, dma)
            if kk not in km:
                n_ = kc_.get((self.phase, eng), 0)
                kc_[(self.phase, eng)] = n_ + 1
                km[kk] = (eng, n_)
            dma = km[kk]
        self.ops.append(dict(eng=eng, fn=fn, reads=list(reads), writes=list(writes), dma=dma,
                             sync=set(), order=set(), sig=None, need_sig=False, phase=self.phase))

    def barrier(self):
        self.phase += 1

    @staticmethod
    def _is_psum(r):
        name = r
        while isinstance(name, tuple):
            name = name[0]
        return name.startswith("ps")

    def resolve(self):
        import heapq
        ops = self.ops
        n = len(ops)
        last_w = {}
        readers = {}
        dma_count = {}
        for i, o in enumerate(ops):
            pr = _CostProbe()
            o["fn"](pr)
            o["cost"] = pr.cost
            o["dma_time"] = pr.dma_time
            if o["dma"] is not None:
                dma_count[o["dma"]] = dma_count.get(o["dma"], 0) + 1
                o["dma_val"] = 16 * dma_count[o["dma"]]
            deps = {}
            for r in o["reads"]:
                w = last_w.get(r)
                if w is not None:
                    deps[w] = "raw"
                if self._is_psum(r):
                    for rd in readers.get(r, ()):
                        if ops[rd]["eng"] != o["eng"]:
                            deps.setdefault(rd, "psrr")
            for r in o["writes"]:
                w = last_w.get(r)
                if w is not None:
                    deps.setdefault(w, "waw")
                for rd in readers.get(r, ()):
                    deps.setdefault(rd, "war")
            for r in o["reads"]:
                readers.setdefault(r, []).append(i)
            for r in o["writes"]:
                last_w[r] = i
                readers[r] = []
            deps.pop(i, None)
            for d, kind in deps.items():
                y = ops[d]
                if y["phase"] != o["phase"]:
                    continue
                if y["dma"] is not None or o["dma"] is not None or y["eng"] != o["eng"]:
                    o["sync"].add(d)
                elif o["eng"] != "pe":
                    o["sync"].add(d)
                else:
                    o["order"].add(d)
        succ = [[] for _ in range(n)]
        indeg = [0] * n
        for i, o in enumerate(ops):
            for d in o["sync"] | o["order"]:
                succ[d].append(i)
                indeg[i] += 1
        fin = [0.0] * n
        start = [0.0] * n
        eng_free = {e: 0.0 for e in self.ENGS}
        order = {e: [] for e in self.ENGS}
        LAT = 0.35
        nph = self.phase + 1
        byphase = [[] for _ in range(nph)]
        for i, o in enumerate(ops):
            byphase[o["phase"]].append(i)
        tphase = 0.0
        rcause = {}
        self.fin = fin
        self.start = start
        blev = [0.0] * n
        for i in range(n - 1, -1, -1):
            o = ops[i]
            m = 0.0
            for sidx in succ[i]:
                if blev[sidx] > m:
                    m = blev[sidx]
            blev[i] = m + o["cost"] + (o["dma_time"] if o["dma"] is not None else 0.0)
        PRIO = getattr(self, "prio_mode", 1)
        for ph in range(nph):
            ready = {e: [] for e in self.ENGS}
            rtime = {}
            for i in byphase[ph]:
                if indeg[i] == 0:
                    rtime[i] = tphase
                    ready[ops[i]["eng"]].append(i)
            left = len(byphase[ph])
            while left:
                best = None
                for e in self.ENGS:
                    lst = ready[e]
                    if not lst:
                        continue
                    ef = eng_free[e]
                    cand = None
                    for i in lst:
                        rt = rtime[i]
                        st_ = rt if rt > ef else ef
                        if PRIO:
                            key = (st_, -blev[i], i) if st_ > ef else (ef, -blev[i], i)
                        else:
                            key = (st_, i, i)
                        if cand is None or key < cand[0]:
                            cand = (key, st_, i)
                    if best is None or cand[0] < best[0]:
                        best = (cand[0], cand[1], cand[2], e)
                _, st_, i, e = best
                ready[e].remove(i)
                o = ops[i]
                start[i] = st_
                o['crit'] = ('eng', order[e][-1]) if (order[e] and eng_free[e] >= rtime[i]) else ('dep', rcause.get(i))
                eng_free[e] = st_ + o["cost"]
                fin[i] = st_ + o["cost"] + (o["dma_time"] if o["dma"] is not None else 0.0)
                order[e].append(i)
                left -= 1
                for sidx in succ[i]:
                    indeg[sidx] -= 1
                    same = (ops[sidx]["eng"] == e and o["dma"] is None and ops[sidx]["dma"] is None)
                    t_ = fin[i] + (0.0 if same else LAT)
                    if t_ > rtime.get(sidx, tphase):
                        rcause[sidx] = i
                    rtime[sidx] = max(rtime.get(sidx, tphase), t_)
                    if indeg[sidx] == 0:
                        ready[ops[sidx]["eng"]].append(sidx)
            t_prev = tphase
            tphase = max([tphase] + [fin[i] for i in byphase[ph]]) + 2.0
            self.phase_span = getattr(self, 'phase_span', []) + [tphase - t_prev]
            for e in self.ENGS:
                eng_free[e] = max(eng_free[e], tphase)
        self.sim_time = tphase
        self.order = order
        posn = {}
        for e in self.ENGS:
            for k_, idx in enumerate(order[e]):
                posn[idx] = k_
        for o in ops:
            best = {}
            keep = set()
            for d in o["sync"]:
                y = ops[d]
                if y["dma"] is not None:
                    keep.add(d)
                else:
                    b_ = best.get(y["eng"])
                    if b_ is None or posn[d] > posn[b_]:
                        best[y["eng"]] = d
            keep.update(best.values())
            o["sync"] = keep
            for d in keep:
                if ops[d]["dma"] is None:
                    ops[d]["need_sig"] = True
        self.bar_wait = []
        cnt = {e: 0 for e in self.ENGS}
        pos = {e: 0 for e in self.ENGS}
        dma_hi = {}
        for ph in range(nph):
            self.bar_wait.append(dict([(("eng", e), cnt[e]) for e in self.ENGS if cnt[e] > 0]
                                      + [(("dma", k), v) for k, v in dma_hi.items()]))
            for e in self.ENGS:
                lst = order[e]
                lastc = None
                p0 = pos[e]
                while pos[e] < len(lst) and ops[lst[pos[e]]]["phase"] == ph:
                    pos[e] += 1
                for idx in lst[p0:pos[e]]:
                    if ops[idx]["dma"] is None:
                        lastc = idx
                if lastc is not None and ph < nph - 1:
                    ops[lastc]["need_sig"] = True
                for idx in lst[p0:pos[e]]:
                    oo = ops[idx]
                    if oo["dma"] is None:
                        if oo["need_sig"]:
                            cnt[e] += 1
                            oo["sig"] = cnt[e]
                    else:
                        dma_hi[oo["dma"]] = max(dma_hi.get(oo["dma"], 0), oo["dma_val"])
        self.dma_keys = sorted(dma_count.keys(), key=str)
        self.dma_final = dict(dma_hi)

    def emit(self):
        nc = self.nc
        ops = self.ops
        self.resolve()
        with ExitStack() as st:
            sems = {}
            for e in self.ENGS:
                sems[("eng", e)] = st.enter_context(nc.semaphore("s_" + e))
            for n_, k in enumerate(self.dma_keys):
                sems[("dma", k)] = st.enter_context(nc.semaphore("d%d" % n_))
            block = st.enter_context(nc.Block())

            def run(engname, eng):
                waited = {}

                def wait(key, val):
                    if val <= 0 or waited.get(key, 0) >= val:
                        return
                    eng.wait_ge(sems[key], val)
                    waited[key] = val
                cur_phase = 0
                for idx in self.order[engname]:
                    o = ops[idx]
                    if o["phase"] != cur_phase:
                        cur_phase = o["phase"]
                        for key, val in self.bar_wait[cur_phase].items():
                            if key == ("eng", engname):
                                continue
                            wait(key, val)
                    for d in sorted(o["sync"]):
                        y = ops[d]
                        if y["dma"] is not None:
                            wait(("dma", y["dma"]), y["dma_val"])
                        else:
                            wait(("eng", y["eng"]), y["sig"])
                    ins = o["fn"](eng)
                    if o["dma"] is not None:
                        ins.then_inc(sems[("dma", o["dma"])], 16)
                    elif o["need_sig"]:
                        ins.then_inc(sems[("eng", engname)], 1)
                if engname == "sp":
                    for k, v in self.dma_final.items():
                        wait(("dma", k), v)

            @block.tensor
            def _(e):
                run("pe", e)

            @block.scalar
            def _(e):
                run("act", e)

            @block.vector
            def _(e):
                run("dve", e)

            @block.gpsimd
            def _(e):
                run("pool", e)

            @block.sync
            def _(e):
                run("sp", e)


def build_nc(debug=None, stop_after=None):
    debug = debug or []
    nc = bass.Bass("TRN2", target_bir_lowering=False)
    dr = lambda name, shape, dt=F32: nc.dram_tensor(name, shape, dt, kind="ExternalInput").ap()
    xv = dr("xv", [NB * 128, D])
    padm_d = dr("padm", [128, NB])
    mem_d = dr("mem", [256, D])
    ng_d = dr("norm_gain", [1, D])
    mg_d = dr("mem_norm_gain", [1, D])
    w_in = dr("w_in", [D, INW])
    bf_d = dr("b_forget", [1, 12])
    gqa_d = dr("q_gain_a", [1, 64]); gka_d = dr("k_gain_a", [1, 64])
    gqb_d = dr("q_gain_b", [1, 64]); gkb_d = dr("k_gain_b", [1, 64])
    gqc_d = dr("q_gain_c", [1, 128]); gkc_d = dr("k_gain_c", [1, 128])
    sinks_d = dr("sinks2", [2, 6])
    wmkv = dr("w_mem_kv", [D, 1024])
    wba = dr("w_branch_a", [768, D]); wbb = dr("w_branch_b", [768, D]); wbc = dr("w_branch_c", [512, D])
    wout = dr("w_out", [D, D])
    cmask_d = dr("cmask", [128, 128])
    al_l_d = dr("alibi_l", [6, 12 * 128])
    al_r_d = dr("alibi_r", [6, 12 * 2 * 128])
    am_d = dr("amask", [128, 2 * 128])
    am0_d = dr("amask_s0", [128, 128])
    y_d = nc.dram_tensor("y", [NS * 128, D], F32, kind="ExternalOutput").ap()
    dbg_out = {}

    P = Prog(nc)
    st = ExitStack()
    sb = lambda name, shape, dt: st.enter_context(nc.sbuf_tensor(name, shape, dt))
    with st:
        BIGN = 72000
        big = sb("big", [128, BIGN], BF16)
        gain_bc = sb("gain_bc", [128, D], F32)
        xt = [sb("xt0", [128, D], F32), sb("xt1", [128, D], F32)]
        hn = sb("hn", [128, D], BF16)
        hnb = sb("hnb", [128, D], BF16)
        sqjunk = sb("sqjunk", [128, D], BF16)
        hTt = [sb("hTt0", [128, KC, 128], BF16), sb("hTt1", [128, KC, 128], BF16)]
        ksq = sb("ksq", [128, 768], F32)
        kn = sb("kn", [128, 768], BF16)
        qtmp = sb("qtmp", [128, 768], F32)
        qn = sb("qn", [128, 768], BF16)
        ident = sb("ident", [128, 128], BF16)
        identf = sb("identf", [128, 128], F32)
        U = sb("U", [128, 128], F32)
        onesf = sb("onesf", [128, 128], F32)
        onesb = sb("onesb", [128, 128], BF16)
        cmf = sb("cmf", [128, 128], F32)
        cmb = sb("cmb", [128, 128], BF16)
        bf_bc = sb("bf_bc", [128, 12], F32)
        gA = sb("gA", [128, 64], F32); gA2 = sb("gA2", [128, 64], F32)
        gB = sb("gB", [128, 64], F32); gB2 = sb("gB2", [128, 64], F32)
        gC = sb("gC", [128, 128], F32); gC2 = sb("gC2", [128, 128], F32)
        sinkexp = sb("sinkexp", [128, 6], F32)
        padm = sb("padm_s", [128, NB], F32)
        ss_all = sb("ss_all", [128, 64], F32)
        ln_all = sb("ln_all", [128, 64], F32)
        rstd_all = sb("rstd_all", [128, 64], F32)
        kss = sb("kss", [128, 8, 12], F32)
        kln = sb("kln", [128, 8, 12], F32)
        krs = sb("krs", [128, 8, 12], F32)
        fl = sb("fl", [128, 4, 12], F32)
        lneg = sb("lneg", [128, 4, 12], F32)
        cumprev = sb("cumprev", [128, 12], F32)
        n_all = sb("n_all", [128, NB, 12], F32)
        Ntab = sb("Ntab", [128, NS, 12], F32)
        btab = sb("btab", [128, NB, 12], F32)
        Dfull = sb("Dfull", [128, NS, 12], F32)
        Dpair = sb("Dpair", [128, NS, 6], F32)
        psall = st.enter_context(nc.psum_tensor("psall", [128, 4096], F32))

        def bank(b, n=1):
            return psall[:, b * 512:(b + n) * 512]

        def bankb(b, n=1):
            return psall[:, b * 512:(b + n) * 512].bitcast(BF16)

        def bigv(off, n):
            return big[:, off:off + n]

        def v3(off, a, b):
            return big[:, off:off + a * b].rearrange("p (a b) -> p a b", b=b)

        def dump(name, ap, res, dt):
            if name not in debug:
                return
            shape = list(ap.shape)
            t = nc.dram_tensor("dbg_" + name, shape, dt, kind="ExternalOutput").ap()
            dbg_out[name] = t
            P.op("sp", lambda e: e.dma_start(out=t, in_=ap), reads=res, dma="dbg_" + name)

        A = P.op
        A("sp", lambda e: e.dma_start(out=gain_bc[:], in_=ng_d.partition_broadcast(128)), writes=["gain"], dma="gain")
        A("sp", lambda e: e.dma_start(out=bf_bc[:], in_=bf_d.partition_broadcast(128)), writes=["bf_bc"], dma="c0")
        A("sp", lambda e: e.dma_start(out=gA[:], in_=gqa_d.partition_broadcast(128)), writes=["gA"], dma="c1")
        A("sp", lambda e: e.dma_start(out=gA2[:], in_=gka_d.partition_broadcast(128)), writes=["gA2"], dma="c2")
        A("sp", lambda e: e.dma_start(out=gB[:], in_=gqb_d.partition_broadcast(128)), writes=["gB"], dma="c3")
        A("sp", lambda e: e.dma_start(out=gB2[:], in_=gkb_d.partition_broadcast(128)), writes=["gB2"], dma="c4")
        A("sp", lambda e: e.dma_start(out=gC[:], in_=gqc_d.partition_broadcast(128)), writes=["gC"], dma="c5")
        A("sp", lambda e: e.dma_start(out=gC2[:], in_=gkc_d.partition_broadcast(128)), writes=["gC2"], dma="c6")
        A("sp", lambda e: e.dma_start(out=sinkexp[0:64, :], in_=sinks_d[0:1, :].partition_broadcast(64)),
          writes=["sk0"], dma="c7")
        A("sp", lambda e: e.dma_start(out=sinkexp[64:128, :], in_=sinks_d[1:2, :].partition_broadcast(64)),
          writes=["sk1"], dma="c8")
        A("sp", lambda e: e.dma_start(out=padm[:], in_=padm_d), writes=["padm"], dma="c9")
        A("sp", lambda e: e.dma_start(out=cmf[:], in_=cmask_d), writes=["cmf"], dma="c10")
        A("dve", lambda e: e.memset(identf[:], 1.0), writes=["identf"])
        A("pool", lambda e: e.affine_select(out=identf[:], in_=identf[:], pattern=[[-1, 128]],
                                            compare_op=ALU.is_equal, fill=0.0, base=0, channel_multiplier=1),
          reads=["identf"], writes=["identf"])
        A("dve", lambda e: e.tensor_copy(out=ident[:], in_=identf[:]), reads=["identf"], writes=["ident"])
        A("dve", lambda e: e.memset(U[:], 1.0), writes=["U"])
        A("pool", lambda e: e.affine_select(out=U[:], in_=U[:], pattern=[[1, 128]],
                                            compare_op=ALU.is_ge, fill=0.0, base=0, channel_multiplier=-1),
          reads=["U"], writes=["U"])
        A("dve", lambda e: e.memset(onesf[:], 1.0), writes=["onesf"])
        A("dve", lambda e: e.memset(onesb[:], 1.0), writes=["onesb"])
        A("dve", lambda e: e.memset(cumprev[:], 0.0), writes=["cumprev"])
        A("dve", lambda e: e.tensor_copy(out=cmb[:], in_=cmf[:]), reads=["cmf"], writes=["cmb"])
        A("dve", lambda e: e.scalar_tensor_tensor(out=gA[:], in0=gA[:], scalar=0.125, in1=gA2[:],
                                                  op0=ALU.mult, op1=ALU.mult), reads=["gA", "gA2"], writes=["gA"])
        A("dve", lambda e: e.scalar_tensor_tensor(out=gB[:], in0=gB[:], scalar=0.125, in1=gB2[:],
                                                  op0=ALU.mult, op1=ALU.mult), reads=["gB", "gB2"], writes=["gB"])
        A("dve", lambda e: e.scalar_tensor_tensor(out=gC[:], in0=gC[:], scalar=float(128 ** -0.5), in1=gC2[:],
                                                  op0=ALU.mult, op1=ALU.mult), reads=["gC", "gC2"], writes=["gC"])
        A("act", lambda e: e.activation(out=sinkexp[:], in_=sinkexp[:], func=AF.Exp),
          reads=["sk0", "sk1"], writes=["sinkexp"])

        w_in_v = w_in.rearrange("(kc p) n -> p kc n", p=128)


        def load_w4(dst, src, resname, key):
            for q4 in range(4):
                A("pool", lambda e, q4=q4: e.dma_start(out=dst[:, 4 * q4:4 * q4 + 4, :], in_=src[:, 4 * q4:4 * q4 + 4, :]),
                  writes=[(resname, q4)], dma=(key, q4))

        def norm_block(tag, idx, src_ap, dst_lo, dst_hi, dst_res, junk_ap, junk_res):
            s = idx % 2
            hx = hn if idx % 2 == 0 else hnb
            hres_ = ("hn", idx % 2)
            A("sp", lambda e: e.dma_start(out=xt[s][:], in_=src_ap), writes=[("xt", s)], dma=("xt", s))
            A("act", lambda e: e.activation(out=junk_ap, in_=xt[s][:], func=AF.Square, accum_out=ss_all[:, idx:idx + 1]),
              reads=[("xt", s)], writes=junk_res + [("ss", idx)])
            A("act", lambda e: e.activation(out=ln_all[:, idx:idx + 1], in_=ss_all[:, idx:idx + 1], func=AF.Ln,
                                            scale=1.0 / D, bias=EPS), reads=[("ss", idx)], writes=[("ln", idx)])
            A("act", lambda e: e.activation(out=rstd_all[:, idx:idx + 1], in_=ln_all[:, idx:idx + 1], func=AF.Exp,
                                            scale=-0.5), reads=[("ln", idx)], writes=[("rstd", idx)])
            A("dve", lambda e: e.scalar_tensor_tensor(out=hx[:], in0=xt[s][:], scalar=rstd_all[:, idx:idx + 1],
                                                      in1=gain_bc[:], op0=ALU.mult, op1=ALU.mult),
              reads=[("xt", s), ("rstd", idx), "gain"], writes=[hres_])
            pT = bankb(0, 2)
            for kc in range(KC):
                A("pe", lambda e, kc=kc: e.transpose(out=pT[:, kc * 128:(kc + 1) * 128],
                                                     in_=hx[:, kc * 128:(kc + 1) * 128], identity=ident[:]),
                  reads=[hres_, "ident"], writes=["ps0" if kc < 8 else "ps1"])
            A("act", lambda e: e.activation(out=dst_lo, in_=pT[:, 0:1024].rearrange("p (a b) -> p a b", b=128),
                                            func=AF.Copy), reads=["ps0"], writes=[dst_res + ("lo",)])
            A("dve", lambda e: e.tensor_copy(out=dst_hi, in_=pT[:, 1024:2048].rearrange("p (a b) -> p a b", b=128)),
              reads=["ps1"], writes=[dst_res + ("hi",)])

        def proj_tm(lhs_fn, lhs_res, w_view, w_res, c0, n, out_ap, out_res):
            for kc in range(KC):
                A("pe", lambda e, kc=kc: e.matmul(out=out_ap, lhsT=lhs_fn(kc), rhs=w_view[:, kc, c0:c0 + n],
                                                  start=(kc == 0), stop=(kc == KC - 1)),
                  reads=[lhs_res + ("lo",) if kc < 8 else lhs_res + ("hi",), (w_res, kc // 4)], writes=out_res)

        def head_rstd(tag, idx, ps_ap, ps_res, nh, dh):
            idx = idx % 8
            A("act", lambda e: e.activation(out=ksq[:, 0:nh * dh], in_=ps_ap, func=AF.Square),
              reads=ps_res, writes=["ksq"])
            A("dve", lambda e: e.tensor_reduce(out=kss[:, idx, 0:nh],
                                               in_=ksq[:, 0:nh * dh].rearrange("p (h d) -> p h d", d=dh),
                                               axis=AX.X, op=ALU.add), reads=["ksq"], writes=[("kss", idx)])
            A("act", lambda e: e.activation(out=kln[:, idx, 0:nh], in_=kss[:, idx, 0:nh], func=AF.Ln,
                                            scale=1.0 / dh, bias=EPS), reads=[("kss", idx)], writes=[("kln", idx)])
            A("act", lambda e: e.activation(out=krs[:, idx, 0:nh], in_=kln[:, idx, 0:nh], func=AF.Exp, scale=-0.5),
              reads=[("kln", idx)], writes=[("krs", idx)])

        W1a = v3(0, KC, 1548)
        QT_B = v3(24768, 6, 1024)
        KT_B = v3(30912, 6, 4096)
        load_w4(W1a[:, :, 0:768], w_in_v[:, :, OFF["kB"]:OFF["kB"] + 768], "W1a_k", "W1a_k")
        load_w4(W1a[:, :, 768:780], w_in_v[:, :, OFF["fB"]:OFF["fB"] + 12], "W1a_f", "W1a_f")
        load_w4(W1a[:, :, 780:1548], w_in_v[:, :, OFF["qB"]:OFF["qB"] + 768], "W1a_q", "W1a_q")
        pKT = bankb(6)
        for j in range(NB):
            s = j % 2
            hres = ("hTt", s)
            norm_block("c1", j, xv[j * 128:(j + 1) * 128, :], hTt[s][:, 0:8, :], hTt[s][:, 8:16, :], hres,
                       sqjunk[:], ["sqjunk"])
            lhs = lambda kc, s=s: hTt[s][:, kc, :]
            kb0 = 2 + 2 * (j % 2)
            kr0, kr1 = "ps%d" % kb0, "ps%d" % (kb0 + 1)
            proj_tm(lhs, hres, W1a, "W1a_k", 0, 512, bank(kb0)[:, 0:512], [kr0])
            for kc in range(KC):
                A("pe", lambda e, kc=kc, s=s, kb0=kb0: e.matmul(out=bank(kb0 + 1)[:, 0:268], lhsT=hTt[s][:, kc, :],
                                                       rhs=W1a[:, kc, 512:780], start=(kc == 0), stop=(kc == KC - 1)),
                  reads=[hres + ("lo",) if kc < 8 else hres + ("hi",), ("W1a_k", kc // 4), ("W1a_f", kc // 4)], writes=[kr1])
            pK = psall[:, kb0 * 512:kb0 * 512 + 768]
            head_rstd("k", j, pK, [kr0, kr1], 12, 64)
            A("dve", lambda e, j=j, pK=pK: e.tensor_tensor(out=kn[:].rearrange("p (h d) -> p h d", d=64),
                                                    in0=pK.rearrange("p (h d) -> p h d", d=64),
                                                    in1=krs[:, j % 8, 0:12].unsqueeze(2).to_broadcast([128, 12, 64]),
                                                    op=ALU.mult),
              reads=[kr0, kr1, ("krs", j % 8)], writes=["kn"])
            for hp in range(6):
                A("pe", lambda e, hp=hp: e.transpose(out=pKT[:, hp * 128:(hp + 1) * 128],
                                                     in_=kn[:, hp * 128:(hp + 1) * 128], identity=ident[:]),
                  reads=["kn", "ident"], writes=["ps6"])
            A("act", lambda e, j=j: e.activation(out=KT_B[:, :, j * 128:(j + 1) * 128],
                                                 in_=pKT[:, 0:768].rearrange("p (a b) -> p a b", b=128), func=AF.Copy),
              reads=["ps6"], writes=[("KT_B", j)])
            A("dve", lambda e, j=j, kb0=kb0: e.tensor_tensor(out=fl[:, j % 4, :], in0=bank(kb0 + 1)[:, 256:268], in1=bf_bc[:], op=ALU.add),
              reads=[kr1, "bf_bc"], writes=[("fl", j % 4)])
            A("act", lambda e, j=j: e.activation(out=fl[:, j % 4, :], in_=fl[:, j % 4, :], func=AF.Exp, scale=-1.0),
              reads=[("fl", j % 4)], writes=[("fl", j % 4)])
            A("act", lambda e, j=j: e.activation(out=lneg[:, j % 4, :], in_=fl[:, j % 4, :], func=AF.Ln, bias=1.0),
              reads=[("fl", j % 4)], writes=[("lneg", j % 4)])
            A("pe", lambda e, j=j: e.matmul(out=bank(7)[:, 0:12], lhsT=U[:], rhs=lneg[:, j % 4, :], start=True, stop=False),
              reads=["U", ("lneg", j % 4)], writes=["ps7"])
            A("pe", lambda e: e.matmul(out=bank(7)[:, 0:12], lhsT=onesf[:], rhs=cumprev[:], start=False, stop=True),
              reads=["onesf", "cumprev"], writes=["ps7"])
            A("dve", lambda e, j=j: e.tensor_copy(out=n_all[:, j, :], in_=bank(7)[:, 0:12]),
              reads=["ps7"], writes=[("n_all", j)])
            A("dve", lambda e, j=j: e.tensor_tensor(out=cumprev[:], in0=cumprev[:], in1=lneg[:, j % 4, :], op=ALU.add),
              reads=["cumprev", ("lneg", j % 4)], writes=["cumprev"])
            if j % 4 == 3:
                g = j // 4
                A("pe", lambda e: e.matmul(out=bank(7)[:, 16:28], lhsT=onesf[:], rhs=cumprev[:], start=True, stop=True),
                  reads=["onesf", "cumprev"], writes=["ps7"])
                A("dve", lambda e, g=g: e.tensor_copy(out=Ntab[:, g, :], in_=bank(7)[:, 16:28]),
                  reads=["ps7"], writes=[("Ntab", g)])
                pQ = psall[:, 2 * 512:2 * 512 + 768]
                proj_tm(lhs, hres, W1a, "W1a_q", 780, 512, bank(2)[:, 0:512], ["ps2"])
                proj_tm(lhs, hres, W1a, "W1a_q", 1292, 256, bank(3)[:, 0:256], ["ps3"])
                qi = (32 + g) % 8
                A("act", lambda e, pQ=pQ: e.activation(out=qtmp[:], in_=pQ, func=AF.Square), reads=["ps2", "ps3"], writes=["qtmp"])
                A("dve", lambda e, qi=qi: e.tensor_reduce(out=kss[:, qi, 0:12], in_=qtmp[:].rearrange("p (h d) -> p h d", d=64),
                                                         axis=AX.X, op=ALU.add), reads=["qtmp"], writes=[("kss", qi)])
                A("act", lambda e, qi=qi: e.activation(out=kln[:, qi, 0:12], in_=kss[:, qi, 0:12], func=AF.Ln, scale=1.0 / 64, bias=EPS),
                  reads=[("kss", qi)], writes=[("kln", qi)])
                A("act", lambda e, qi=qi: e.activation(out=krs[:, qi, 0:12], in_=kln[:, qi, 0:12], func=AF.Exp, scale=-0.5),
                  reads=[("kln", qi)], writes=[("krs", qi)])
                A("dve", lambda e, qi=qi, pQ=pQ: e.tensor_tensor(out=qtmp[:].rearrange("p (h d) -> p h d", d=64),
                                                               in0=pQ.rearrange("p (h d) -> p h d", d=64),
                                                               in1=krs[:, qi, 0:12].unsqueeze(2).to_broadcast([128, 12, 64]),
                                                               op=ALU.mult),
                  reads=["ps2", "ps3", ("krs", qi)], writes=["qtmp"])
                A("dve", lambda e: e.tensor_tensor(out=qn[:].rearrange("p (h d) -> p h d", d=64),
                                                   in0=qtmp[:].rearrange("p (h d) -> p h d", d=64),
                                                   in1=gB[:].unsqueeze(1).to_broadcast([128, 12, 64]), op=ALU.mult),
                  reads=["qtmp", "gB"], writes=["qn"])
                pQT = bank(7)[:, 128:512].bitcast(BF16)
                for hp in range(6):
                    A("pe", lambda e, hp=hp, pQT=pQT: e.transpose(out=pQT[:, hp * 128:(hp + 1) * 128],
                                                                 in_=qn[:, hp * 128:(hp + 1) * 128], identity=ident[:]),
                      reads=["qn", "ident"], writes=["ps7"])
                A("dve", lambda e, g=g, pQT=pQT: e.tensor_copy(out=QT_B[:, :, g * 128:(g + 1) * 128],
                                                              in_=pQT.rearrange("p (a b) -> p a b", b=128)),
                  reads=["ps7"], writes=[("QT_B", g)])
        dump("KT_B", KT_B, [("KT_B", j) for j in range(NB)], BF16)
        dump("QT_B", QT_B, [("QT_B", g) for g in range(NS)], BF16)
        dump("n_all", n_all[:], [("n_all", j) for j in range(NB)], F32)
        dump("Ntab", Ntab[:], [("Ntab", g) for g in range(NS)], F32)
        if stop_after == "1a":
            A("sp", lambda e: e.dma_start(out=y_d[0:128, :], in_=xt[0][:]), reads=[("xt", 0)], dma="y")
            P.emit()
            return nc, dbg_out
        P.barrier()

        W1b = v3(0, KC, 768)
        V_Blo = v3(12288, 16, 768)
        V_Bhi = v3(55488, 16, 768)

        def VB(j):
            return V_Blo[:, j, :] if j < 16 else V_Bhi[:, j - 16, :]
        W1A_NAMES = [("W1a_k", q_) for q_ in range(4)] + [("W1a_f", q_) for q_ in range(4)] + [("W1a_q", q_) for q_ in range(4)]
        load_w4(W1b, w_in_v[:, :, OFF["vB"]:OFF["vB"] + 768], "W1b", "W1b")
        for j in range(NB):
            s = j % 2
            hres = ("hTt", s)
            norm_block("c2", j, xv[j * 128:(j + 1) * 128, :], hTt[s][:, 0:8, :], hTt[s][:, 8:16, :], hres,
                       sqjunk[:], ["sqjunk"])
            lhs = lambda kc, s=s: hTt[s][:, kc, :]
            b0 = 2 + 2 * (j % 2)
            proj_tm(lhs, hres, W1b, "W1b", 0, 512, bank(b0)[:, 0:512], ["ps%d" % b0])
            proj_tm(lhs, hres, W1b, "W1b", 512, 256, bank(b0 + 1)[:, 0:256], ["ps%d" % (b0 + 1)])
            pV = psall[:, b0 * 512:b0 * 512 + 768]
            eng = "act" if j % 2 == 0 else "dve"
            if eng == "act":
                A("act", lambda e, j=j, pV=pV: e.activation(out=VB(j), in_=pV, func=AF.Copy),
                  reads=["ps%d" % b0, "ps%d" % (b0 + 1)], writes=[("V_B", j)])
            else:
                A("dve", lambda e, j=j, pV=pV: e.tensor_copy(out=VB(j), in_=pV),
                  reads=["ps%d" % b0, "ps%d" % (b0 + 1)], writes=[("V_B", j)])
        dump("V_Blo", V_Blo, [("V_B", j) for j in range(16)], BF16)
        dump("V_Bhi", V_Bhi, [("V_B", j) for j in range(16, 32)], BF16)
        if stop_after == "1b":
            A("sp", lambda e: e.dma_start(out=y_d[0:128, :], in_=xt[0][:]), reads=[("xt", 0)], dma="y")
            P.emit()
            return nc, dbg_out
        P.barrier()

        ybT = v3(0, 6, 1024)
        PT = [bigv(6144, 512), bigv(6656, 512), bigv(11264, 512), bigv(11776, 512)]
        Anum = bigv(7168, 1024).bitcast(F32)
        Aden = bigv(8192, 1024).bitcast(F32)
        rden = bigv(9216, 2048).bitcast(F32)
        for g in range(NS):
            A("dve", lambda e, g=g: e.tensor_tensor(out=btab[:, 4 * g:4 * g + 4, :], in0=n_all[:, 4 * g:4 * g + 4, :],
                                                    in1=Ntab[:, g, :].unsqueeze(1).to_broadcast([128, 4, 12]),
                                                    op=ALU.subtract),
              reads=[("n_all", 4 * g + r) for r in range(4)] + [("Ntab", g)], writes=[("btab0", g)])
        A("dve", lambda e: e.tensor_tensor(out=btab[:], in0=btab[:],
                                           in1=padm[:].unsqueeze(2).to_broadcast([128, NB, 12]), op=ALU.subtract),
          reads=[("btab0", g) for g in range(NS)] + ["padm"], writes=["btab"])
        A("dve", lambda e: e.tensor_tensor(out=Dfull[:, 1:8, :], in0=Ntab[:, 0:7, :], in1=Ntab[:, 1:8, :], op=ALU.subtract),
          reads=[("Ntab", g) for g in range(NS)], writes=["Dfull"])
        A("act", lambda e: e.activation(out=Dfull[:, 1:8, :], in_=Dfull[:, 1:8, :], func=AF.Exp),
          reads=["Dfull"], writes=["Dfull"])
        for e2 in range(2):
            A("dve", lambda e, e2=e2: e.tensor_copy(out=Dpair[64 * e2:64 * e2 + 64, 1:8, :],
                                                    in_=Dfull[64 * e2:64 * e2 + 64, 1:8, e2::2]),
              reads=["Dfull"], writes=[("Dpair", e2)])
        dump("btab", btab[:], ["btab"], F32)
        dump("Dpair", Dpair[:], [("Dpair", 0), ("Dpair", 1)], F32)
        PASSES = [(0, 3), (4, 7)]
        cnt = 0
        gcnt = 0
        for hp in range(6):
            for (slo, shi) in PASSES:
                qlo_pass = slo * 128
                for g in range(shi + 1):
                    q0 = max(g, slo) * 128
                    Nq = (shi + 1) * 128 - q0
                    ob = 4 + (gcnt % 2)
                    db = 6 + (gcnt % 2)
                    gcnt += 1
                    for r in range(4):
                        kb = 4 * g + r
                        for e2 in range(2):
                            h = 2 * hp + e2
                            sbk = cnt % 4
                            cnt += 1
                            S = bank(sbk)[:, 0:Nq]
                            diag = (r == 3 and g >= slo)
                            A("pe", lambda e, S=S, e2=e2, hp=hp, kb=kb, q0=q0, Nq=Nq, diag=diag: e.matmul(
                                out=S, lhsT=KT_B[64 * e2:64 * e2 + 64, hp, kb * 128:(kb + 1) * 128],
                                rhs=QT_B[64 * e2:64 * e2 + 64, hp, q0:q0 + Nq], start=True, stop=(not diag)),
                              reads=[("KT_B", kb)] + [("QT_B", i) for i in range(q0 // 128, shi + 1)],
                              writes=["ps%d" % sbk])
                            if diag:
                                A("pe", lambda e, S=S: e.matmul(out=S[:, 0:128], lhsT=ident[:], rhs=cmb[:],
                                                                start=False, stop=True),
                                  reads=["ident", "cmb"], writes=["ps%d" % sbk])
                            A("act", lambda e, S=S, sbk=sbk, Nq=Nq, kb=kb, h=h: e.activation(
                                out=PT[sbk][:, 0:Nq], in_=S, func=AF.Exp, bias=btab[:, kb, h:h + 1], scale=1.0),
                              reads=["ps%d" % sbk, "btab"], writes=[("PT", sbk)])
                            A("pe", lambda e, e2=e2, ob=ob, Nq=Nq, kb=kb, h=h, sbk=sbk, r=r: e.matmul(
                                out=bank(ob)[64 * e2:64 * e2 + 64, 0:Nq], lhsT=VB(kb)[:, h * 64:(h + 1) * 64],
                                rhs=PT[sbk][:, 0:Nq], start=(r == 0), stop=(r == 3)),
                              reads=[("V_B", kb), ("PT", sbk)], writes=["ps%d" % ob])
                            A("pe", lambda e, e2=e2, db=db, Nq=Nq, sbk=sbk, r=r: e.matmul(
                                out=bank(db)[64 * e2:64 * e2 + 64, 0:Nq], lhsT=onesb[:, 0:64],
                                rhs=PT[sbk][:, 0:Nq], start=(r == 0), stop=(r == 3)),
                              reads=["onesb", ("PT", sbk)], writes=["ps%d" % db])
                    a0 = q0 - qlo_pass
                    if g == 0:
                        A("dve", lambda e, ob=ob, Nq=Nq, a0=a0: e.tensor_copy(out=Anum[:, a0:a0 + Nq], in_=bank(ob)[:, 0:Nq]),
                          reads=["ps%d" % ob], writes=["Anum"])
                        A("act", lambda e, db=db, Nq=Nq, a0=a0: e.activation(out=Aden[:, a0:a0 + Nq], in_=bank(db)[:, 0:Nq],
                                                                            func=AF.Copy),
                          reads=["ps%d" % db], writes=["Aden"])
                    else:
                        A("dve", lambda e, ob=ob, Nq=Nq, a0=a0, g=g, hp=hp: e.scalar_tensor_tensor(
                            out=Anum[:, a0:a0 + Nq], in0=Anum[:, a0:a0 + Nq], scalar=Dpair[:, g, hp:hp + 1],
                            in1=bank(ob)[:, 0:Nq], op0=ALU.mult, op1=ALU.add),
                          reads=["ps%d" % ob, "Anum", ("Dpair", 0), ("Dpair", 1)], writes=["Anum"])
                        A("dve", lambda e, db=db, Nq=Nq, a0=a0, g=g, hp=hp: e.scalar_tensor_tensor(
                            out=Aden[:, a0:a0 + Nq], in0=Aden[:, a0:a0 + Nq], scalar=Dpair[:, g, hp:hp + 1],
                            in1=bank(db)[:, 0:Nq], op0=ALU.mult, op1=ALU.add),
                          reads=["ps%d" % db, "Aden", ("Dpair", 0), ("Dpair", 1)], writes=["Aden"])
                A("dve", lambda e: e.reciprocal(out=rden[:, 0:512], in_=Aden[:, 0:512]), reads=["Aden"], writes=["rden"])
                A("dve", lambda e, hp=hp, qlo=qlo_pass: e.tensor_tensor(out=ybT[:, hp, qlo:qlo + 512], in0=Anum[:, 0:512],
                                                                        in1=rden[:, 0:512], op=ALU.mult),
                  reads=["Anum", "rden"], writes=[("ybT", hp, slo)])
        dump("ybT", ybT, [("ybT", hp, slo) for hp in range(6) for slo in (0, 4)], BF16)
        if stop_after == "2b":
            A("sp", lambda e: e.dma_start(out=y_d[0:128, :], in_=xt[0][:]), reads=[("xt", 0)], dma="y")
            P.emit()
            return nc, dbg_out
        P.barrier()

        hT_own = v3(6144, KC, 1024)
        W_kvA = v3(22528, KC, 512)
        KT_A = big[:, 30720:38912].rearrange("p (k b t) -> p k b t", k=4, b=16)
        V_A = v3(38912, 16, 256)
        pKT = bankb(6)
        load_w4(W_kvA, w_in_v[:, :, OFF["kA"]:OFF["kA"] + 512], "W_kvA", "W_kvA")
        junkf = sqjunk[:]
        junkr = ["sqjunk"]
        for i in range(NS):
            for own in (0, 1):
                vb = 4 * i + 2 + own
                blk = 2 * i + own
                if own:
                    hres = ("hT_own", i)
                    norm_block("o", blk, xv[vb * 128:(vb + 1) * 128, :], hT_own[:, 0:8, i * 128:(i + 1) * 128],
                               hT_own[:, 8:16, i * 128:(i + 1) * 128], hres, junkf, junkr)
                    lhs = lambda kc, i=i: hT_own[:, kc, i * 128:(i + 1) * 128]
                else:
                    hres = ("hTt", 0)
                    norm_block("o", blk, xv[vb * 128:(vb + 1) * 128, :], hTt[0][:, 0:8, :], hTt[0][:, 8:16, :],
                               hres, junkf, junkr)
                    lhs = lambda kc: hTt[0][:, kc, :]
                pb = 2 + (blk % 2)
                proj_tm(lhs, hres, W_kvA, "W_kvA", 0, 512, bank(pb)[:, 0:512], ["ps%d" % pb])
                head_rstd("ka", blk, bank(pb)[:, 0:256], ["ps%d" % pb], 4, 64)
                A("dve", lambda e, pb=pb, blk=blk: e.tensor_tensor(
                    out=kn[:, 0:512].rearrange("p (k u d) -> p k u d", k=4, u=2),
                    in0=bank(pb)[:, 0:256].rearrange("p (k d) -> p k d", d=64).unsqueeze(2).to_broadcast([128, 4, 2, 64]),
                    in1=krs[:, blk % 8, 0:4].unsqueeze(2).unsqueeze(3).to_broadcast([128, 4, 2, 64]), op=ALU.mult),
                  reads=["ps%d" % pb, ("krs", blk % 8)], writes=["kn"])
                for k4 in range(4):
                    A("pe", lambda e, k4=k4: e.transpose(out=pKT[:, k4 * 128:(k4 + 1) * 128],
                                                         in_=kn[:, k4 * 128:(k4 + 1) * 128], identity=ident[:]),
                      reads=["kn", "ident"], writes=["ps6"])
                A("act", lambda e, blk=blk: e.activation(out=KT_A[:, :, blk, :],
                                                         in_=pKT[:, 0:512].rearrange("p (a b) -> p a b", b=128), func=AF.Copy),
                  reads=["ps6"], writes=[("KT_A", blk)])
                A("act", lambda e, blk=blk, pb=pb: e.activation(out=V_A[:, blk, :], in_=bank(pb)[:, 256:512], func=AF.Copy),
                  reads=["ps%d" % pb], writes=[("V_A", blk)])
        dump("hT_own", hT_own, [("hT_own", i, x) for i in range(NS) for x in ("lo", "hi")], BF16)
        dump("KT_A", KT_A, [("KT_A", b_) for b_ in range(16)], BF16)
        dump("V_A", V_A, [("V_A", b_) for b_ in range(16)], BF16)
        hT_res = [("hT_own", i, x) for i in range(NS) for x in ("lo", "hi")]
        if stop_after == "2a":
            A("sp", lambda e: e.dma_start(out=y_d[0:128, :], in_=xt[0][:]), reads=[("xt", 0)], dma="y")
            P.emit()
            return nc, dbg_out

        wm = v3(43008, KC, 1024)
        mnT = v3(59392, KC, 256)
        KT_C = v3(63488, 4, 256)
        V_C = v3(64512, 2, 512)
        wm_v = wmkv.rearrange("(kc p) n -> p kc n", p=128)
        A("sp", lambda e: e.dma_start(out=gain_bc[:], in_=mg_d.partition_broadcast(128)), writes=["gain"], dma="gain")
        load_w4(wm[:, :, 0:512], wm_v[:, :, 0:512], "wm0", "wm0")
        load_w4(wm[:, :, 512:1024], wm_v[:, :, 512:1024], "wm1", "wm1")
        for mb in range(2):
            hres = ("mnT", mb)
            norm_block("m", 16 + mb, mem_d[mb * 128:(mb + 1) * 128, :], mnT[:, 0:8, mb * 128:(mb + 1) * 128],
                       mnT[:, 8:16, mb * 128:(mb + 1) * 128], hres, junkf, junkr)
            lhs = lambda kc, mb=mb: mnT[:, kc, mb * 128:(mb + 1) * 128]
            proj_tm(lhs, hres, wm, "wm0", 0, 512, bank(2)[:, 0:512], ["ps2"])
            proj_tm(lhs, hres, wm, "wm1", 512, 512, bank(3)[:, 0:512], ["ps3"])
            head_rstd("kc", 16 + mb, bank(2)[:, 0:512], ["ps2"], 4, 128)
            A("dve", lambda e, mb=mb: e.tensor_tensor(
                out=kn[:, 0:512].rearrange("p (k d) -> p k d", d=128),
                in0=bank(2)[:, 0:512].rearrange("p (k d) -> p k d", d=128),
                in1=krs[:, (16 + mb) % 8, 0:4].unsqueeze(2).to_broadcast([128, 4, 128]), op=ALU.mult),
              reads=["ps2", ("krs", (16 + mb) % 8)], writes=["kn"])
            for k4 in range(4):
                A("pe", lambda e, k4=k4: e.transpose(out=pKT[:, k4 * 128:(k4 + 1) * 128],
                                                     in_=kn[:, k4 * 128:(k4 + 1) * 128], identity=ident[:]),
                  reads=["kn", "ident"], writes=["ps6"])
            A("act", lambda e, mb=mb: e.activation(out=KT_C[:, :, mb * 128:(mb + 1) * 128],
                                                   in_=pKT[:, 0:512].rearrange("p (a b) -> p a b", b=128), func=AF.Copy),
              reads=["ps6"], writes=[("KT_C", mb)])
            A("dve", lambda e, mb=mb: e.tensor_copy(out=V_C[:, mb, :], in_=bank(3)[:, 0:512]),
              reads=["ps3"], writes=[("V_C", mb)])
        dump("KT_C", KT_C, [("KT_C", 0), ("KT_C", 1)], BF16)
        dump("V_C", V_C, [("V_C", 0), ("V_C", 1)], BF16)
        if stop_after == "C1":
            A("sp", lambda e: e.dma_start(out=y_d[0:128, :], in_=xt[0][:]), reads=[("xt", 0)], dma="y")
            P.emit()
            return nc, dbg_out
        P.barrier()

        ws = [v3(22528, KC, 512), v3(43008, KC, 512)]
        wsc = [0]

        def load_ws(col0, n, dst0=0):
            k = wsc[0] % 2
            wsc[0] += 1
            load_w4(ws[k][:, :, dst0:dst0 + n], w_in_v[:, :, col0:col0 + n], ("ws", k), ("ws", k))
            return k

        bctr = [0]

        def next_bank():
            b_ = bctr[0] % 8
            bctr[0] += 1
            return b_

        def q_proj(k, c0, nh, dh, gtile, gres, QT, hp0, tagi):
            n = nh * dh
            nt = n // 128
            for i in range(NS):
                pb = next_bank()
                idx = tagi * 8 + i
                proj_tm(lambda kc, i=i: hT_own[:, kc, i * 128:(i + 1) * 128], ("hT_own", i), ws[k], ("ws", k),
                        c0, n, bank(pb)[:, 0:n], ["ps%d" % pb])
                head_rstd("q", idx, bank(pb)[:, 0:n], ["ps%d" % pb], nh, dh)
                A("dve", lambda e, pb=pb, idx=idx: e.tensor_tensor(
                    out=qtmp[:, 0:n].rearrange("p (h d) -> p h d", d=dh),
                    in0=bank(pb)[:, 0:n].rearrange("p (h d) -> p h d", d=dh),
                    in1=krs[:, idx % 8, 0:nh].unsqueeze(2).to_broadcast([128, nh, dh]), op=ALU.mult),
                  reads=["ps%d" % pb, ("krs", idx % 8)], writes=["qtmp"])
                A("dve", lambda e: e.tensor_tensor(
                    out=kn[:, 0:n].rearrange("p (h d) -> p h d", d=dh),
                    in0=qtmp[:, 0:n].rearrange("p (h d) -> p h d", d=dh),
                    in1=gtile[:].unsqueeze(1).to_broadcast([128, nh, dh]), op=ALU.mult),
                  reads=["qtmp", gres], writes=["kn"])
                pt = next_bank()
                pTb = bankb(pt)
                for t in range(nt):
                    A("pe", lambda e, t=t, pTb=pTb: e.transpose(out=pTb[:, t * 128:(t + 1) * 128],
                                                               in_=kn[:, t * 128:(t + 1) * 128], identity=ident[:]),
                      reads=["kn", "ident"], writes=["ps%d" % pt])
                A("act", lambda e, i=i, pTb=pTb: e.activation(
                    out=QT[:, hp0:hp0 + nt, i * 128:(i + 1) * 128],
                    in_=pTb[:, 0:nt * 128].rearrange("p (a b) -> p a b", b=128), func=AF.Copy),
                  reads=["ps%d" % pt], writes=[("QT", hp0, i)])

        def z_proj(k, c0, nch, SZ, ch0):
            for c in range(nch):
                for half in range(2):
                    pb = next_bank()
                    for kc in range(KC):
                        A("pe", lambda e, kc=kc, c=c, half=half, pb=pb: e.matmul(
                            out=bank(pb)[:, 0:512], lhsT=ws[k][:, kc, c0 + c * 128:c0 + (c + 1) * 128],
                            rhs=hT_own[:, kc, half * 512:(half + 1) * 512], start=(kc == 0), stop=(kc == KC - 1)),
                          reads=[(("ws", k), kc // 4)] + [("hT_own", i, "lo" if kc < 8 else "hi") for i in range(4 * half, 4 * half + 4)],
                          writes=["ps%d" % pb])
                    A("act", lambda e, c=c, half=half, pb=pb: e.activation(
                        out=SZ[:, ch0 + c, half * 512:(half + 1) * 512], in_=bank(pb)[:, 0:512], func=AF.Silu),
                      reads=["ps%d" % pb], writes=[("SZ", ch0 + c, half)])

        QT_A = v3(51200, 6, 1024)
        szA = v3(57344, 6, 1024)
        yazT = v3(65536, 6, 1024)
        TL = xt[0][:].bitcast(BF16)[0:6, 0:1536].rearrange("p (h q) -> p h q", h=12)
        TR = xt[1][:].bitcast(BF16)[0:6, 0:3072].rearrange("p (h l q) -> p h l q", h=12, l=2)
        AM = gain_bc[:].bitcast(BF16)[:, 0:256].rearrange("p (l q) -> p l q", l=2)
        AM0 = gain_bc[:].bitcast(BF16)[:, 256:384]
        A("pool", lambda e: e.dma_start(out=xt[0][:].bitcast(BF16)[0:6, 0:1536], in_=al_l_d), writes=[("xt", 0)], dma=("xtp", 0))
        A("pool", lambda e: e.dma_start(out=xt[1][:].bitcast(BF16)[0:6, 0:3072], in_=al_r_d), writes=[("xt", 1)], dma=("xtp", 1))
        A("pool", lambda e: e.dma_start(out=gain_bc[:].bitcast(BF16)[:, 0:256], in_=am_d), writes=["gain"], dma="gainp")
        A("pool", lambda e: e.dma_start(out=gain_bc[:].bitcast(BF16)[:, 256:384], in_=am0_d), writes=["gain0"], dma="gainp0")
        k = load_ws(OFF["qA"], 512)
        q_proj(k, 0, 8, 64, gA, "gA", QT_A, 0, 0)
        k = load_ws(OFF["qA"] + 512, 256)
        q_proj(k, 0, 4, 64, gA, "gA", QT_A, 4, 1)
        k = load_ws(OFF["zA"], 512)
        z_proj(k, 0, 4, szA, 0)
        k = load_ws(OFF["zA"] + 512, 256)
        z_proj(k, 0, 2, szA, 4)
        PTa = [hn[:, 0:512], hn[:, 512:1024]]
        tmpf = hn[:, 1024:2048].bitcast(F32)
        QA_res = lambda hp, i: [("QT", 0 if hp < 4 else 4, i)]
        pcnt = 0
        for i in range(NS):
            for hp in range(6):
                sb_ = pcnt % 2
                pcnt += 1
                S = bank(sb_)
                OD = bank(2 + sb_)
                for e2 in range(2):
                    h = 2 * hp + e2
                    kvh = h // 3
                    for blk in range(2):
                        mk = AM0 if (i == 0 and blk == 0) else AM[:, blk, :]
                        mres = "gain0" if (i == 0 and blk == 0) else "gain"
                        reg = S[:, (e2 * 2 + blk) * 128:(e2 * 2 + blk + 1) * 128]
                        A("pe", lambda e, reg=reg, e2=e2, kvh=kvh, i=i, blk=blk, hp=hp: e.matmul(
                            out=reg, lhsT=KT_A[64 * e2:64 * e2 + 64, kvh, 2 * i + blk, :],
                            rhs=QT_A[64 * e2:64 * e2 + 64, hp, i * 128:(i + 1) * 128], start=True, stop=False),
                          reads=[("KT_A", 2 * i + blk)] + QA_res(hp, i), writes=["ps%d" % sb_])
                        A("pe", lambda e, reg=reg, mk=mk: e.matmul(out=reg, lhsT=ident[:], rhs=mk, start=False, stop=False),
                          reads=["ident", mres], writes=["ps%d" % sb_])
                        A("pe", lambda e, reg=reg, h=h, blk=blk: e.matmul(out=reg, lhsT=TL[:, h, :], rhs=TR[:, h, blk, :],
                                                                         start=False, stop=True),
                          reads=[("xt", 0), ("xt", 1)], writes=["ps%d" % sb_])
                A("act", lambda e, S=S, sb_=sb_: e.activation(out=PTa[sb_], in_=S[:, 0:512], func=AF.Exp),
                  reads=["ps%d" % sb_], writes=[("PTa", sb_)])
                for e2 in range(2):
                    h = 2 * hp + e2
                    kvh = h // 3
                    for blk in range(2):
                        A("pe", lambda e, OD=OD, e2=e2, kvh=kvh, i=i, blk=blk, sb_=sb_: e.matmul(
                            out=OD[64 * e2:64 * e2 + 64, 0:128], lhsT=V_A[:, 2 * i + blk, kvh * 64:(kvh + 1) * 64],
                            rhs=PTa[sb_][:, (e2 * 2 + blk) * 128:(e2 * 2 + blk + 1) * 128], start=(blk == 0), stop=(blk == 1)),
                          reads=[("V_A", 2 * i + blk), ("PTa", sb_)], writes=["ps%d" % (2 + sb_)])
                    for blk in range(2):
                        A("pe", lambda e, OD=OD, e2=e2, blk=blk, sb_=sb_: e.matmul(
                            out=OD[64 * e2:64 * e2 + 64, 128:256], lhsT=onesb[:, 0:64],
                            rhs=PTa[sb_][:, (e2 * 2 + blk) * 128:(e2 * 2 + blk + 1) * 128], start=(blk == 0), stop=(blk == 1)),
                          reads=["onesb", ("PTa", sb_)], writes=["ps%d" % (2 + sb_)])
                rd = tmpf[:, sb_ * 256:sb_ * 256 + 128]
                tm = tmpf[:, sb_ * 256 + 128:sb_ * 256 + 256]
                A("dve", lambda e, OD=OD, rd=rd, hp=hp: e.tensor_scalar(out=rd, in0=OD[:, 128:256], scalar1=sinkexp[:, hp:hp + 1],
                                                                       scalar2=None, op0=ALU.add),
                  reads=["ps%d" % (2 + sb_), "sinkexp"], writes=[("rd", sb_)])
                A("dve", lambda e, rd=rd: e.reciprocal(out=rd, in_=rd), reads=[("rd", sb_)], writes=[("rd", sb_)])
                A("dve", lambda e, OD=OD, rd=rd, tm=tm: e.tensor_tensor(out=tm, in0=OD[:, 0:128], in1=rd, op=ALU.mult),
                  reads=["ps%d" % (2 + sb_), ("rd", sb_)], writes=[("tm", sb_)])
                A("dve", lambda e, tm=tm, hp=hp, i=i: e.tensor_tensor(out=yazT[:, hp, i * 128:(i + 1) * 128], in0=tm,
                                                                     in1=szA[:, hp, i * 128:(i + 1) * 128], op=ALU.mult),
                  reads=[("tm", sb_), ("SZ", hp, i // 4)], writes=[("yazT", hp, i)])
        dump("QT_A", QT_A, [("QT", 0, i) for i in range(NS)] + [("QT", 4, i) for i in range(NS)], BF16)
        dump("yazT", yazT, [("yazT", hp, i) for hp in range(6) for i in range(NS)], BF16)
        yaz_res = [("yazT", hp, i) for hp in range(6) for i in range(NS)]
        if stop_after == "C2":
            A("sp", lambda e: e.dma_start(out=y_d[0:128, 0:512], in_=tmpf), reads=[("tm", 0), ("tm", 1)], dma="y")
            P.emit()
            return nc, dbg_out
        P.barrier()

        QT_C = v3(51200, 4, 1024)
        szC = v3(55296, 4, 1024)
        yczT = v3(59392, 4, 1024)
        k = load_ws(OFF["qC"], 512)
        q_proj(k, 0, 4, 128, gC, "gC", QT_C, 0, 2)
        k = load_ws(OFF["zC"], 512)
        z_proj(k, 0, 4, szC, 0)
        pcnt = 0
        for h in range(4):
            for half in range(2):
                ob = 2 + pcnt % 2
                db = 4 + pcnt % 2
                tq = tmpf
                for blk in range(2):
                    sb_ = pcnt % 2 if blk == 0 else (pcnt + 1) % 2
                    sb_ = blk
                    A("pe", lambda e, h=h, half=half, blk=blk, sb_=sb_: e.matmul(
                        out=bank(sb_)[:, 0:512], lhsT=KT_C[:, h, blk * 128:(blk + 1) * 128],
                        rhs=QT_C[:, h, half * 512:(half + 1) * 512], start=True, stop=True),
                      reads=[("KT_C", blk)] + [("QT", 0, i) for i in range(4 * half, 4 * half + 4)], writes=["ps%d" % sb_])
                    A("act", lambda e, sb_=sb_: e.activation(out=PTa[sb_], in_=bank(sb_)[:, 0:512], func=AF.Exp),
                      reads=["ps%d" % sb_], writes=[("PTa", sb_)])
                    A("pe", lambda e, h=h, blk=blk, sb_=sb_, ob=ob: e.matmul(
                        out=bank(ob)[:, 0:512], lhsT=V_C[:, blk, h * 128:(h + 1) * 128], rhs=PTa[sb_],
                        start=(blk == 0), stop=(blk == 1)),
                      reads=[("V_C", blk), ("PTa", sb_)], writes=["ps%d" % ob])
                    A("pe", lambda e, blk=blk, sb_=sb_, db=db: e.matmul(
                        out=bank(db)[:, 0:512], lhsT=onesb[:, :], rhs=PTa[sb_], start=(blk == 0), stop=(blk == 1)),
                      reads=["onesb", ("PTa", sb_)], writes=["ps%d" % db])
                pcnt += 1
                A("dve", lambda e, db=db: e.reciprocal(out=tmpf, in_=bank(db)[:, 0:512]), reads=["ps%d" % db], writes=["tmpf"])
                A("dve", lambda e, ob=ob: e.tensor_tensor(out=tmpf, in0=bank(ob)[:, 0:512], in1=tmpf, op=ALU.mult),
                  reads=["ps%d" % ob, "tmpf"], writes=["tmpf"])
                A("dve", lambda e, h=h, half=half: e.tensor_tensor(out=yczT[:, h, half * 512:(half + 1) * 512], in0=tmpf,
                                                                  in1=szC[:, h, half * 512:(half + 1) * 512], op=ALU.mult),
                  reads=["tmpf", ("SZ", h, half)], writes=[("yczT", h, half)])
        dump("yczT", yczT, [("yczT", h, half) for h in range(4) for half in range(2)], BF16)
        ycz_res = [("yczT", h, half) for h in range(4) for half in range(2)]
        if stop_after == "C3":
            A("sp", lambda e: e.dma_start(out=y_d[0:128, :], in_=xt[0][:]), reads=[("xt", 0)], dma="y")
            P.emit()
            return nc, dbg_out
        P.barrier()

        szB = v3(51200, 6, 1024)
        k = load_ws(OFF["zB"], 512)
        z_proj(k, 0, 4, szB, 0)
        k = load_ws(OFF["zB"] + 512, 256)
        z_proj(k, 0, 2, szB, 4)
        for hp in range(6):
            A("dve", lambda e, hp=hp: e.tensor_tensor(out=ybT[:, hp, :], in0=ybT[:, hp, :], in1=szB[:, hp, :], op=ALU.mult),
              reads=[("SZ", hp, 0), ("SZ", hp, 1), ("ybT", hp, 0), ("ybT", hp, 4)], writes=[("ybzT", hp)])
        dump("ybzT", ybT, [("ybzT", hp) for hp in range(6)], BF16)
        ybz_res = [("ybzT", hp) for hp in range(6)]
        if stop_after == "C4":
            A("sp", lambda e: e.dma_start(out=y_d[0:128, :], in_=xt[0][:]), reads=[("xt", 0)], dma="y")
            P.emit()
            return nc, dbg_out
        P.barrier()

        yT_lo = v3(30720, 8, 1024)
        yT_hi = v3(51200, 8, 1024)

        def yT(cc):
            return yT_lo[:, cc, :] if cc < 8 else yT_hi[:, cc - 8, :]
        wb_v = [wba.rearrange("(kc p) n -> p kc n", p=128), wbb.rearrange("(kc p) n -> p kc n", p=128),
                wbc.rearrange("(kc p) n -> p kc n", p=128)]
        yz = [yazT, ybT, yczT]
        yz_res = [yaz_res, ybz_res, ycz_res]
        nkc = [6, 6, 4]
        koff = [0, 6, 12]
        sg = [ksq[:, 0:512], qtmp[:, 0:512], tmpf]
        acc = [hn[:, 0:1024].bitcast(F32), None]
        for cc in range(16):
            k = wsc[0] % 2
            wsc[0] += 1
            for b3 in range(3):
                A("pool", lambda e, k=k, b3=b3, cc=cc: e.dma_start(
                    out=ws[k][:, :, b3 * 128:(b3 + 1) * 128],
                    in_=w_in_v[:, :, OFF["G"] + b3 * D + cc * 128:OFF["G"] + b3 * D + (cc + 1) * 128]),
                  writes=[("wsg", k, b3)], dma=("wsg", k, b3))
                A("pool", lambda e, k=k, b3=b3, cc=cc: e.dma_start(
                    out=ws[k][:, koff[b3]:koff[b3] + nkc[b3], 384:512], in_=wb_v[b3][:, :, cc * 128:(cc + 1) * 128]),
                  writes=[("wsb", k, b3)], dma=("wsb", k, b3))
            wres = [("wsg", k, 0), ("wsg", k, 1), ("wsg", k, 2)]
            for half in range(2):
                gb = []
                ub = []
                for b3 in range(3):
                    pb = next_bank()
                    gb.append(pb)
                    for kc in range(KC):
                        A("pe", lambda e, kc=kc, b3=b3, half=half, pb=pb, k=k: e.matmul(
                            out=bank(pb)[:, 0:512], lhsT=ws[k][:, kc, b3 * 128:(b3 + 1) * 128],
                            rhs=hT_own[:, kc, half * 512:(half + 1) * 512], start=(kc == 0), stop=(kc == KC - 1)),
                          reads=[wres[b3]] + [("hT_own", i, "lo" if kc < 8 else "hi") for i in range(4 * half, 4 * half + 4)],
                          writes=["ps%d" % pb])
                    A("act", lambda e, b3=b3, pb=pb: e.activation(out=sg[b3], in_=bank(pb)[:, 0:512], func=AF.Sigmoid),
                      reads=["ps%d" % pb], writes=[("sg", b3)])
                for b3 in range(3):
                    pb = next_bank()
                    ub.append(pb)
                    for kk in range(nkc[b3]):
                        A("pe", lambda e, kk=kk, b3=b3, half=half, pb=pb, k=k: e.matmul(
                            out=bank(pb)[:, 0:512], lhsT=ws[k][:, koff[b3] + kk, 384:512],
                            rhs=yz[b3][:, kk, half * 512:(half + 1) * 512], start=(kk == 0), stop=(kk == nkc[b3] - 1)),
                          reads=[("wsb", k, b3)] + yz_res[b3], writes=["ps%d" % pb])
                accA = hn[:, 0:1024].bitcast(F32)
                A("dve", lambda e, ub=ub, accA=accA: e.tensor_tensor(out=accA, in0=sg[0], in1=bank(ub[0])[:, 0:512], op=ALU.mult),
                  reads=[("sg", 0), "ps%d" % ub[0]], writes=["accA"])
                A("dve", lambda e, ub=ub: e.tensor_tensor(out=sg[1], in0=sg[1], in1=bank(ub[1])[:, 0:512], op=ALU.mult),
                  reads=[("sg", 1), "ps%d" % ub[1]], writes=[("sg", 1)])
                A("dve", lambda e, accA=accA: e.tensor_tensor(out=accA, in0=accA, in1=sg[1], op=ALU.add),
                  reads=["accA", ("sg", 1)], writes=["accA"])
                A("dve", lambda e, ub=ub: e.tensor_tensor(out=sg[2], in0=sg[2], in1=bank(ub[2])[:, 0:512], op=ALU.mult),
                  reads=[("sg", 2), "ps%d" % ub[2]], writes=[("sg", 2)])
                A("dve", lambda e, accA=accA, cc=cc, half=half: e.tensor_tensor(
                    out=yT(cc)[:, half * 512:(half + 1) * 512], in0=accA, in1=sg[2], op=ALU.add),
                  reads=["accA", ("sg", 2)], writes=[("yT", cc, half)])
        dump("yT_lo", yT_lo, [("yT", cc, half) for cc in range(8) for half in range(2)], BF16)
        dump("yT_hi", yT_hi, [("yT", cc, half) for cc in range(8, 16) for half in range(2)], BF16)
        if stop_after == "2d":
            A("sp", lambda e: e.dma_start(out=y_d[0:128, :], in_=xt[0][:]), reads=[("xt", 0)], dma="y")
            P.emit()
            return nc, dbg_out

        wo_v = wout.rearrange("(kc p) n -> p kc n", p=128)
        xr = [xt[0][:, 0:512], xt[0][:, 512:1024], xt[0][:, 1024:1536], xt[0][:, 1536:2048],
              xt[1][:, 0:512], xt[1][:, 512:1024], xt[1][:, 1024:1536], xt[1][:, 1536:2048]]
        rc = 0
        for cg in range(4):
            k = wsc[0] % 2
            wsc[0] += 1
            for q4 in range(4):
                A("pool", lambda e, q4=q4, k=k, cg=cg: e.dma_start(out=ws[k][:, 4 * q4:4 * q4 + 4, :],
                                                                   in_=wo_v[:, 4 * q4:4 * q4 + 4, cg * 512:(cg + 1) * 512]),
                  writes=[(("wso", k), q4)] + [("wsg", k, b_) for b_ in range(3)] + [("wsb", k, b_) for b_ in range(3)],
                  dma=(("wso", k), q4))
            for i in range(NS):
                r8 = rc % 8
                rc += 1
                vb = 4 * i + 3
                A("sp", lambda e, r8=r8, vb=vb, cg=cg: e.dma_start(out=xr[r8], in_=xv[vb * 128:(vb + 1) * 128, cg * 512:(cg + 1) * 512]),
                  writes=[("xr", r8)], dma=("xr", r8))
                pb = next_bank()
                for kc in range(KC):
                    A("pe", lambda e, kc=kc, i=i, pb=pb, k=k: e.matmul(
                        out=bank(pb)[:, 0:512], lhsT=yT(kc)[:, i * 128:(i + 1) * 128], rhs=ws[k][:, kc, :],
                        start=(kc == 0), stop=(kc == KC - 1)),
                      reads=[(("wso", k), kc // 4), ("yT", kc, i // 4)], writes=["ps%d" % pb])
                A("dve", lambda e, r8=r8, pb=pb: e.tensor_tensor(out=xr[r8], in0=xr[r8], in1=bank(pb)[:, 0:512], op=ALU.add),
                  reads=[("xr", r8), "ps%d" % pb], writes=[("xr", r8)])
                A("sp", lambda e, r8=r8, i=i, cg=cg: e.dma_start(out=y_d[i * 128:(i + 1) * 128, cg * 512:(cg + 1) * 512], in_=xr[r8]),
                  reads=[("xr", r8)], dma=("yo", r8))
        P.emit()
    return nc, dbg_out


def _bf16_round(a):
    u = np.ascontiguousarray(a, dtype=np.float32).view(np.uint32).astype(np.uint64)
    r = ((u + 0x7FFF + ((u >> 16) & 1)) & 0xFFFF0000).astype(np.uint32)
    return r.view(np.float32)


def _const_tables():
    s = np.arange(128)[:, None].astype(np.float64)
    t = np.arange(128)[None, :].astype(np.float64)
    cmask = np.where(s <= t, 0.0, NEGM).astype(np.float32)
    slopes = np.exp2(-8.0 * np.arange(1, 13) / 12.0)
    sl = np.zeros((3, 12), np.float64)
    rem = slopes.copy()
    for j in range(3):
        sl[j] = _bf16_round(rem.astype(np.float32)).astype(np.float64)
        rem = rem - sl[j]
    al_l = np.zeros((6, 12, 128), np.float32)
    al_r = np.zeros((6, 12, 2, 128), np.float32)
    for j in range(3):
        al_l[j] = sl[j][:, None]
        al_l[3 + j] = np.arange(128)[None, :]
        for blk in range(2):
            al_r[j, :, blk, :] = -(np.arange(128)[None, :] + (128 if blk == 0 else 0))
            al_r[3 + j, :, blk, :] = sl[j][:, None]
    m0 = np.where(s > t, 0.0, NEGM).astype(np.float32)
    m1 = np.where(s <= t, 0.0, NEGM).astype(np.float32)
    amask = np.concatenate([m0, m1], axis=1)
    masked = np.full((128, 128), NEGM, np.float32)
    return cmask, al_l.reshape(6, 1536), al_r.reshape(6, 3072), amask, m0, masked


def make_in_maps(inp):
    f = lambda a: np.ascontiguousarray(np.asarray(a, dtype=np.float32))
    x = f(inp["x"]); mem = f(inp["mem"])
    cmask, al_l, al_r, amask, m0, masked = _const_tables()
    shared = dict(
        norm_gain=f(inp["norm_gain"]).reshape(1, D), mem_norm_gain=f(inp["mem_norm_gain"]).reshape(1, D),
        w_in=f(inp["w_in"]).reshape(D, INW), b_forget=f(inp["b_forget"]).reshape(1, 12),
        q_gain_a=f(inp["q_gain_a"]).reshape(1, 64), k_gain_a=f(inp["k_gain_a"]).reshape(1, 64),
        q_gain_b=f(inp["q_gain_b"]).reshape(1, 64), k_gain_b=f(inp["k_gain_b"]).reshape(1, 64),
        q_gain_c=f(inp["q_gain_c"]).reshape(1, 128), k_gain_c=f(inp["k_gain_c"]).reshape(1, 128),
        sinks2=np.ascontiguousarray(f(inp["sinks_a"]).reshape(6, 2).T),
        w_mem_kv=f(inp["w_mem_kv"]).reshape(D, 1024),
        w_branch_a=f(inp["w_branch_a"]).reshape(768, D), w_branch_b=f(inp["w_branch_b"]).reshape(768, D),
        w_branch_c=f(inp["w_branch_c"]).reshape(512, D), w_out=f(inp["w_out"]).reshape(D, D),
        cmask=cmask, alibi_l=al_l, alibi_r=al_r, amask=amask,
    )
    maps = []
    for core in range(8):
        b, c = core // 4, core % 4
        npad = 3 - c
        xv = np.zeros((NB * 128, D), np.float32)
        xv[npad * 128:] = x[b, :(NB - npad) * 128]
        padm = np.zeros((128, NB), np.float32)
        padm[:, :npad] = -NEGM
        m = dict(shared)
        m.update(xv=xv, padm=padm, mem=np.ascontiguousarray(mem[b]),
                 amask_s0=(masked if c == 0 else m0))
        maps.append(m)
    return maps


_NC_CACHE = {}


def kernel(**inputs):
    if "nc" not in _NC_CACHE:
        _NC_CACHE["nc"] = build_nc()[0]
    nc = _NC_CACHE["nc"]
    maps = make_in_maps(inputs)
    res = run_bass_kernel_spmd(nc, maps, core_ids=list(range(8)))
    out = np.zeros((2, 4096, D), np.float32)
    for core in range(8):
        b, c = core // 4, core % 4
        y = res.results[core]["y"].reshape(NS, 128, D)
        for i in range(NS):
            blk = 4 * i + c
            out[b, blk * 128:(blk + 1) * 128] = y[i]
    return out
```
